# Optimizing a Trainium2 kernel written in Bass

```python
import jax, jax.numpy as jnp
from jax import lax
import numpy as np

D_MODEL = 1024
BATCH = 8
SEQ = 2048
DEPTH = 2

HEAD_DIM = 64
FOX_HEADS = 6
SB_HEADS = 4
ML_HEADS = 6
D_FOX = FOX_HEADS * HEAD_DIM
D_SB = SB_HEADS * HEAD_DIM
D_ML = ML_HEADS * HEAD_DIM
D_MIX = D_FOX + D_SB + D_ML
N_IN = 3 * D_FOX + FOX_HEADS + 3 * D_SB + 4 * D_ML + 2 * ML_HEADS
BLOCK_Q = 128
ML_CHUNK = 128
CONV_WIDTH = 4
MEM_LEN = 256
X_HEADS = 4
X_HEAD_DIM = 128
D_X = X_HEADS * X_HEAD_DIM
D_FF = -(-8 * D_MODEL // (3 * 256)) * 256
RMS_EPS = 1e-6

kernel_name = "hybrid_fox_stickbreak_mlstm_block"


def rms_norm(x, w):
    xf = x.astype(jnp.float32)
    y = xf * lax.rsqrt(jnp.mean(xf * xf, axis=-1, keepdims=True) + RMS_EPS)
    return (y * w.astype(jnp.float32)).astype(x.dtype)


def split_columns(u, sizes):
    idx, acc = [], 0
    for s in sizes[:-1]:
        acc += s
        idx.append(acc)
    return jnp.split(u, idx, axis=-1)


def heads(t, n_heads):
    return t.reshape(t.shape[0], t.shape[1], n_heads, HEAD_DIM)


def query_blocks(t):
    b, s = t.shape[:2]
    return jnp.moveaxis(t.reshape((b, s // BLOCK_Q, BLOCK_Q) + t.shape[2:]), 1, 0)


def unblock(o):
    o = jnp.moveaxis(o, 0, 1)
    return o.reshape((o.shape[0], o.shape[1] * o.shape[2]) + o.shape[3:])


def fox_attention(q, k, v, log_f):
    s_len = q.shape[1]
    c = lax.cumsum(log_f, axis=1)
    c_k = jnp.transpose(c, (0, 2, 1))
    kpos = jnp.arange(s_len)
    scale = HEAD_DIM ** -0.5

    def block(args):
        qb, cb, start = args
        qpos = start + jnp.arange(BLOCK_Q)
        logits = jnp.einsum('bqhd,bkhd->bhqk', qb, k).astype(jnp.float32) * scale
        logits = logits + jnp.transpose(cb, (0, 2, 1))[..., None] - c_k[:, :, None, :]
        logits = jnp.where(kpos[None, :] <= qpos[:, None], logits, -jnp.inf)
        p = jax.nn.softmax(logits, axis=-1).astype(v.dtype)
        return jnp.einsum('bhqk,bkhd->bqhd', p, v)

    starts = jnp.arange(s_len // BLOCK_Q, dtype=jnp.int32) * BLOCK_Q
    return unblock(lax.map(block, (query_blocks(q), query_blocks(c), starts)))


def stick_breaking_attention(q, k, v):
    s_len = q.shape[1]
    kpos = jnp.arange(s_len)
    scale = HEAD_DIM ** -0.5

    def block(args):
        qb, start = args
        qpos = start + jnp.arange(BLOCK_Q)
        z = jnp.einsum('bqhd,bkhd->bhqk', qb, k).astype(jnp.float32) * scale
        valid = kpos[None, :] < qpos[:, None]
        log_keep = jnp.where(valid, jax.nn.log_sigmoid(-z), 0.0)
        suffix = lax.cumsum(log_keep, axis=3, reverse=True) - log_keep
        a = jnp.where(valid, jnp.exp(jax.nn.log_sigmoid(z) + suffix), 0.0).astype(v.dtype)
        return jnp.einsum('bhqk,bkhd->bqhd', a, v)

    starts = jnp.arange(s_len // BLOCK_Q, dtype=jnp.int32) * BLOCK_Q
    return unblock(lax.map(block, (query_blocks(q), starts)))


def mlstm_chunkwise(q, k, v, i_pre, f_pre):
    b_sz, s_len, n_h, dh = q.shape
    nc = s_len // ML_CHUNK

    def to_chunks(t):
        t = t.astype(jnp.float32).reshape((b_sz, nc, ML_CHUNK, n_h) + t.shape[3:])
        return jnp.moveaxis(jnp.moveaxis(t, 3, 2), 1, 0)

    qc = to_chunks(q)
    kc = to_chunks(k) * (dh ** -0.5)
    vc = to_chunks(v)
    ic = to_chunks(i_pre)
    lfc = jax.nn.log_sigmoid(to_chunks(f_pre))
    tri = jnp.arange(ML_CHUNK)[:, None] >= jnp.arange(ML_CHUNK)[None, :]

    def body(carry, inp):
        c_st, n_st, m_st = carry
        qx, kx, vx, ix, lfx = inp
        b = lax.cumsum(lfx, axis=2)
        g = b[..., -1]
        dlog = jnp.where(tri, b[..., :, None] - b[..., None, :] + ix[..., None, :], -jnp.inf)
        inter = b + m_st[..., None]
        m_t = jnp.maximum(inter, jnp.max(dlog, axis=-1))
        d = jnp.exp(dlog - m_t[..., None])
        sc = jnp.einsum('bhtd,bhsd->bhts', qx, kx) * d
        w_inter = jnp.exp(inter - m_t)
        num = jnp.einsum('bhts,bhsd->bhtd', sc, vx) + w_inter[..., None] * jnp.einsum('bhtk,bhkv->bhtv', qx, c_st)
        den = jnp.sum(sc, axis=-1) + w_inter * jnp.einsum('bhtk,bhk->bht', qx, n_st)
        h = num / jnp.maximum(jnp.abs(den), jnp.exp(-m_t))[..., None]
        src = g[..., None] - b + ix
        m_new = jnp.maximum(g + m_st, jnp.max(src, axis=-1))
        decay = jnp.exp(g + m_st - m_new)
        w = jnp.exp(src - m_new[..., None])
        c_new = decay[..., None, None] * c_st + jnp.einsum('bhs,bhsk,bhsv->bhkv', w, kx, vx)
        n_new = decay[..., None] * n_st + jnp.einsum('bhs,bhsk->bhk', w, kx)
        return (c_new, n_new, m_new), h

    init = (jnp.zeros((b_sz, n_h, dh, dh), jnp.float32),
            jnp.zeros((b_sz, n_h, dh), jnp.float32),
            jnp.zeros((b_sz, n_h), jnp.float32))
    _, h = lax.scan(body, init, (qc, kc, vc, ic, lfc))
    h = jnp.moveaxis(jnp.moveaxis(h, 0, 1), 2, 3)
    return h.reshape(b_sz, s_len, n_h, dh)


def causal_depthwise_conv(x, w):
    return lax.conv_general_dilated(
        x, w.astype(x.dtype)[:, None, :], window_strides=(1,), padding=[(CONV_WIDTH - 1, 0)],
        dimension_numbers=('NWC', 'WIO', 'NWC'), feature_group_count=x.shape[-1])


def hybrid_mixer(xn, w_in, fox_f_b, ml_conv_w, ml_i_b, ml_f_b, ml_norm_w, w_out):
    b_sz, s_len, _ = xn.shape
    u = xn @ w_in
    (fox_q, fox_k, fox_v, fox_f, sb_q, sb_k, sb_v,
     ml_qk, ml_v, ml_o, ml_i, ml_f) = split_columns(
        u, [D_FOX, D_FOX, D_FOX, FOX_HEADS, D_SB, D_SB, D_SB, 2 * D_ML, D_ML, D_ML, ML_HEADS, ML_HEADS])

    fox_log_f = jax.nn.log_sigmoid((fox_f + fox_f_b).astype(jnp.float32))
    y_fox = fox_attention(heads(fox_q, FOX_HEADS), heads(fox_k, FOX_HEADS), heads(fox_v, FOX_HEADS), fox_log_f)

    y_sb = stick_breaking_attention(heads(sb_q, SB_HEADS), heads(sb_k, SB_HEADS), heads(sb_v, SB_HEADS))

    qk = jax.nn.silu(causal_depthwise_conv(ml_qk, ml_conv_w))
    ml_q, ml_k = jnp.split(qk, 2, axis=-1)
    h_ml = mlstm_chunkwise(heads(ml_q, ML_HEADS), heads(ml_k, ML_HEADS), heads(ml_v, ML_HEADS),
                           ml_i + ml_i_b, ml_f + ml_f_b)
    h_ml = h_ml * lax.rsqrt(jnp.mean(h_ml * h_ml, axis=-1, keepdims=True) + RMS_EPS)
    h_ml = h_ml * ml_norm_w.astype(jnp.float32).reshape(ML_HEADS, HEAD_DIM)
    y_ml = (jax.nn.sigmoid(ml_o.astype(jnp.float32)).reshape(b_sz, s_len, ML_HEADS, HEAD_DIM) * h_ml).astype(xn.dtype)

    y = jnp.concatenate([y_fox.reshape(b_sz, s_len, D_FOX),
                         y_sb.reshape(b_sz, s_len, D_SB),
                         y_ml.reshape(b_sz, s_len, D_ML)], axis=-1)
    return y @ w_out


def memory_cross_attention(hn, mem_n, wx_q, wx_kv, wx_o):
    b_sz, s_len, _ = hn.shape
    q = (hn @ wx_q).reshape(b_sz, s_len, X_HEADS, X_HEAD_DIM)
    k, v = jnp.split(mem_n @ wx_kv, 2, axis=-1)
    k = k.reshape(b_sz, -1, X_HEADS, X_HEAD_DIM)
    v = v.reshape(b_sz, -1, X_HEADS, X_HEAD_DIM)
    logits = jnp.einsum('bshd,bmhd->bhsm', q, k).astype(jnp.float32) * (X_HEAD_DIM ** -0.5)
    p = jax.nn.softmax(logits, axis=-1).astype(v.dtype)
    o = jnp.einsum('bhsm,bmhd->bshd', p, v).reshape(b_sz, s_len, D_X)
    return o @ wx_o


def swiglu_ffn(hn, w_gate, w_up, w_down):
    return (jax.nn.silu(hn @ w_gate) * (hn @ w_up)) @ w_down


def setup_inputs(seed: int = 0) -> dict:
    key = jax.random.key(seed)
    ks = jax.random.split(key, 24)

    def normal(k, shape, scale):
        return jax.random.normal(k, shape, jnp.float32) * scale

    def gain(k, shape):
        return 1.0 + 0.02 * jax.random.normal(k, shape, jnp.float32)

    fox_f_b = jnp.linspace(1.0, 4.0, FOX_HEADS, dtype=jnp.float32)[None, :] + normal(ks[3], (DEPTH, FOX_HEADS), 0.1)
    ml_f_b = jnp.linspace(3.0, 6.0, ML_HEADS, dtype=jnp.float32)[None, :] + normal(ks[6], (DEPTH, ML_HEADS), 0.1)
    return {
        "x": normal(ks[0], (BATCH, SEQ, D_MODEL), 1.0),
        "mem": normal(ks[1], (BATCH, MEM_LEN, D_MODEL), 1.0),
        "norm_mix_w": gain(ks[2], (DEPTH, D_MODEL)),
        "w_in": normal(ks[4], (DEPTH, D_MODEL, N_IN), D_MODEL ** -0.5),
        "fox_f_b": fox_f_b,
        "ml_conv_w": normal(ks[5], (DEPTH, CONV_WIDTH, 2 * D_ML), CONV_WIDTH ** -0.5),
        "ml_i_b": normal(ks[7], (DEPTH, ML_HEADS), 0.1),
        "ml_f_b": ml_f_b,
        "ml_norm_w": gain(ks[8], (DEPTH, D_ML)),
        "w_out": normal(ks[9], (DEPTH, D_MIX, D_MODEL), D_MIX ** -0.5),
        "norm_x_w": gain(ks[10], (DEPTH, D_MODEL)),
        "mem_norm_w": gain(ks[11], (DEPTH, D_MODEL)),
        "wx_q": normal(ks[12], (DEPTH, D_MODEL, D_X), D_MODEL ** -0.5),
        "wx_kv": normal(ks[13], (DEPTH, D_MODEL, 2 * D_X), D_MODEL ** -0.5),
        "wx_o": normal(ks[14], (DEPTH, D_X, D_MODEL), D_X ** -0.5),
        "norm_ffn_w": gain(ks[15], (DEPTH, D_MODEL)),
        "w_gate": normal(ks[16], (DEPTH, D_MODEL, D_FF), D_MODEL ** -0.5),
        "w_up": normal(ks[17], (DEPTH, D_MODEL, D_FF), D_MODEL ** -0.5),
        "w_down": normal(ks[18], (DEPTH, D_FF, D_MODEL), D_FF ** -0.5),
        "final_norm_w": gain(ks[19], (D_MODEL,)),
    }


def reference(x, mem, norm_mix_w, w_in, fox_f_b, ml_conv_w, ml_i_b, ml_f_b, ml_norm_w, w_out,
              norm_x_w, mem_norm_w, wx_q, wx_kv, wx_o, norm_ffn_w, w_gate, w_up, w_down, final_norm_w):
    h = x
    for l in range(DEPTH):
        xn = rms_norm(h, norm_mix_w[l])
        h = h + hybrid_mixer(xn, w_in[l], fox_f_b[l], ml_conv_w[l], ml_i_b[l], ml_f_b[l], ml_norm_w[l], w_out[l])
        hn = rms_norm(h, norm_x_w[l])
        mn = rms_norm(mem, mem_norm_w[l])
        h = h + memory_cross_attention(hn, mn, wx_q[l], wx_kv[l], wx_o[l])
        hn = rms_norm(h, norm_ffn_w[l])
        h = h + swiglu_ffn(hn, w_gate[l], w_up[l], w_down[l])
    return rms_norm(h, final_norm_w)
```

```python
import numpy as np
from contextlib import ExitStack
import concourse.bass as bass
import concourse.mybir as mybir
from concourse.bass_utils import run_bass_kernel_spmd

F32 = mybir.dt.float32
BF16 = mybir.dt.bfloat16
AF = mybir.ActivationFunctionType
ALU = mybir.AluOpType
AX = mybir.AxisListType

ENGS = ("pe", "act", "dve", "pool", "sp")

D = 1024
T = 2048
KC = 8
NTB = 16
NTT = 4
DEPTH = 2
MEM = 256
DFF = 2816
NIN = 3474
EPS = 1e-6


class Prog:
    def __init__(self, nc):
        self.nc = nc
        self.ins = {e: [] for e in ENGS}
        self.known = {e: {} for e in ENGS}
        self.lastw = {}
        self.readers = {}
        self.dma_cnt = {}
        self.snap = {}
        self.out_tokens = []

    def _deps(self, eng, reads, writes):
        deps = {}

        def add(tok):
            if tok is None:
                return
            sk, v = tok
            if sk == "pe" and eng == "pe":
                return
            if deps.get(sk, -1) < v:
                deps[sk] = v
        for k in reads:
            add(self.lastw.get(k))
        for k in writes:
            add(self.lastw.get(k))
            for t in self.readers.get(k, ()):
                add(t)
        kn = self.known[eng]
        waits = []
        for sk, v in deps.items():
            if kn.get(sk, -1) >= v:
                continue
            waits.append((sk, v))
        for sk, v in waits:
            sn = self.snap.get((sk, v))
            if sn:
                for a, b in sn.items():
                    if kn.get(a, -1) < b:
                        kn[a] = b
            if kn.get(sk, -1) < v:
                kn[sk] = v
        return waits

    def _commit(self, tok, reads, writes):
        for k in reads:
            self.readers.setdefault(k, []).append(tok)
        for k in writes:
            self.lastw[k] = tok
            self.readers[k] = []

    def op(self, eng, meth, args, kw, reads=(), writes=()):
        waits = self._deps(eng, reads, writes)
        tok = (eng, len(self.ins[eng]))
        self.ins[eng].append(dict(fn=(meth, args, kw), waits=waits, tok=tok, dma=None))
        self.snap[tok] = dict(self.known[eng])
        self._commit(tok, reads, writes)
        return tok

    def dma(self, eng, out, in_, semname, reads=(), writes=(), is_out=False):
        fn = ("dma_start", (), dict(out=out, in_=in_))
        waits = self._deps(eng, reads, writes)
        sk = ("dma", semname)
        n = self.dma_cnt.get(sk, 0)
        self.dma_cnt[sk] = n + 1
        tok = (sk, n)
        self.ins[eng].append(dict(fn=fn, waits=waits, tok=tok, dma=sk))
        self.snap[tok] = dict(self.known[eng])
        self._commit(tok, reads, writes)
        if is_out:
            self.out_tokens.append(tok)
        return tok

    def barrier(self):
        last = []
        for e in ENGS:
            for rec in reversed(self.ins[e]):
                if rec["tok"] is not None and rec["dma"] is None:
                    last.append(rec["tok"])
                    break
        for sk, n in self.dma_cnt.items():
            last.append((sk, n - 1))
        for e in ENGS:
            kn = self.known[e]
            waits = []
            for sk, v in last:
                if sk == e:
                    continue
                if kn.get(sk, -1) >= v:
                    continue
                waits.append((sk, v))
                kn[sk] = v
            if waits:
                self.ins[e].append(dict(fn=None, waits=waits, tok=None, dma=None))
        self.lastw = {}
        self.readers = {}

    def finish(self, eng="sp"):
        best = {}
        for sk, v in self.out_tokens:
            if best.get(sk, -1) < v:
                best[sk] = v
        self.ins[eng].append(dict(fn=None, waits=list(best.items()), tok=None, dma=None))

    def emit(self, es):
        nc = self.nc
        needed = set()
        for e in ENGS:
            for rec in self.ins[e]:
                for w in rec["waits"]:
                    needed.add(w)
        rank = {}
        for e in ENGS:
            c = 0
            for rec in self.ins[e]:
                if rec["dma"] is None and rec["tok"] is not None and rec["tok"] in needed:
                    c += 1
                    rank[rec["tok"]] = c
        sems = {}
        for e in ENGS:
            sems[e] = es.enter_context(nc.semaphore("sem_" + e))
        for sk in self.dma_cnt:
            sems[sk] = es.enter_context(nc.semaphore("semd_" + str(sk[1])))

        def val(tok):
            sk, v = tok
            if isinstance(sk, tuple):
                return 16 * (v + 1)
            return rank[tok]

        def run(e, h):
            for rec in self.ins[e]:
                for w in rec["waits"]:
                    h.wait_ge(sems[w[0]], val(w))
                if rec["fn"] is None:
                    continue
                meth, a, kw = rec["fn"]
                ins = getattr(h, meth)(*a, **kw)
                if rec["dma"] is not None:
                    ins.then_inc(sems[rec["dma"]], 16)
                elif rec["tok"] in needed:
                    ins.then_inc(sems[e], 1)

        block = es.enter_context(nc.Block())

        @block.tensor
        def _(h):
            run("pe", h)

        @block.scalar
        def _(h):
            run("act", h)

        @block.vector
        def _(h):
            run("dve", h)

        @block.gpsimd
        def _(h):
            run("pool", h)

        @block.sync
        def _(h):
            run("sp", h)


class Arena:
    def __init__(self, ap, nwords):
        self.ap = ap
        self.n = nwords
        self.top = 0

    def f32(self, cols):
        a = self.ap[:, self.top:self.top + cols]
        self.top += cols
        self.hw = max(getattr(self, "hw", 0), self.top)
        assert self.top <= self.n, ("arena overflow", self.top, self.n)
        return a

    def bf16(self, cols):
        w = (cols + 1) // 2
        a = self.ap[:, self.top:self.top + w].bitcast(BF16)
        self.top += w
        self.hw = max(getattr(self, "hw", 0), self.top)
        assert self.top <= self.n, ("arena overflow", self.top, self.n)
        return a[:, 0:cols]

    def mark(self):
        return self.top

    def release(self, m):
        self.top = m


def build(nlayers=DEPTH, stop=None):
    nc = bass.Bass("TRN2", target_bir_lowering=False, dynamic_dma_scratch_size=4096)

    def din(name, shape):
        return nc.dram_tensor(name, list(shape), F32, kind="ExternalInput").ap()

    xT_d = din("xT", [D, T])
    memT_d = din("memT", [D, MEM])
    normw_d = din("normw", [128, 9 * KC])
    w_in_d = din("w_in", [DEPTH, D, NIN])
    wg_d = din("w_gates", [DEPTH, D, 24])
    gb_d = din("gate_b", [12, DEPTH * 2])
    convw_d = din("convw", [128, DEPTH * 6 * 4])
    mlnw_d = din("mlnw", [128, DEPTH * 384])
    w_out_d = din("w_out", [DEPTH, D, D])
    wx_q_d = din("wx_q", [DEPTH, D, 512])
    wx_kv_d = din("wx_kv", [DEPTH, D, 1024])
    wx_o_d = din("wx_o", [DEPTH, 512, D])
    w_gate_d = din("w_gate", [DEPTH, D, DFF])
    w_up_d = din("w_up", [DEPTH, D, DFF])
    w_down_d = din("w_down", [DEPTH, DFF, D])
    outT_d = nc.dram_tensor("outT", [D, T], F32, kind="ExternalOutput").ap()

    es = ExitStack()
    P = Prog(nc)
    dbg = []

    def dump(name, ap, keys):
        d = nc.dram_tensor("dbg_" + name, list(ap.shape), ap.dtype, kind="ExternalOutput").ap()
        dbg.append((name, d, ap, list(keys)))

    sbt = lambda name, shape, dt: es.enter_context(nc.sbuf_tensor(name, shape, dt))
    hT = sbt("hT", [128, KC, T], F32)
    xnT = sbt("xnT", [128, KC, T], BF16)
    NSLOT = 3
    WSLOT = 4096
    wring = sbt("wring", [128, NSLOT, WSLOT], BF16)
    consts = sbt("consts", [128, 1280], F32)
    mlnw = sbt("mlnw_sb", [128, DEPTH * 384], F32)
    ARW = 20200 + 3072
    arena_t = sbt("arena", [128, ARW], F32)
    AR = Arena(arena_t, ARW)
    pp = [es.enter_context(nc.psum_tensor("pp%d" % i, [128, 512], F32)) for i in range(8)]
    PK = ["ps%d" % i for i in range(8)]

    def mk(eng):
        def f(meth, *a, reads=(), writes=(), **kw):
            reads, writes = list(reads), list(writes)
            if eng != "pe":
                for k in reads:
                    if isinstance(k, str) and k.startswith("ps") and k not in writes:
                        writes.append(k)
            if defer["list"] is not None:
                defer["list"].append((eng, meth, a, kw, reads, writes))
                return None
            return P.op(eng, meth, a, kw, reads, writes)
        return f

    defer = {"list": None}

    def flush_deferred(lst, n):
        for _ in range(n):
            if not lst:
                return
            eng_, meth_, a_, kw_, r_, w_ = lst.pop(0)
            P.op(eng_, meth_, a_, kw_, r_, w_)
    pe, act, dve, pool = mk("pe"), mk("act"), mk("dve"), mk("pool")

    CA = Arena(consts, 1280)
    ident_f = CA.f32(128)
    ident_b = CA.bf16(128)
    ones_b = CA.bf16(128)
    maskneg = CA.bf16(128)
    strict01 = CA.f32(128)
    tincl = CA.bf16(128)
    mask01T = CA.f32(128)
    zeros_f = CA.f32(128)
    ones_f = CA.f32(128)
    normw = CA.f32(9 * KC)
    gate_b = CA.f32(DEPTH * 2)
    neg_gb = CA.f32(DEPTH * 2)
    convw = CA.f32(DEPTH * 24)
    epsc = CA.f32(1)
    onec = CA.f32(1)
    msel = CA.f32(6)
    ln8c = CA.f32(1)
    masknegS = CA.bf16(128)

    pool("memset", zeros_f, 0.0, writes=["c_zeros"])
    pool("memset", ones_f, 1.0, writes=["c_ones"])
    pool("memset", ones_b, 1.0, writes=["c_onesb"])
    pool("memset", epsc, EPS, writes=["c_eps"])
    pool("memset", onec, 1.0, writes=["c_one"])
    pool("memset", ln8c, float(np.log(8.0)), writes=["c_ln8"])

    def asel(out, in_, pattern, cm, cmp, fill, base=0, rk=(), wk=()):
        pool("affine_select", out=out, in_=in_, pattern=pattern, compare_op=cmp, fill=fill, base=base,
             channel_multiplier=cm, reads=rk, writes=wk)

    asel(ident_f, ones_f, [[-1, 128]], 1, ALU.is_equal, 0.0, rk=["c_ones"], wk=["c_identf"])
    asel(ident_b, ones_f, [[-1, 128]], 1, ALU.is_equal, 0.0, rk=["c_ones"], wk=["c_identb"])
    asel(maskneg, zeros_f, [[1, 128]], -1, ALU.is_ge, -30000.0, rk=["c_zeros"], wk=["c_maskneg"])
    asel(strict01, ones_f, [[1, 128]], -1, ALU.is_gt, 0.0, rk=["c_ones"], wk=["c_strict"])
    asel(masknegS, zeros_f, [[1, 128]], -1, ALU.is_gt, -30000.0, rk=["c_zeros"], wk=["c_masknegS"])
    asel(tincl, ones_f, [[-1, 128]], 1, ALU.is_ge, 0.0, rk=["c_ones"], wk=["c_tincl"])
    asel(mask01T, ones_f, [[1, 128]], -1, ALU.is_ge, 0.0, rk=["c_ones"], wk=["c_mask01T"])
    asel(msel[0:12, :], ones_f[0:12, 0:6], [[-1, 6]], 1, ALU.is_equal, 0.0, base=-6, rk=["c_ones"], wk=["c_msel"])

    P.dma("sp", normw, normw_d, "cst0", writes=["c_normw"])
    P.dma("sp", gate_b[0:12, :], gb_d, "cst1", writes=["c_gateb"])
    P.dma("sp", convw, convw_d, "cst2", writes=["c_convw"])
    P.dma("sp", mlnw[:], mlnw_d, "cst3", writes=["c_mlnw"])
    dve("tensor_scalar", out=neg_gb[0:12, :], in0=gate_b[0:12, :], scalar1=-1.0, scalar2=None, op0=ALU.mult,
        reads=["c_gateb"], writes=["c_neggb"])

    for kc in range(KC):
        P.dma("sp", hT[:, kc, :], xT_d[kc * 128:(kc + 1) * 128, :], "hload%d" % kc, writes=[("hT", kc, tt) for tt in range(NTT)])

    wstate = dict(n=0)

    def wload(segs, kcn):
        tot = sum(s.shape[1] for s in segs)
        assert kcn * tot <= WSLOT, (kcn, tot)
        s_ = wstate["n"] % NSLOT
        wstate["n"] += 1
        key = ("w", s_)
        view = wring[:, s_, 0:kcn * tot].rearrange("p (k n) -> p k n", n=tot)
        off = 0
        for sg in segs:
            n = sg.shape[1]
            P.dma("pool", view[:, :, off:off + n], sg.rearrange("(k p) n -> p k n", p=128), "w%d" % s_, writes=[key])
            off += n
        return view, key

    def rmsnorm(src, skey, ntok, wcol, dst, dkey, tw=512, scr=None):
        m = AR.mark()
        if scr is None:
            sq = [AR.bf16(tw) for _ in range(2)]
            lnv = AR.f32(tw)
            rstd = AR.f32(tw)
        else:
            sq = [scr[:, 0:tw // 2].bitcast(BF16), scr[:, tw // 2:tw].bitcast(BF16)]
            lnv = scr[:, tw:2 * tw]
            rstd = scr[:, 2 * tw:3 * tw]
        for tt in range(ntok // tw):
            sl = slice(tt * tw, (tt + 1) * tw)
            ps, pk = pp[tt % 2], PK[tt % 2]
            for kc in range(KC):
                s, sk = sq[kc % 2], ("nsq", kc % 2)
                act("activation", out=s, in_=src[:, kc, sl], func=AF.Square, reads=[(skey, kc, tt)], writes=[sk])
                pe("matmul", ps[:, 0:tw], lhsT=ones_b, rhs=s, start=(kc == 0), stop=(kc == KC - 1), reads=[sk, "c_onesb"], writes=[pk])
            act("activation", out=lnv, in_=ps[:, 0:tw], func=AF.Ln, bias=epsc, scale=1.0 / D, reads=[pk, "c_eps"], writes=["nln"])
            act("activation", out=rstd, in_=lnv, func=AF.Exp, scale=-0.5, reads=["nln"], writes=["nrstd"])
            for kc in range(KC):
                dve("scalar_tensor_tensor", out=dst[:, kc, sl], in0=src[:, kc, sl], scalar=normw[:, wcol * KC + kc:wcol * KC + kc + 1],
                    in1=rstd, op0=ALU.mult, op1=ALU.mult, reads=[(skey, kc, tt), "nrstd", "c_normw"], writes=[(dkey, kc, tt)])
        AR.release(m)

    def proj_fm(wv, wkey, c0, mcols, xT_, xkey, kcn, tt, ps, pk, tw=512):
        for kc in range(kcn):
            pe("matmul", ps[0:mcols, 0:tw], lhsT=wv[:, kc, c0:c0 + mcols], rhs=xT_[:, kc, tt * tw:(tt + 1) * tw],
               start=(kc == 0), stop=(kc == kcn - 1), reads=[wkey, (xkey, kc, tt)], writes=[pk])

    def resid_add(wv, wkey, ncol0, nch, xT_, xkey, kcn):
        for tt in range(NTT):
            b = (nch * NTT + tt) % 2
            proj_fm(wv, wkey, ncol0, 128, xT_, xkey, kcn, tt, pp[4 + b], PK[4 + b])
            sl = slice(tt * 512, (tt + 1) * 512)
            dve("tensor_tensor", out=hT[:, nch, sl], in0=pp[4 + b][:, :], in1=hT[:, nch, sl], op=ALU.add,
                reads=[PK[4 + b], ("hT", nch, tt)], writes=[("hT", nch, tt)])

    def resid_add_ttmajor(groups, xT_, xkey, kcn):
        cnt_ = 0
        for tt in range(NTT):
            sl = slice(tt * 512, (tt + 1) * 512)
            for (wv, wk, lst) in groups:
                for (c0, nch) in lst:
                    b = cnt_ % 2
                    cnt_ += 1
                    proj_fm(wv, wk, c0, 128, xT_, xkey, kcn, tt, pp[4 + b], PK[4 + b])
                    dve("tensor_tensor", out=hT[:, nch, sl], in0=pp[4 + b][:, :], in1=hT[:, nch, sl], op=ALU.add,
                        reads=[PK[4 + b], ("hT", nch, tt)], writes=[("hT", nch, tt)])

    def to_fm(src, skeyf, nfc, dst, dkey):
        for tb in range(NTB):
            b = tb % 2
            pt = pp[6 + b][:, :].bitcast(BF16)
            for fc in range(nfc):
                pe("transpose", pt[:, fc * 128:(fc + 1) * 128], src[:, tb, fc * 128:(fc + 1) * 128], ident_b,
                   reads=list(skeyf(tb)) + ["c_identb"], writes=[PK[6 + b]])
            o = dst[:, 0:nfc, tb * 128:(tb + 1) * 128]
            i = pt[:, 0:nfc * 128].rearrange("p (f t) -> p f t", t=128)
            if tb % 2 == 0:
                dve("tensor_copy", o, i, reads=[PK[6 + b]], writes=[(dkey, fc, tb // 4) for fc in range(nfc)])
            else:
                act("copy", out=o, in_=i, reads=[PK[6 + b]], writes=[(dkey, fc, tb // 4) for fc in range(nfc)])

    m_persist = AR.mark()
    R12 = slice(0, 12)
    done = False
    for l in range(nlayers):
        rmsnorm(hT, "hT", T, 0 * DEPTH + l, xnT, "xnT", scr=arena_t[:, AR.top + 6400:AR.top + 6400 + 1536])
        m_layer = AR.mark()
        y0 = AR.mark()
        yT = AR.bf16(KC * T).rearrange("p (k t) -> p k t", t=T)
        negcT = AR.f32(192).rearrange("p (b r) -> p b r", r=12)
        stat = [AR.f32(192).rearrange("p (b r) -> p b r", r=12) for _ in range(4)]
        decay_bc = AR.f32(96).rearrange("p (b r) -> p b r", r=6)
        m_cb = AR.mark()
        Cb = AR.bf16(T)

        wgv, wgk = wload([wg_d[l]], KC)
        X1, X2, X3 = (arena_t[:, y0 + i_ * T:y0 + (i_ + 1) * T] for i_ in range(3))
        Mp, Dexp, dec = arena_t[:, y0 + 3 * T:y0 + 3 * T + 17], arena_t[:, y0 + 3 * T + 32:y0 + 3 * T + 128], arena_t[:, y0 + 3 * T + 128:y0 + 3 * T + 144]
        for gi in range(2):
            for tt in range(NTT):
                ps, pk = pp[4 + tt % 2], PK[4 + tt % 2]
                sl = slice(tt * 512, (tt + 1) * 512)
                proj_fm(wgv, wgk, gi * 12, 12, xnT, "xnT", KC, tt, ps, pk)
                if gi == 0:
                    act("activation", out=X1[R12, sl], in_=ps[R12, :], func=AF.Exp, bias=neg_gb[R12, 2 * l:2 * l + 1], scale=-1.0,
                        reads=[pk, "c_neggb"], writes=["X1"])
                else:
                    act("activation", out=X2[R12, sl], in_=ps[R12, :], func=AF.Identity, bias=gate_b[R12, 2 * l + 1:2 * l + 2], scale=1.0,
                        reads=[pk, "c_gateb"], writes=["X2"])
        gate_ops = []
        defer["list"] = gate_ops
        act("activation", out=X1[R12, :], in_=X1[R12, :], func=AF.Ln, bias=onec[R12, :], scale=1.0, reads=["X1", "c_one"], writes=["X1"])
        dve("tensor_tensor_scan", out=X3[R12, :], data0=ones_f[R12, 0:1].broadcast_to([12, T]), data1=X1[R12, :], initial=0.0,
            op0=ALU.mult, op1=ALU.subtract, reads=["X1", "c_ones"], writes=["X3"])
        dve("tensor_copy", Cb[R12, :], X3[R12, :], reads=["X3"], writes=["Cb"])
        for blk in range(NTB):
            pe("transpose", pp[4][:, blk * 12:(blk + 1) * 12], X3[R12, blk * 128:(blk + 1) * 128], ident_f[R12, 0:12],
               reads=["X3", "c_identf"], writes=[PK[4]])
        dve("tensor_scalar", out=negcT.rearrange("p b r -> p (b r)"), in0=pp[4][:, 0:192], scalar1=-1.0, scalar2=None, op0=ALU.mult,
            reads=[PK[4]], writes=["negcT"])
        dve("tensor_tensor", out=X2[R12, :], in0=X2[R12, :], in1=X3[R12, :], op=ALU.subtract, reads=["X2", "X3"], writes=["X2"])
        dve("tensor_tensor_scan", out=X1[R12, :], data0=X2[R12, :], data1=X2[R12, :], initial=0.0, op0=ALU.max, op1=ALU.max,
            reads=["X2", "X1"], writes=["X1"])
        dve("memset", Mp[R12, 0:1], 0.0, writes=["Mp"])
        v3 = lambda a: a[R12, :].rearrange("p (c t) -> p c t", t=128)
        dve("tensor_copy", Mp[R12, 1:17], v3(X1)[:, :, 127], reads=["X1", "Mp"], writes=["Mp"])
        mend_bc = Mp[R12, 1:17].unsqueeze(2).broadcast_to([12, 16, 128])
        mprev_bc = Mp[R12, 0:16].unsqueeze(2).broadcast_to([12, 16, 128])
        dve("tensor_tensor", out=dec[R12, :], in0=Mp[R12, 0:16], in1=Mp[R12, 1:17], op=ALU.subtract, reads=["Mp"], writes=["dec"])
        act("activation", out=dec[R12, :], in_=dec[R12, :], func=AF.Exp, reads=["dec"], writes=["dec"])
        dve("tensor_tensor", out=v3(X2), in0=v3(X2), in1=mend_bc, op=ALU.subtract, reads=["Mp", "X2"], writes=["X2"])
        dve("scalar_tensor_tensor", out=X3[R12, :], in0=X3[R12, :], scalar=-1.0, in1=X1[R12, :], op0=ALU.mult, op1=ALU.subtract,
            reads=["X3", "X1", "Cb", PK[4]], writes=["X3"])

        def exp_tr(qi, X, xk):
            if qi == 3:
                act("activation", out=X[R12, :], in_=X[R12, :], func=AF.Exp, bias=ln8c[R12, :], scale=1.0, reads=[xk, "c_ln8"], writes=[xk])
            else:
                act("activation", out=X[R12, :], in_=X[R12, :], func=AF.Exp, reads=[xk], writes=[xk])
            bank = 4 + qi
            for blk in range(NTB):
                pe("transpose", pp[bank][:, blk * 12:(blk + 1) * 12], X[R12, blk * 128:(blk + 1) * 128], ident_f[R12, 0:12],
                   reads=[xk, "c_identf", "negcT"], writes=[PK[bank]])
            dve("tensor_copy", stat[qi].rearrange("p b r -> p (b r)"), pp[bank][:, 0:192], reads=[PK[bank]], writes=[("stat", qi)])

        exp_tr(0, X2, "X2")
        dve("tensor_tensor", out=v3(X2), in0=mend_bc, in1=v3(X1), op=ALU.subtract, reads=["Mp", "X1", "X2"], writes=["X2"])
        exp_tr(1, X2, "X2")
        dve("tensor_tensor", out=v3(X1), in0=mprev_bc, in1=v3(X1), op=ALU.subtract, reads=["Mp", "X1"], writes=["X1"])
        exp_tr(2, X1, "X1")
        exp_tr(3, X3, "X3")
        dexp3 = Dexp[R12, :].rearrange("p (c h) -> p c h", h=6)
        for hh in range(6):
            dve("tensor_scalar", out=dexp3[:, :, hh], in0=dec[R12, :], scalar1=msel[R12, hh:hh + 1], scalar2=None, op0=ALU.mult,
                reads=["dec", "c_msel"], writes=["Dexp"])
        pe("matmul", pp[6][:, 0:96], lhsT=ones_f[R12, :], rhs=Dexp[R12, :], start=True, stop=True, reads=["Dexp", "c_ones"], writes=[PK[6]])
        dve("tensor_copy", decay_bc.rearrange("p b r -> p (b r)"), pp[6][:, 0:96], reads=[PK[6]], writes=["decay_bc"])
        defer["list"] = None
        if stop == "gates":
            flush_deferred(gate_ops, len(gate_ops))
            dump("negcT", negcT.rearrange("p b r -> p (b r)"), ["negcT"])
            for qi in range(4):
                dump("stat%d" % qi, stat[qi].rearrange("p b r -> p (b r)"), [("stat", qi)])
            dump("decay", decay_bc.rearrange("p b r -> p (b r)"), ["decay_bc"])
            break

        m_m = AR.mark()
        qTs, kTs = [AR.bf16(T) for _ in range(3)], [AR.bf16(T) for _ in range(3)]
        m_c = AR.mark()
        Ub2 = [AR.f32(T + 8) for _ in range(2)]
        Dg2 = [[AR.f32(128) for _ in range(4)] for _ in range(2)]
        for i_ in range(2):
            dve("memset", Ub2[i_][:, 0:4], 0.0, writes=[("Ub", i_)])
        MQ, MV, MO = 1926, 2694, 3078
        nhp_, nch_ = (int(stop.split(":")[1]), int(stop.split(":")[2])) if (stop or "").startswith("ml:") else (3, NTB)
        ml_w = {}
        cts = [(hp, which) for hp in range(nhp_) for which in ("q", "k")]

        def conv_proj(ci):
            hp, which = cts[ci]
            if which == "q":
                ml_w[hp] = wload([w_in_d[l][:, MQ + hp * 128:MQ + (hp + 1) * 128], w_in_d[l][:, MQ + 384 + hp * 128:MQ + 384 + (hp + 1) * 128],
                                  w_in_d[l][:, MV + hp * 128:MV + (hp + 1) * 128], w_in_d[l][:, MO + hp * 128:MO + (hp + 1) * 128]], KC)
            wmv, wmk = ml_w[hp]
            c0, chunk = (0, hp) if which == "q" else (128, 3 + hp)
            Ub, ubk, Dg = Ub2[ci % 2], ("Ub", ci % 2), Dg2[ci % 2]
            for tap in range(4):
                col = (l * 6 + chunk) * 4 + tap
                dve("tensor_scalar", out=Dg[tap], in0=ident_f, scalar1=convw[:, col:col + 1], scalar2=None, op0=ALU.mult,
                    reads=["c_identf", "c_convw"], writes=[("Dg", ci % 2, tap)])
            for tt in range(NTT):
                ps, pk = pp[tt % 2], PK[tt % 2]
                proj_fm(wmv, wmk, c0, 128, xnT, "xnT", KC, tt, ps, pk)
                if tt % 2 == 0:
                    dve("tensor_copy", Ub[:, 4 + tt * 512:4 + (tt + 1) * 512], ps[:, :], reads=[pk], writes=[ubk])
                else:
                    act("copy", out=Ub[:, 4 + tt * 512:4 + (tt + 1) * 512], in_=ps[:, :], reads=[pk], writes=[ubk])
                flush_deferred(gate_ops, 3)

        def conv_conv(ci):
            hp, which = cts[ci]
            Ub, ubk, Dg = Ub2[ci % 2], ("Ub", ci % 2), Dg2[ci % 2]
            dstT, dk = (qTs[hp], ("qT", hp)) if which == "q" else (kTs[hp], ("kT", hp))
            for tt in range(NTT):
                pc, pck = pp[2 + tt % 2], PK[2 + tt % 2]
                for tap in range(4):
                    pe("matmul", pc[:, :], lhsT=Dg[tap], rhs=Ub[:, tt * 512 + tap + 1:tt * 512 + tap + 513], start=(tap == 0), stop=(tap == 3),
                       reads=[ubk, ("Dg", ci % 2, tap)], writes=[pck])
                act("activation", out=dstT[:, tt * 512:(tt + 1) * 512], in_=pc[:, :], func=AF.Silu, reads=[pck], writes=[dk])
                flush_deferred(gate_ops, 3)

        conv_proj(0)
        for ci in range(len(cts)):
            if ci + 1 < len(cts):
                conv_proj(ci + 1)
            conv_conv(ci)
        flush_deferred(gate_ops, len(gate_ops))
        P.barrier()
        AR.release(m_c)
        Cn = AR.f32(65)
        Cnb3 = [AR.bf16(66) for _ in range(3)]
        two = lambda f: [f() for _ in range(2)]
        four = lambda f: [f() for _ in range(4)]
        three = lambda f: [f() for _ in range(3)]
        eo, sig = two(lambda: AR.f32(128)), [AR.f32(128) for _ in range(6)]
        Vw = three(lambda: AR.bf16(130).rearrange("p (e d) -> p e d", d=65))
        Ktok = two(lambda: AR.bf16(128))
        scm = three(lambda: AR.bf16(256).rearrange("p (e t) -> p e t", t=128))
        t1 = four(lambda: AR.f32(130).rearrange("p (e d) -> p e d", d=65))
        t2 = four(lambda: AR.f32(130).rearrange("p (e d) -> p e d", d=65))
        den, rcp, ss, rstd = four(lambda: AR.f32(2)), four(lambda: AR.f32(2)), four(lambda: AR.f32(2)), four(lambda: AR.f32(2))
        hh = four(lambda: AR.f32(128))
        hsq = four(lambda: AR.f32(128))
        ymlS = four(lambda: AR.bf16(128))
        for hp in range(nhp_):
            wmv, wmk = ml_w[hp]
            qT, kT = qTs[hp], kTs[hp]
            QK, KK = ("qT", hp), ("kT", hp)
            bgc = None
            dve("memset", Cn, 0.0, writes=["Cn"])
            dve("memset", Cnb3[2][:, 0:66], 0.0, writes=[("Cnb", 2)])
            r0 = 6 + 2 * hp
            bc = lambda a_, n_: a_.unsqueeze(2).broadcast_to([128, 2, n_])

            pO1 = pp[6][:, 0:130].rearrange("p (e d) -> p e d", d=65)
            pO2 = [pp[7][:, 0:65], pp[6][:, 256:321]]
            pO2k = [PK[7], PK[6]]
            p7t = pp[7][:, :].bitcast(BF16)[:, 256:384]
            ok = lambda c: 0 <= c < nch_

            def CBs(c):
                return slice(c * 128, (c + 1) * 128)

            def a_abs(c):
                q_ = c % 4
                act("activation", out=den[q_], in_=t2[q_][:, :, 64], func=AF.Abs, reads=[("t2", q_)], writes=[("den", q_)])

            def d_reduce(c):
                q_ = c % 4
                dve("tensor_reduce", out=ss[q_], in_=hsq[q_].rearrange("p (e d) -> p e d", d=64), axis=AX.X, op=ALU.add, reads=[("hsq", q_)], writes=[("ss", q_)])

            def d_norm(c):
                q_ = c % 4
                dve("tensor_tensor", out=den[q_], in0=den[q_], in1=stat[3][:, c, r0:r0 + 2], op=ALU.max, reads=[("den", q_), ("stat", 3)], writes=[("den", q_)])
                dve("reciprocal", rcp[q_], den[q_], reads=[("den", q_)], writes=[("rcp", q_)])
                hh3 = hh[q_].rearrange("p (e d) -> p e d", d=64)
                pool("tensor_tensor", out=hh3, in0=t2[q_][:, :, 0:64], in1=bc(rcp[q_], 64), op=ALU.mult, reads=[("t2", q_), ("rcp", q_)], writes=[("hh", q_)])

            def p_U(c):
                s_, s3 = c % 2, c % 3
                pU = pp[2 + s_][:, 256:386].rearrange("p (e d) -> p e d", d=65)
                for e in range(2):
                    pe("matmul", pU[:, e, :], lhsT=Ktok[s_], rhs=Vw[s3][:, e, :], start=True, stop=True, reads=[("Ktok", s_), ("Vw", s3)], writes=[PK[2 + s_]])

            def p_out(c):
                s3 = c % 3
                CB = CBs(c)
                cprev = (c - 1) % 3
                for e in range(2):
                    pe("matmul", pO1[:, e, :], lhsT=scm[s3][:, e, :], rhs=Vw[s3][:, e, :], start=True, stop=True,
                       reads=[("scm", s3), ("Vw", s3)], writes=[PK[6]])
                for e in range(2):
                    ER = slice(e * 64, (e + 1) * 64)
                    pe("matmul", pO2[e], lhsT=qT[ER, CB], rhs=Cnb3[cprev][ER, 0:65], start=True, stop=True, reads=[QK, ("Cnb", cprev)], writes=[pO2k[e]])

            def a_ycopy(c):
                act("copy", out=yT[:, 5 + hp, c * 128:(c + 1) * 128], in_=p7t, reads=[PK[7]], writes=[("yT", 5 + hp, c // 4)])

            def a_rstd(c):
                q_ = c % 4
                act("activation", out=rstd[q_], in_=ss[q_], func=AF.Ln, bias=epsc, scale=1.0 / 64, reads=[("ss", q_), "c_eps"], writes=[("rstd", q_)])
                act("activation", out=rstd[q_], in_=rstd[q_], func=AF.Exp, scale=-0.5, reads=[("rstd", q_)], writes=[("rstd", q_)])

            def d_y(c):
                q_ = c % 4
                for e in range(2):
                    dve("scalar_tensor_tensor", out=ymlS[q_][:, e * 64:(e + 1) * 64], in0=hh[q_][:, e * 64:(e + 1) * 64],
                        scalar=rstd[q_][:, e:e + 1], in1=sig[c % 6][:, e * 64:(e + 1) * 64], op0=ALU.mult, op1=ALU.mult,
                        reads=[("hh", q_), ("rstd", q_), ("sig", c % 6)], writes=[("ymlS", q_)])

            def p_front(c):
                s_ = c % 2
                CB = CBs(c)
                pvo, vok = pp[s_], PK[s_]
                for kc in range(KC):
                    pe("matmul", pvo[:, 0:256], lhsT=xnT[:, kc, CB], rhs=wmv[:, kc, 256:512], start=(kc == 0), stop=(kc == KC - 1),
                       reads=[wmk, ("xnT", kc, c // 4)], writes=[vok])
                pkt = pp[2 + s_][:, :].bitcast(BF16)
                pe("transpose", pkt[:, 0:128], kT[:, CB], ident_b, reads=[KK, "c_identb"], writes=[PK[2 + s_]])
                for e in range(2):
                    ER = slice(e * 64, (e + 1) * 64)
                    pe("matmul", pp[4 + e][:, 0:128], lhsT=kT[ER, CB], rhs=qT[ER, CB], start=True, stop=True,
                       reads=[KK, QK], writes=[PK[4 + e]])

            def a_t1(c):
                q_ = c % 4
                for e in range(2):
                    act("activation", out=t1[q_][:, e, :], in_=pO1[:, e, :], func=AF.Copy, scale=stat[1][:, c, r0 + e:r0 + e + 1],
                        reads=[PK[6], ("stat", 1)], writes=[("t1", q_)])

            def a_front(c):
                s_, q_ = c % 2, c % 6
                pvo, vok = pp[s_], PK[s_]
                act("activation", out=eo[s_], in_=pvo[:, 128:256], func=AF.Exp, scale=-1.0, reads=[vok], writes=[("eo", s_)])
                act("activation", out=eo[s_], in_=eo[s_], func=AF.Ln, bias=onec, scale=1.0, reads=[("eo", s_), "c_one"], writes=[("eo", s_)])
                act("activation", out=eo[s_], in_=eo[s_], func=AF.Exp, scale=-1.0, reads=[("eo", s_)], writes=[("eo", s_)])
                pool("tensor_tensor", out=sig[q_], in0=eo[s_], in1=mlnw[:, l * 384 + hp * 128:l * 384 + (hp + 1) * 128], op=ALU.mult,
                     reads=[("eo", s_), "c_mlnw"], writes=[("sig", q_)])

            def d_front(c):
                s_, s3 = c % 2, c % 3
                pvo, vok = pp[s_], PK[s_]
                dve("tensor_tensor", out=Vw[s3][:, :, 0:64], in0=pvo[:, 0:128].rearrange("p (e d) -> p e d", d=64),
                    in1=bc(stat[0][:, c, r0:r0 + 2], 64), op=ALU.mult, reads=[vok, ("stat", 0)], writes=[("Vw", s3)])
                act("copy", out=Vw[s3][:, :, 64], in_=stat[0][:, c, r0:r0 + 2], reads=[("stat", 0)], writes=[("Vw", s3)])
                for e in range(2):
                    dve("tensor_tensor", out=scm[s3][:, e, :], in0=pp[4 + e][:, 0:128], in1=mask01T, op=ALU.mult,
                        reads=[PK[4 + e], "c_mask01T"], writes=[("scm", s3)])

            def a_ktok(c):
                s_ = c % 2
                pkt = pp[2 + s_][:, :].bitcast(BF16)
                act("copy", out=Ktok[s_], in_=pkt[:, 0:128], reads=[PK[2 + s_]], writes=[("Ktok", s_)])

            def d_cn(c):
                s_ = c % 2
                pU = pp[2 + s_][:, 256:386].rearrange("p (e d) -> p e d", d=65)
                for e in range(2):
                    ER = slice(e * 64, (e + 1) * 64)
                    dve("scalar_tensor_tensor", out=Cn[ER, :], in0=Cn[ER, :], scalar=decay_bc[ER, c, 2 * hp + e:2 * hp + e + 1], in1=pU[ER, e, :],
                        op0=ALU.mult, op1=ALU.add, reads=["Cn", "decay_bc", PK[2 + s_]], writes=["Cn"])

            def d_t2(c):
                q_ = c % 4
                for e in range(2):
                    dve("scalar_tensor_tensor", out=t2[q_][:, e, :], in0=pO2[e], scalar=stat[2][:, c, r0 + e:r0 + e + 1], in1=t1[q_][:, e, :],
                        op0=ALU.mult, op1=ALU.add, reads=[pO2k[e], ("stat", 2), ("t1", q_)], writes=[("t2", q_)])

            def a_cnb(c):
                act("copy", out=Cnb3[c % 3][:, 0:65], in_=Cn, reads=["Cn"], writes=[("Cnb", c % 3)])

            def a_square(c):
                q_ = c % 4
                act("activation", out=hsq[q_], in_=hh[q_], func=AF.Square, reads=[("hh", q_)], writes=[("hsq", q_)])

            def p_ytr(c):
                q_ = c % 4
                pe("transpose", p7t, ymlS[q_], ident_b, reads=[("ymlS", q_), "c_identb"], writes=[PK[7]])

            for st in range(nch_ + 7):
                if ok(st - 3): a_abs(st - 3)
                if ok(st - 4): d_reduce(st - 4)
                if ok(st - 1): p_U(st - 1)
                if ok(st - 2): p_out(st - 2)
                if ok(st - 6): a_ycopy(st - 6)
                if ok(st - 3): d_norm(st - 3)
                if ok(st - 4): a_rstd(st - 4)
                if ok(st - 5): d_y(st - 5)
                if ok(st): p_front(st)
                if ok(st - 1): d_cn(st - 1)
                if ok(st - 2): a_t1(st - 2)
                if ok(st): a_front(st)
                if ok(st): d_front(st)
                if ok(st): a_ktok(st)
                if ok(st - 2): d_t2(st - 2)
                if ok(st - 1): a_cnb(st - 1)
                if ok(st - 3): a_square(st - 3)
                if bgc is not None:
                    if next(bgc, "end") == "end":
                        bgc = None
                if ok(st - 5): p_ytr(st - 5)
            if bgc is not None:
                for _ in bgc:
                    pass
        if (stop or "").startswith("ml"):
            dump("qT", qT, [QK])
            dump("kT", kT, [KK])
            dump("yT", yT[:, 5:8, :], [("yT", 5 + k_, j_) for k_ in range(3) for j_ in range(NTT)])
            break
        P.barrier()
        AR.release(m_m)
        m_f = AR.mark()
        QE = [AR.bf16(T) for _ in range(2)]
        KE = [AR.bf16(T) for _ in range(2)]
        Vf = AR.bf16(NTB * 6 * 128).rearrange("p (b h d) -> p b h d", h=6, d=128)
        PT = [AR.bf16(512) for _ in range(4)]
        Rr = AR.f32(512)
        pool("memset", Vf[:, :, :, 64:128], 1.0, writes=["Vf_ones"])
        for i in range(2):
            pool("memset", KE[i][64:128, :], 0.0, writes=[("KEpad", i)])
            pool("memset", QE[i][64:128, :], 0.0, writes=[("QEpad", i)])
            pool("memset", KE[i][64:65, :], 1.0, writes=[("KEpad", i)])
        wvv, wvk = wload([w_in_d[l][:, 768:1152]], KC)
        for tb in range(NTB):
            ps, pk = pp[2 + tb % 2], PK[2 + tb % 2]
            for kc in range(KC):
                pe("matmul", ps[:, 0:384], lhsT=xnT[:, kc, tb * 128:(tb + 1) * 128], rhs=wvv[:, kc, :], start=(kc == 0), stop=(kc == KC - 1),
                   reads=[wvk, ("xnT", kc, tb // 4)], writes=[pk])
            o_, i_ = Vf[:, tb, :, 0:64], ps[:, 0:384].rearrange("p (h d) -> p h d", d=64)
            if tb % 2 == 0:
                dve("tensor_copy", o_, i_, reads=[pk], writes=["Vf"])
            else:
                act("copy", out=o_, in_=i_, reads=[pk], writes=["Vf"])
        wqv, wqk = wload([w_in_d[l][:, 0:384]], KC)
        wkv, wkk = wload([w_in_d[l][:, 384:768]], KC)

        def fox_proj(hd):
            b = hd % 2
            for tt in range(NTT):
                sl = slice(tt * 512, (tt + 1) * 512)
                ps, pk = pp[6], PK[6]
                proj_fm(wqv, wqk, hd * 64, 64, xnT, "xnT", KC, tt, ps, pk)
                dve("tensor_scalar", out=QE[b][0:64, sl], in0=ps[0:64, :], scalar1=0.125, scalar2=None, op0=ALU.mult,
                    reads=[pk], writes=[("QE", b)])
                yield
                ps, pk = pp[7], PK[7]
                proj_fm(wkv, wkk, hd * 64, 64, xnT, "xnT", KC, tt, ps, pk)
                dve("tensor_copy", KE[b][0:64, sl], ps[0:64, :], reads=[pk], writes=[("KE", b)])
                yield
            P.dma("sp", QE[b][64:65, :], Cb[hd:hd + 1, :], "qerow%d" % b, reads=["Cb"], writes=[("QE", b), ("QEpad", b)])

        ftiles = [(hd, j, kb) for hd in range(6) for j in range(NTT) for kb in range(4 * j + 4)]

        def f_geom(i):
            hd, j, kb = ftiles[i]
            ii = kb - 4 * j
            return hd, j, kb, ii, max(0, ii) * 128

        def f_S(i):
            hd, j, kb, ii, lo = f_geom(i)
            b = hd % 2
            pss, sk = pp[i % 4], PK[i % 4]
            pe("matmul", pss[:, lo:512], lhsT=KE[b][:, kb * 128:(kb + 1) * 128], rhs=QE[b][:, j * 512 + lo:(j + 1) * 512],
               start=True, stop=(ii < 0), reads=[("QE", b), ("KE", b), ("QEpad", b), ("KEpad", b)], writes=[sk])
            if ii >= 0:
                pe("matmul", pss[:, lo:lo + 128], lhsT=ident_b, rhs=maskneg, start=False, stop=True,
                   reads=["c_identb", "c_maskneg"], writes=[sk])

        def f_EXP(i):
            hd, j, kb, ii, lo = f_geom(i)
            pss, sk = pp[i % 4], PK[i % 4]
            act("activation", out=PT[i % 4][:, lo:512], in_=pss[:, lo:512], func=AF.Exp, bias=negcT[:, kb, hd:hd + 1], scale=1.0,
                reads=[sk, "negcT"], writes=[("PT", i % 4)])

        def f_AV(i):
            hd, j, kb, ii, lo = f_geom(i)
            grp = hd * NTT + j
            ybank = 4 + grp % 2
            psyT, yk = pp[ybank], PK[ybank]
            pe("matmul", psyT[:, lo:512], lhsT=Vf[:, kb, hd, :], rhs=PT[i % 4][:, lo:512],
               start=(kb == 0), stop=(kb == 4 * j + 3), reads=[("PT", i % 4), "Vf", "Vf_ones"], writes=[yk])
            if kb == 4 * j + 3:
                dve("reciprocal", Rr[64:128, :], psyT[64:128, :], reads=[yk], writes=["Rr"])
                dve("tensor_copy", Rr[0:64, :], Rr[64:128, :], reads=["Rr"], writes=["Rr"])
                dst = (hd % 2) * 64
                dve("tensor_tensor", out=yT[dst:dst + 64, hd // 2, j * 512:(j + 1) * 512], in0=psyT[0:64, :], in1=Rr[0:64, :], op=ALU.mult,
                    reads=[yk, "Rr"], writes=[("yT", hd // 2, j)])

        for _ in fox_proj(0):
            pass
        nft = len(ftiles)
        sb_w = {}
        LA = 3
        bg = None
        for i0 in range(min(LA, nft)):
            f_S(i0)
        for i in range(nft):
            hd, j, kb = ftiles[i]
            if j == 0 and kb == 0:
                bg = fox_proj(hd + 1) if hd + 1 < 6 else None
                if hd == 5:
                    sb_w["v"] = wload([w_in_d[l][:, 1670:1926]], KC)
                    sb_w["qk"] = wload([w_in_d[l][:, 1158:1670]], KC)
            if i + LA < nft:
                if ftiles[i + LA][0] != hd and bg is not None:
                    for _ in bg:
                        pass
                    bg = None
                f_S(i + LA)
            f_EXP(i)
            f_AV(i)
            if bg is not None and (i % 4 == 3):
                if next(bg, "end") == "end":
                    bg = None
        if stop == "fox":
            dump("yT", yT[:, 0:3, :], [("yT", k_, j_) for k_ in range(3) for j_ in range(NTT)])
            break
        P.barrier()
        AR.release(m_cb)
        m_s = AR.mark()
        qs = [AR.bf16(T) for _ in range(2)]
        ks = [AR.bf16(T) for _ in range(2)]
        Vs = AR.bf16(NTB * 256).rearrange("p (b n) -> p b n", n=256)
        Et = [AR.f32(512) for _ in range(4)]
        SPt = [AR.bf16(512) for _ in range(2)]
        Wt = [AR.f32(512) for _ in range(2)]
        At = [AR.bf16(512) for _ in range(2)]
        acc2 = [AR.bf16(512) for _ in range(2)]
        wsv, wsk = sb_w["v"]
        for tb in range(NTB):
            ps, pk = pp[2 + tb % 2], PK[2 + tb % 2]
            for kc in range(KC):
                pe("matmul", ps[:, 0:256], lhsT=xnT[:, kc, tb * 128:(tb + 1) * 128], rhs=wsv[:, kc, :], start=(kc == 0), stop=(kc == KC - 1),
                   reads=[wsk, ("xnT", kc, tb // 4)], writes=[pk])
            if tb % 2 == 0:
                dve("tensor_copy", Vs[:, tb, :], ps[:, 0:256], reads=[pk], writes=["Vs"])
            else:
                act("copy", out=Vs[:, tb, :], in_=ps[:, 0:256], reads=[pk], writes=["Vs"])
        wqkv, wqkk = sb_w["qk"]
        for hp in range(2):
            for tt in range(NTT):
                sl = slice(tt * 512, (tt + 1) * 512)
                ps, pk = pp[4 + tt % 2], PK[4 + tt % 2]
                proj_fm(wqkv, wqkk, hp * 128, 128, xnT, "xnT", KC, tt, ps, pk)
                dve("tensor_scalar", out=qs[hp][:, sl], in0=ps[:, :], scalar1=0.125, scalar2=None, op0=ALU.mult, reads=[pk], writes=[("qs", hp)])
                ps, pk = pp[6 + tt % 2], PK[6 + tt % 2]
                proj_fm(wqkv, wqkk, 256 + hp * 128, 128, xnT, "xnT", KC, tt, ps, pk)
                act("copy", out=ks[hp][:, sl], in_=ps[:, :], reads=[pk], writes=[("ks", hp)])
        wout_grp = []
        for g in range(2):
            wov, wok = wload([w_out_d[l][:, g * 512:(g + 1) * 512]], KC)
            wout_grp.append((wov, wok, [(ci * 128, g * 4 + ci) for ci in range(4)]))
        stiles = [(hd, j, kb) for hd in range(4) for j in range(NTT) for kb in range(4 * j + 3, -1, -1)]

        def s_geom(i):
            hd, j, kb = stiles[i]
            ii = kb - 4 * j
            lo = max(0, ii) * 128
            return hd, j, kb, ii, lo, slice(lo, 512), hd // 2, slice((hd % 2) * 64, (hd % 2) * 64 + 64)

        def s_Z(i):
            hd, j, kb, ii, lo, CS, hp, PR = s_geom(i)
            zb = (0, 1, 6)[i % 3]
            pe("matmul", pp[zb][:, CS], lhsT=ks[hp][PR, kb * 128:(kb + 1) * 128], rhs=qs[hp][PR, j * 512 + lo:(j + 1) * 512],
               start=True, stop=(ii < 0), reads=[("qs", hp), ("ks", hp)], writes=[PK[zb]])
            if ii >= 0:
                pe("matmul", pp[zb][:, lo:lo + 128], lhsT=ident_b, rhs=masknegS, start=False, stop=True,
                   reads=["c_identb", "c_masknegS"], writes=[PK[zb]])

        def s_E(i):
            hd, j, kb, ii, lo, CS, hp, PR = s_geom(i)
            zb = (0, 1, 6)[i % 3]
            act("activation", out=Et[i % 4][:, CS], in_=pp[zb][:, CS], func=AF.Exp, reads=[PK[zb]], writes=[("E", i % 4)])

        def s_SP(i):
            hd, j, kb, ii, lo, CS, hp, PR = s_geom(i)
            s_ = i % 2
            t_in_grp = (4 * j + 3) - kb
            acc_o, ako = acc2[t_in_grp % 2], ("acc", t_in_grp % 2)
            acc_n, akn = acc2[(t_in_grp + 1) % 2], ("acc", (t_in_grp + 1) % 2)
            if kb == 4 * j + 3:
                pool("memset", acc2[0], 0.0, writes=[("acc", 0)])
                pool("memset", acc2[1], 0.0, writes=[("acc", 1)])
            act("activation", out=SPt[s_][:, CS], in_=Et[i % 4][:, CS], func=AF.Ln, bias=onec, scale=1.0,
                reads=[("E", i % 4), "c_one"], writes=[("SP", s_)])
            pc, ck = pp[4 + s_], PK[4 + s_]
            lastblk = (kb == 4 * j + 3)
            pe("matmul", pc[:, CS], lhsT=tincl, rhs=SPt[s_][:, CS], start=True, stop=lastblk, reads=[("SP", s_), "c_tincl"], writes=[ck])
            if not lastblk:
                pe("matmul", pc[:, CS], lhsT=ones_b, rhs=acc_o[:, CS], start=False, stop=True, reads=[ako, "c_onesb"], writes=[ck])
            if kb > 0:
                pool("tensor_tensor", out=acc_n[:, CS], in0=acc_o[:, CS], in1=SPt[s_][:, CS], op=ALU.add, reads=[ako, ("SP", s_)], writes=[akn])
            for _w in range(2):
                pe("matmul", pp[7][:, :], lhsT=ones_b, rhs=xnT[:, _w, 0:512], start=True, stop=True, reads=["c_onesb"], writes=["ps7_dummy"])

        def s_W(i):
            hd, j, kb, ii, lo, CS, hp, PR = s_geom(i)
            s_ = i % 2
            grp = hd * NTT + j
            ybank = 2 + grp % 2
            psyT, yk = pp[ybank], PK[ybank]
            act("activation", out=Wt[s_][:, CS], in_=pp[4 + s_][:, CS], func=AF.Exp, scale=-1.0, reads=[PK[4 + s_]], writes=[("W", s_)])
            dve("tensor_tensor", out=At[s_][:, CS], in0=Et[i % 4][:, CS], in1=Wt[s_][:, CS], op=ALU.mult,
                reads=[("E", i % 4), ("W", s_)], writes=[("A", s_)])
            pe("matmul", psyT[0:64, CS], lhsT=Vs[:, kb, hd * 64:(hd + 1) * 64], rhs=At[s_][:, CS],
               start=(kb == 4 * j + 3), stop=(kb == 0), skip_group_check=True, reads=[("A", s_), "Vs"], writes=[yk])
            if kb == 0:
                dst = (hd % 2) * 64
                dve("tensor_copy", yT[dst:dst + 64, 3 + hd // 2, j * 512:(j + 1) * 512], psyT[0:64, :], reads=[yk], writes=[("yT", 3 + hd // 2, j)])

        nst = len(stiles)
        s_Z(0)
        if nst > 1:
            s_Z(1)
        s_E(0)
        for k in range(nst + 1):
            if k + 2 < nst:
                s_Z(k + 2)
            if k + 1 < nst:
                s_E(k + 1)
            if k < nst:
                s_SP(k)
            if k >= 1:
                s_W(k - 1)
        if stop == "sb":
            dump("yT", yT[:, 3:5, :], [("yT", 3 + k_, j_) for k_ in range(2) for j_ in range(NTT)])
            break
        P.barrier()
        AR.release(m_s)
        AR.release(m_layer)
        AR.bf16(KC * T)
        m_o = AR.mark()
        resid_add_ttmajor(wout_grp, yT, "yT", KC)
        if stop == "mixer":
            dump("yTall", yT[:, :, :], [("yT", k_, j_) for k_ in range(8) for j_ in range(NTT)])
            break
        P.barrier()
        AR.release(m_layer)

        m_x = AR.mark()
        memT = AR.f32(KC * MEM).rearrange("p (k m) -> p k m", m=MEM)
        mnT = AR.bf16(KC * MEM).rearrange("p (k m) -> p k m", m=MEM)
        KxT = AR.bf16(4 * MEM).rearrange("p (h m) -> p h m", m=MEM)
        Vx = AR.bf16(2 * 4 * 129).rearrange("p (b h d) -> p b h d", h=4, d=129)
        QxT = AR.bf16(4 * T).rearrange("p (h t) -> p h t", t=T)
        ox = AR.bf16(NTB * 512).rearrange("p (b n) -> p b n", n=512)
        oxT = AR.bf16(4 * T).rearrange("p (h t) -> p h t", t=T)
        PTx = [AR.bf16(512) for _ in range(3)]
        recx = AR.f32(4)
        rmsnorm(hT, "hT", T, 1 * DEPTH + l, xnT, "xnT")
        P.dma("sp", memT, memT_d.rearrange("(k p) m -> p k m", p=128), "memload", writes=[("memT", kc, 0) for kc in range(KC)])
        rmsnorm(memT, "memT", MEM, 2 * DEPTH + l, mnT, "mnT", tw=MEM)
        pool("memset", Vx[:, :, :, 128:129], 1.0, writes=["Vx"])
        wkxv, wkxk = wload([wx_kv_d[l][:, 0:512]], KC)
        for xh in range(4):
            ps, pk = pp[2 + xh % 2], PK[2 + xh % 2]
            for kc in range(KC):
                pe("matmul", ps[:, 0:MEM], lhsT=wkxv[:, kc, xh * 128:(xh + 1) * 128], rhs=mnT[:, kc, :], start=(kc == 0), stop=(kc == KC - 1),
                   reads=[wkxk, ("mnT", kc, 0)], writes=[pk])
            dve("tensor_copy", KxT[:, xh, :], ps[:, 0:MEM], reads=[pk], writes=["KxT"])
        wvxv, wvxk = wload([wx_kv_d[l][:, 512:1024]], KC)
        for mb in range(2):
            ps, pk = pp[4 + mb], PK[4 + mb]
            for kc in range(KC):
                pe("matmul", ps[:, :], lhsT=mnT[:, kc, mb * 128:(mb + 1) * 128], rhs=wvxv[:, kc, :], start=(kc == 0), stop=(kc == KC - 1),
                   reads=[wvxk, ("mnT", kc, 0)], writes=[pk])
            dve("tensor_copy", Vx[:, mb, :, 0:128], ps[:, :].rearrange("p (h d) -> p h d", d=128), reads=[pk], writes=["Vx"])
        wqxv, wqxk = wload([wx_q_d[l]], KC)
        for xh in range(4):
            for tt in range(NTT):
                ps, pk = pp[4 + tt % 2], PK[4 + tt % 2]
                proj_fm(wqxv, wqxk, xh * 128, 128, xnT, "xnT", KC, tt, ps, pk)
                if tt % 2 == 0:
                    dve("tensor_copy", QxT[:, xh, tt * 512:(tt + 1) * 512], ps[:, :], reads=[pk], writes=[("QxT", xh, tt)])
                else:
                    act("copy", out=QxT[:, xh, tt * 512:(tt + 1) * 512], in_=ps[:, :], reads=[pk], writes=[("QxT", xh, tt)])
        XS = float(128 ** -0.5)
        xtiles = [(xh, tt, mb) for xh in range(4) for tt in range(NTT) for mb in range(2)]
        XB = (0, 1, 6, 7)

        def x_S(i):
            xh, tt, mb = xtiles[i]
            bnk = XB[i % 4]
            pe("matmul", pp[bnk][:, :], lhsT=KxT[:, xh, mb * 128:(mb + 1) * 128], rhs=QxT[:, xh, tt * 512:(tt + 1) * 512], start=True, stop=True,
               reads=["KxT", ("QxT", xh, tt)], writes=[PK[bnk]])

        def x_rest(i):
            xh, tt, mb = xtiles[i]
            bnk = XB[i % 4]
            s_ = i % 3
            pOa = pp[2][:, 0:258].rearrange("p (q d) -> p q d", d=129)
            pOb = pp[3][:, 0:258].rearrange("p (q d) -> p q d", d=129)
            act("activation", out=PTx[s_], in_=pp[bnk][:, :], func=AF.Exp, scale=XS, reads=[PK[bnk]], writes=[("PTx", s_)])
            for qi in range(4):
                po, pok = (pOa, PK[2]) if qi < 2 else (pOb, PK[3])
                pe("matmul", po[:, qi % 2, :], lhsT=PTx[s_][:, qi * 128:(qi + 1) * 128], rhs=Vx[:, mb, xh, :],
                   start=(mb == 0 and qi % 2 == 0), stop=(mb == 1), skip_group_check=True, reads=[("PTx", s_), "Vx"], writes=[pok])
            if mb == 1:
                for half, (po, pok) in enumerate(((pOa, PK[2]), (pOb, PK[3]))):
                    dve("reciprocal", recx[:, 2 * half:2 * half + 2], po[:, :, 128], reads=[pok], writes=["recx"])
                    dve("tensor_tensor", out=ox[:, 4 * tt + 2 * half:4 * tt + 2 * half + 2, xh * 128:(xh + 1) * 128], in0=po[:, :, 0:128],
                        in1=recx[:, 2 * half:2 * half + 2].unsqueeze(2).broadcast_to([128, 2, 128]), op=ALU.mult,
                        reads=[pok, "recx"], writes=[("ox", tt)])

        nxt = len(xtiles)
        x_S(0)
        x_S(1)
        for i in range(nxt):
            if i + 2 < nxt:
                x_S(i + 2)
            x_rest(i)
        to_fm(ox, lambda tb: [("ox", tb // 4)], 4, oxT, "oxT")
        grp_ = []
        for g in range(2):
            wxov, wxok = wload([wx_o_d[l][:, g * 512:(g + 1) * 512]], 4)
            grp_.append((wxov, wxok, [(ci * 128, g * 4 + ci) for ci in range(4)]))
        resid_add_ttmajor(grp_, oxT, "oxT", 4)
        if stop == "cross":
            break
        P.barrier()
        AR.release(m_x)

        m_ff = AR.mark()
        NFH = 11
        aT = AR.bf16(NFH * T).rearrange("p (f t) -> p f t", t=T)
        sg = [AR.f32(512) for _ in range(2)]
        rmsnorm(hT, "hT", T, 3 * DEPTH + l, xnT, "xnT")
        cnt_s = 0
        for half in range(2):
            fbase = half * NFH * 128
            for g in range(6):
                nfc = 2 if g < 5 else 1
                c0 = fbase + g * 256
                wfv, wfk = wload([w_gate_d[l][:, c0:c0 + nfc * 128], w_up_d[l][:, c0:c0 + nfc * 128]], KC)
                for f_ in range(nfc):
                    fc = g * 2 + f_
                    for tt in range(NTT):
                        s_ = cnt_s % 2
                        cnt_s += 1
                        pg, pgk = pp[s_], PK[s_]
                        pu, puk = pp[2 + s_], PK[2 + s_]
                        proj_fm(wfv, wfk, f_ * 128, 128, xnT, "xnT", KC, tt, pg, pgk)
                        proj_fm(wfv, wfk, nfc * 128 + f_ * 128, 128, xnT, "xnT", KC, tt, pu, puk)
                        act("activation", out=sg[s_], in_=pg[:, :], func=AF.Silu, reads=[pgk], writes=[("sg", s_)])
                        dve("tensor_tensor", out=aT[:, fc, tt * 512:(tt + 1) * 512], in0=pu[:, :], in1=sg[s_], op=ALU.mult,
                            reads=[puk, ("sg", s_)], writes=[("aT", fc, tt)])
            for g in range(4):
                wdv, wdk = wload([w_down_d[l][fbase:fbase + NFH * 128, g * 256:(g + 1) * 256]], NFH)
                for ci in range(2):
                    resid_add(wdv, wdk, ci * 128, g * 2 + ci, aT, "aT", NFH)
        P.barrier()
        AR.release(m_ff)
        AR.release(m_layer)

    if stop is None:
        rmsnorm(hT, "hT", T, 8, hT, "hT")
    for (name, d, ap, keys) in dbg:
        P.dma("sp", d, ap, "dbg_" + name, reads=keys, is_out=True)
    P.barrier()
    for kc in range(KC):
        P.dma("sp", outT_d[kc * 128:(kc + 1) * 128, :], hT[:, kc, :], "ostore", reads=[("hT", kc, tt) for tt in range(NTT)], is_out=True)
    P.finish()
    P.emit(es)
    print('arena high-water', AR.hw, 'of', ARW)
    return nc, es


def prep_inputs(inp, b):
    f = lambda a: np.ascontiguousarray(np.asarray(a, dtype=np.float32))
    w_in = np.asarray(inp["w_in"], dtype=np.float32)
    FF, MI, MF = 1152, 3462, 3468
    w_gates = np.concatenate([w_in[:, :, FF:FF + 6], w_in[:, :, MF:MF + 6], w_in[:, :, FF:FF + 6], w_in[:, :, MI:MI + 6]], axis=2)
    nw = np.stack([inp["norm_mix_w"][0], inp["norm_mix_w"][1], inp["norm_x_w"][0], inp["norm_x_w"][1],
                   inp["mem_norm_w"][0], inp["mem_norm_w"][1], inp["norm_ffn_w"][0], inp["norm_ffn_w"][1],
                   inp["final_norm_w"]], axis=0)
    normw = np.asarray(nw, np.float32).reshape(9, KC, 128).transpose(2, 0, 1).reshape(128, 9 * KC)
    gb = np.zeros((12, DEPTH * 2), np.float32)
    for l in range(DEPTH):
        gb[0:6, 2 * l] = inp["fox_f_b"][l]
        gb[6:12, 2 * l] = inp["ml_f_b"][l]
        gb[0:6, 2 * l + 1] = inp["fox_f_b"][l]
        gb[6:12, 2 * l + 1] = inp["ml_i_b"][l]
    cw = np.asarray(inp["ml_conv_w"], np.float32)
    convw = cw.reshape(DEPTH, 4, 6, 128).transpose(3, 0, 2, 1).reshape(128, DEPTH * 24)
    mlnw = np.broadcast_to(np.asarray(inp["ml_norm_w"], np.float32).reshape(1, DEPTH * 384), (128, DEPTH * 384))
    return {
        "xT": f(np.asarray(inp["x"][b]).T), "memT": f(np.asarray(inp["mem"][b]).T),
        "normw": f(normw), "w_in": f(w_in), "w_gates": f(w_gates), "gate_b": f(gb), "convw": f(convw),
        "mlnw": f(mlnw), "w_out": f(inp["w_out"]), "wx_q": f(inp["wx_q"]), "wx_kv": f(inp["wx_kv"]),
        "wx_o": f(inp["wx_o"]), "w_gate": f(inp["w_gate"]), "w_up": f(inp["w_up"]), "w_down": f(inp["w_down"]),
    }


_CACHE = {}


def kernel(**inputs):
    if "nc" not in _CACHE:
        _CACHE["nc"] = build()
    nc, _es = _CACHE["nc"]
    shared = None
    in_maps = []
    for b in range(8):
        m = prep_inputs(inputs, b)
        if shared is None:
            shared = m
        else:
            for k in m:
                if k not in ("xT", "memT"):
                    m[k] = shared[k]
        in_maps.append(m)
    res = run_bass_kernel_spmd(nc, in_maps, core_ids=list(range(8)))
    out = np.stack([np.ascontiguousarray(r["outT"].T) for r in res.results], axis=0)
    return out.astype(np.float32)
```

```python
import numpy as np
from contextlib import ExitStack
import concourse.bass as bass
import concourse.mybir as mybir
from concourse.bass_utils import run_bass_kernel_spmd

F32 = mybir.dt.float32
BF16 = mybir.dt.bfloat16
AF = mybir.ActivationFunctionType
ALU = mybir.AluOpType
AX = mybir.AxisListType

ENGS = ("pe", "act", "dve", "pool", "sp")

D = 1024
T = 2048
KC = 8
NTB = 16
NTT = 4
DEPTH = 2
MEM = 256
DFF = 2816
NIN = 3474
EPS = 1e-6


class Prog:
    def __init__(self, nc):
        self.nc = nc
        self.ins = {e: [] for e in ENGS}
        self.known = {e: {} for e in ENGS}
        self.lastw = {}
        self.readers = {}
        self.dma_cnt = {}
        self.snap = {}
        self.out_tokens = []

    def _deps(self, eng, reads, writes):
        deps = {}

        def add(tok):
            if tok is None:
                return
            sk, v = tok
            if sk == "pe" and eng == "pe":
                return
            if deps.get(sk, -1) < v:
                deps[sk] = v
        for k in reads:
            add(self.lastw.get(k))
        for k in writes:
            add(self.lastw.get(k))
            for t in self.readers.get(k, ()):
                add(t)
        kn = self.known[eng]
        waits = []
        for sk, v in deps.items():
            if kn.get(sk, -1) >= v:
                continue
            waits.append((sk, v))
        for sk, v in waits:
            sn = self.snap.get((sk, v))
            if sn:
                for a, b in sn.items():
                    if kn.get(a, -1) < b:
                        kn[a] = b
            if kn.get(sk, -1) < v:
                kn[sk] = v
        return waits

    def _commit(self, tok, reads, writes):
        for k in reads:
            self.readers.setdefault(k, []).append(tok)
        for k in writes:
            self.lastw[k] = tok
            self.readers[k] = []

    def op(self, eng, meth, args, kw, reads=(), writes=()):
        waits = self._deps(eng, reads, writes)
        tok = (eng, len(self.ins[eng]))
        self.ins[eng].append(dict(fn=(meth, args, kw), waits=waits, tok=tok, dma=None))
        self.snap[tok] = dict(self.known[eng])
        self._commit(tok, reads, writes)
        return tok

    def dma(self, eng, out, in_, semname, reads=(), writes=(), is_out=False):
        fn = ("dma_start", (), dict(out=out, in_=in_))
        waits = self._deps(eng, reads, writes)
        sk = ("dma", semname)
        n = self.dma_cnt.get(sk, 0)
        self.dma_cnt[sk] = n + 1
        tok = (sk, n)
        self.ins[eng].append(dict(fn=fn, waits=waits, tok=tok, dma=sk))
        self.snap[tok] = dict(self.known[eng])
        self._commit(tok, reads, writes)
        if is_out:
            self.out_tokens.append(tok)
        return tok

    def barrier(self):
        last = []
        for e in ENGS:
            for rec in reversed(self.ins[e]):
                if rec["tok"] is not None and rec["dma"] is None:
                    last.append(rec["tok"])
                    break
        for sk, n in self.dma_cnt.items():
            last.append((sk, n - 1))
        for e in ENGS:
            kn = self.known[e]
            waits = []
            for sk, v in last:
                if sk == e:
                    continue
                if kn.get(sk, -1) >= v:
                    continue
                waits.append((sk, v))
                kn[sk] = v
            if waits:
                self.ins[e].append(dict(fn=None, waits=waits, tok=None, dma=None))
        self.lastw = {}
        self.readers = {}

    def finish(self, eng="sp"):
        best = {}
        for sk, v in self.out_tokens:
            if best.get(sk, -1) < v:
                best[sk] = v
        self.ins[eng].append(dict(fn=None, waits=list(best.items()), tok=None, dma=None))

    def emit(self, es):
        nc = self.nc
        needed = set()
        for e in ENGS:
            for rec in self.ins[e]:
                for w in rec["waits"]:
                    needed.add(w)
        rank = {}
        for e in ENGS:
            c = 0
            for rec in self.ins[e]:
                if rec["dma"] is None and rec["tok"] is not None and rec["tok"] in needed:
                    c += 1
                    rank[rec["tok"]] = c
        sems = {}
        for e in ENGS:
            sems[e] = es.enter_context(nc.semaphore("sem_" + e))
        for sk in self.dma_cnt:
            sems[sk] = es.enter_context(nc.semaphore("semd_" + str(sk[1])))

        def val(tok):
            sk, v = tok
            if isinstance(sk, tuple):
                return 16 * (v + 1)
            return rank[tok]

        def run(e, h):
            for rec in self.ins[e]:
                for w in rec["waits"]:
                    h.wait_ge(sems[w[0]], val(w))
                if rec["fn"] is None:
                    continue
                meth, a, kw = rec["fn"]
                ins = getattr(h, meth)(*a, **kw)
                if rec["dma"] is not None:
                    ins.then_inc(sems[rec["dma"]], 16)
                elif rec["tok"] in needed:
                    ins.then_inc(sems[e], 1)

        block = es.enter_context(nc.Block())

        @block.tensor
        def _(h):
            run("pe", h)

        @block.scalar
        def _(h):
            run("act", h)

        @block.vector
        def _(h):
            run("dve", h)

        @block.gpsimd
        def _(h):
            run("pool", h)

        @block.sync
        def _(h):
            run("sp", h)


class Arena:
    def __init__(self, ap, nwords):
        self.ap = ap
        self.n = nwords
        self.top = 0

    def f32(self, cols):
        a = self.ap[:, self.top:self.top + cols]
        self.top += cols
        self.hw = max(getattr(self, "hw", 0), self.top)
        assert self.top <= self.n, ("arena overflow", self.top, self.n)
        return a

    def bf16(self, cols):
        w = (cols + 1) // 2
        a = self.ap[:, self.top:self.top + w].bitcast(BF16)
        self.top += w
        self.hw = max(getattr(self, "hw", 0), self.top)
        assert self.top <= self.n, ("arena overflow", self.top, self.n)
        return a[:, 0:cols]

    def mark(self):
        return self.top

    def release(self, m):
        self.top = m


def build(nlayers=DEPTH, stop=None):
    nc = bass.Bass("TRN2", target_bir_lowering=False, dynamic_dma_scratch_size=4096)

    def din(name, shape):
        return nc.dram_tensor(name, list(shape), F32, kind="ExternalInput").ap()

    xT_d = din("xT", [D, T])
    memT_d = din("memT", [D, MEM])
    normw_d = din("normw", [128, 9 * KC])
    w_in_d = din("w_in", [DEPTH, D, NIN])
    wg_d = din("w_gates", [DEPTH, D, 24])
    gb_d = din("gate_b", [12, DEPTH * 2])
    convw_d = din("convw", [128, DEPTH * 6 * 4])
    mlnw_d = din("mlnw", [128, DEPTH * 384])
    w_out_d = din("w_out", [DEPTH, D, D])
    wx_q_d = din("wx_q", [DEPTH, D, 512])
    wx_kv_d = din("wx_kv", [DEPTH, D, 1024])
    wx_o_d = din("wx_o", [DEPTH, 512, D])
    w_gate_d = din("w_gate", [DEPTH, D, DFF])
    w_up_d = din("w_up", [DEPTH, D, DFF])
    w_down_d = din("w_down", [DEPTH, DFF, D])
    outT_d = nc.dram_tensor("outT", [D, T], F32, kind="ExternalOutput").ap()

    es = ExitStack()
    P = Prog(nc)
    dbg = []

    def dump(name, ap, keys):
        d = nc.dram_tensor("dbg_" + name, list(ap.shape), ap.dtype, kind="ExternalOutput").ap()
        dbg.append((name, d, ap, list(keys)))

    sbt = lambda name, shape, dt: es.enter_context(nc.sbuf_tensor(name, shape, dt))
    hT = sbt("hT", [128, KC, T], F32)
    xnT = sbt("xnT", [128, KC, T], BF16)
    NSLOT = 3
    WSLOT = 4096
    wring = sbt("wring", [128, NSLOT, WSLOT], BF16)
    consts = sbt("consts", [128, 1280], F32)
    mlnw = sbt("mlnw_sb", [128, DEPTH * 384], F32)
    ARW = 20200 + 3072
    arena_t = sbt("arena", [128, ARW], F32)
    AR = Arena(arena_t, ARW)
    pp = [es.enter_context(nc.psum_tensor("pp%d" % i, [128, 512], F32)) for i in range(8)]
    PK = ["ps%d" % i for i in range(8)]

    def mk(eng):
        def f(meth, *a, reads=(), writes=(), **kw):
            reads, writes = list(reads), list(writes)
            if eng != "pe":
                for k in reads:
                    if isinstance(k, str) and k.startswith("ps") and k not in writes:
                        writes.append(k)
            if defer["list"] is not None:
                defer["list"].append((eng, meth, a, kw, reads, writes))
                return None
            return P.op(eng, meth, a, kw, reads, writes)
        return f

    defer = {"list": None}

    def flush_deferred(lst, n):
        for _ in range(n):
            if not lst:
                return
            eng_, meth_, a_, kw_, r_, w_ = lst.pop(0)
            P.op(eng_, meth_, a_, kw_, r_, w_)
    pe, act, dve, pool = mk("pe"), mk("act"), mk("dve"), mk("pool")

    CA = Arena(consts, 1280)
    ident_f = CA.f32(128)
    ident_b = CA.bf16(128)
    ones_b = CA.bf16(128)
    maskneg = CA.bf16(128)
    strict01 = CA.f32(128)
    tincl = CA.bf16(128)
    mask01T = CA.f32(128)
    zeros_f = CA.f32(128)
    ones_f = CA.f32(128)
    normw = CA.f32(9 * KC)
    gate_b = CA.f32(DEPTH * 2)
    neg_gb = CA.f32(DEPTH * 2)
    convw = CA.f32(DEPTH * 24)
    epsc = CA.f32(1)
    onec = CA.f32(1)
    msel = CA.f32(6)
    ln8c = CA.f32(1)
    masknegS = CA.bf16(128)

    pool("memset", zeros_f, 0.0, writes=["c_zeros"])
    pool("memset", ones_f, 1.0, writes=["c_ones"])
    pool("memset", ones_b, 1.0, writes=["c_onesb"])
    pool("memset", epsc, EPS, writes=["c_eps"])
    pool("memset", onec, 1.0, writes=["c_one"])
    pool("memset", ln8c, float(np.log(8.0)), writes=["c_ln8"])

    def asel(out, in_, pattern, cm, cmp, fill, base=0, rk=(), wk=()):
        pool("affine_select", out=out, in_=in_, pattern=pattern, compare_op=cmp, fill=fill, base=base,
             channel_multiplier=cm, reads=rk, writes=wk)

    asel(ident_f, ones_f, [[-1, 128]], 1, ALU.is_equal, 0.0, rk=["c_ones"], wk=["c_identf"])
    asel(ident_b, ones_f, [[-1, 128]], 1, ALU.is_equal, 0.0, rk=["c_ones"], wk=["c_identb"])
    asel(maskneg, zeros_f, [[1, 128]], -1, ALU.is_ge, -30000.0, rk=["c_zeros"], wk=["c_maskneg"])
    asel(strict01, ones_f, [[1, 128]], -1, ALU.is_gt, 0.0, rk=["c_ones"], wk=["c_strict"])
    asel(masknegS, zeros_f, [[1, 128]], -1, ALU.is_gt, -30000.0, rk=["c_zeros"], wk=["c_masknegS"])
    asel(tincl, ones_f, [[-1, 128]], 1, ALU.is_ge, 0.0, rk=["c_ones"], wk=["c_tincl"])
    asel(mask01T, ones_f, [[1, 128]], -1, ALU.is_ge, 0.0, rk=["c_ones"], wk=["c_mask01T"])
    asel(msel[0:12, :], ones_f[0:12, 0:6], [[-1, 6]], 1, ALU.is_equal, 0.0, base=-6, rk=["c_ones"], wk=["c_msel"])

    P.dma("sp", normw, normw_d, "cst0", writes=["c_normw"])
    P.dma("sp", gate_b[0:12, :], gb_d, "cst1", writes=["c_gateb"])
    P.dma("sp", convw, convw_d, "cst2", writes=["c_convw"])
    P.dma("sp", mlnw[:], mlnw_d, "cst3", writes=["c_mlnw"])
    dve("tensor_scalar", out=neg_gb[0:12, :], in0=gate_b[0:12, :], scalar1=-1.0, scalar2=None, op0=ALU.mult,
        reads=["c_gateb"], writes=["c_neggb"])

    for kc in range(KC):
        P.dma("sp", hT[:, kc, :], xT_d[kc * 128:(kc + 1) * 128, :], "hload%d" % kc, writes=[("hT", kc, tt) for tt in range(NTT)])

    wstate = dict(n=0)

    def wload(segs, kcn):
        tot = sum(s.shape[1] for s in segs)
        assert kcn * tot <= WSLOT, (kcn, tot)
        s_ = wstate["n"] % NSLOT
        wstate["n"] += 1
        key = ("w", s_)
        view = wring[:, s_, 0:kcn * tot].rearrange("p (k n) -> p k n", n=tot)
        off = 0
        for sg in segs:
            n = sg.shape[1]
            P.dma("pool", view[:, :, off:off + n], sg.rearrange("(k p) n -> p k n", p=128), "w%d" % s_, writes=[key])
            off += n
        return view, key

    def rmsnorm(src, skey, ntok, wcol, dst, dkey, tw=512, scr=None):
        m = AR.mark()
        if scr is None:
            sq = [AR.bf16(tw) for _ in range(2)]
            lnv = AR.f32(tw)
            rstd = AR.f32(tw)
        else:
            sq = [scr[:, 0:tw // 2].bitcast(BF16), scr[:, tw // 2:tw].bitcast(BF16)]
            lnv = scr[:, tw:2 * tw]
            rstd = scr[:, 2 * tw:3 * tw]
        for tt in range(ntok // tw):
            sl = slice(tt * tw, (tt + 1) * tw)
            ps, pk = pp[tt % 2], PK[tt % 2]
            for kc in range(KC):
                s, sk = sq[kc % 2], ("nsq", kc % 2)
                act("activation", out=s, in_=src[:, kc, sl], func=AF.Square, reads=[(skey, kc, tt)], writes=[sk])
                pe("matmul", ps[:, 0:tw], lhsT=ones_b, rhs=s, start=(kc == 0), stop=(kc == KC - 1), reads=[sk, "c_onesb"], writes=[pk])
            act("activation", out=lnv, in_=ps[:, 0:tw], func=AF.Ln, bias=epsc, scale=1.0 / D, reads=[pk, "c_eps"], writes=["nln"])
            act("activation", out=rstd, in_=lnv, func=AF.Exp, scale=-0.5, reads=["nln"], writes=["nrstd"])
            for kc in range(KC):
                dve("scalar_tensor_tensor", out=dst[:, kc, sl], in0=src[:, kc, sl], scalar=normw[:, wcol * KC + kc:wcol * KC + kc + 1],
                    in1=rstd, op0=ALU.mult, op1=ALU.mult, reads=[(skey, kc, tt), "nrstd", "c_normw"], writes=[(dkey, kc, tt)])
        AR.release(m)

    def proj_fm(wv, wkey, c0, mcols, xT_, xkey, kcn, tt, ps, pk, tw=512):
        for kc in range(kcn):
            pe("matmul", ps[0:mcols, 0:tw], lhsT=wv[:, kc, c0:c0 + mcols], rhs=xT_[:, kc, tt * tw:(tt + 1) * tw],
               start=(kc == 0), stop=(kc == kcn - 1), reads=[wkey, (xkey, kc, tt)], writes=[pk])

    def resid_add(wv, wkey, ncol0, nch, xT_, xkey, kcn):
        for tt in range(NTT):
            b = (nch * NTT + tt) % 2
            proj_fm(wv, wkey, ncol0, 128, xT_, xkey, kcn, tt, pp[4 + b], PK[4 + b])
            sl = slice(tt * 512, (tt + 1) * 512)
            dve("tensor_tensor", out=hT[:, nch, sl], in0=pp[4 + b][:, :], in1=hT[:, nch, sl], op=ALU.add,
                reads=[PK[4 + b], ("hT", nch, tt)], writes=[("hT", nch, tt)])

    def resid_add_ttmajor(groups, xT_, xkey, kcn):
        cnt_ = 0
        for tt in range(NTT):
            sl = slice(tt * 512, (tt + 1) * 512)
            for (wv, wk, lst) in groups:
                for (c0, nch) in lst:
                    b = cnt_ % 2
                    cnt_ += 1
                    proj_fm(wv, wk, c0, 128, xT_, xkey, kcn, tt, pp[4 + b], PK[4 + b])
                    dve("tensor_tensor", out=hT[:, nch, sl], in0=pp[4 + b][:, :], in1=hT[:, nch, sl], op=ALU.add,
                        reads=[PK[4 + b], ("hT", nch, tt)], writes=[("hT", nch, tt)])

    def to_fm(src, skeyf, nfc, dst, dkey):
        for tb in range(NTB):
            b = tb % 2
            pt = pp[6 + b][:, :].bitcast(BF16)
            for fc in range(nfc):
                pe("transpose", pt[:, fc * 128:(fc + 1) * 128], src[:, tb, fc * 128:(fc + 1) * 128], ident_b,
                   reads=list(skeyf(tb)) + ["c_identb"], writes=[PK[6 + b]])
            o = dst[:, 0:nfc, tb * 128:(tb + 1) * 128]
            i = pt[:, 0:nfc * 128].rearrange("p (f t) -> p f t", t=128)
            if tb % 2 == 0:
                dve("tensor_copy", o, i, reads=[PK[6 + b]], writes=[(dkey, fc, tb // 4) for fc in range(nfc)])
            else:
                act("copy", out=o, in_=i, reads=[PK[6 + b]], writes=[(dkey, fc, tb // 4) for fc in range(nfc)])

    m_persist = AR.mark()
    R12 = slice(0, 12)
    done = False
    for l in range(nlayers):
        rmsnorm(hT, "hT", T, 0 * DEPTH + l, xnT, "xnT", scr=arena_t[:, AR.top + 6400:AR.top + 6400 + 1536])
        m_layer = AR.mark()
        y0 = AR.mark()
        yT = AR.bf16(KC * T).rearrange("p (k t) -> p k t", t=T)
        negcT = AR.f32(192).rearrange("p (b r) -> p b r", r=12)
        stat = [AR.f32(192).rearrange("p (b r) -> p b r", r=12) for _ in range(4)]
        decay_bc = AR.f32(96).rearrange("p (b r) -> p b r", r=6)
        m_cb = AR.mark()
        Cb = AR.bf16(T)

        wgv, wgk = wload([wg_d[l]], KC)
        X1, X2, X3 = (arena_t[:, y0 + i_ * T:y0 + (i_ + 1) * T] for i_ in range(3))
        Mp, Dexp, dec = arena_t[:, y0 + 3 * T:y0 + 3 * T + 17], arena_t[:, y0 + 3 * T + 32:y0 + 3 * T + 128], arena_t[:, y0 + 3 * T + 128:y0 + 3 * T + 144]
        for gi in range(2):
            for tt in range(NTT):
                ps, pk = pp[4 + tt % 2], PK[4 + tt % 2]
                sl = slice(tt * 512, (tt + 1) * 512)
                proj_fm(wgv, wgk, gi * 12, 12, xnT, "xnT", KC, tt, ps, pk)
                if gi == 0:
                    act("activation", out=X1[R12, sl], in_=ps[R12, :], func=AF.Exp, bias=neg_gb[R12, 2 * l:2 * l + 1], scale=-1.0,
                        reads=[pk, "c_neggb"], writes=["X1"])
                else:
                    act("activation", out=X2[R12, sl], in_=ps[R12, :], func=AF.Identity, bias=gate_b[R12, 2 * l + 1:2 * l + 2], scale=1.0,
                        reads=[pk, "c_gateb"], writes=["X2"])
        gate_ops = []
        defer["list"] = gate_ops
        act("activation", out=X1[R12, :], in_=X1[R12, :], func=AF.Ln, bias=onec[R12, :], scale=1.0, reads=["X1", "c_one"], writes=["X1"])
        dve("tensor_tensor_scan", out=X3[R12, :], data0=ones_f[R12, 0:1].broadcast_to([12, T]), data1=X1[R12, :], initial=0.0,
            op0=ALU.mult, op1=ALU.subtract, reads=["X1", "c_ones"], writes=["X3"])
        dve("tensor_copy", Cb[R12, :], X3[R12, :], reads=["X3"], writes=["Cb"])
        for blk in range(NTB):
            pe("transpose", pp[4][:, blk * 12:(blk + 1) * 12], X3[R12, blk * 128:(blk + 1) * 128], ident_f[R12, 0:12],
               reads=["X3", "c_identf"], writes=[PK[4]])
        dve("tensor_scalar", out=negcT.rearrange("p b r -> p (b r)"), in0=pp[4][:, 0:192], scalar1=-1.0, scalar2=None, op0=ALU.mult,
            reads=[PK[4]], writes=["negcT"])
        dve("tensor_tensor", out=X2[R12, :], in0=X2[R12, :], in1=X3[R12, :], op=ALU.subtract, reads=["X2", "X3"], writes=["X2"])
        dve("tensor_tensor_scan", out=X1[R12, :], data0=X2[R12, :], data1=X2[R12, :], initial=0.0, op0=ALU.max, op1=ALU.max,
            reads=["X2", "X1"], writes=["X1"])
        dve("memset", Mp[R12, 0:1], 0.0, writes=["Mp"])
        v3 = lambda a: a[R12, :].rearrange("p (c t) -> p c t", t=128)
        dve("tensor_copy", Mp[R12, 1:17], v3(X1)[:, :, 127], reads=["X1", "Mp"], writes=["Mp"])
        mend_bc = Mp[R12, 1:17].unsqueeze(2).broadcast_to([12, 16, 128])
        mprev_bc = Mp[R12, 0:16].unsqueeze(2).broadcast_to([12, 16, 128])
        dve("tensor_tensor", out=dec[R12, :], in0=Mp[R12, 0:16], in1=Mp[R12, 1:17], op=ALU.subtract, reads=["Mp"], writes=["dec"])
        act("activation", out=dec[R12, :], in_=dec[R12, :], func=AF.Exp, reads=["dec"], writes=["dec"])
        dve("tensor_tensor", out=v3(X2), in0=v3(X2), in1=mend_bc, op=ALU.subtract, reads=["Mp", "X2"], writes=["X2"])
        dve("scalar_tensor_tensor", out=X3[R12, :], in0=X3[R12, :], scalar=-1.0, in1=X1[R12, :], op0=ALU.mult, op1=ALU.subtract,
            reads=["X3", "X1", "Cb", PK[4]], writes=["X3"])

        def exp_tr(qi, X, xk):
            if qi == 3:
                act("activation", out=X[R12, :], in_=X[R12, :], func=AF.Exp, bias=ln8c[R12, :], scale=1.0, reads=[xk, "c_ln8"], writes=[xk])
            else:
                act("activation", out=X[R12, :], in_=X[R12, :], func=AF.Exp, reads=[xk], writes=[xk])
            bank = 4 + qi
            for blk in range(NTB):
                pe("transpose", pp[bank][:, blk * 12:(blk + 1) * 12], X[R12, blk * 128:(blk + 1) * 128], ident_f[R12, 0:12],
                   reads=[xk, "c_identf", "negcT"], writes=[PK[bank]])
            dve("tensor_copy", stat[qi].rearrange("p b r -> p (b r)"), pp[bank][:, 0:192], reads=[PK[bank]], writes=[("stat", qi)])

        exp_tr(0, X2, "X2")
        dve("tensor_tensor", out=v3(X2), in0=mend_bc, in1=v3(X1), op=ALU.subtract, reads=["Mp", "X1", "X2"], writes=["X2"])
        exp_tr(1, X2, "X2")
        dve("tensor_tensor", out=v3(X1), in0=mprev_bc, in1=v3(X1), op=ALU.subtract, reads=["Mp", "X1"], writes=["X1"])
        exp_tr(2, X1, "X1")
        exp_tr(3, X3, "X3")
        dexp3 = Dexp[R12, :].rearrange("p (c h) -> p c h", h=6)
        for hh in range(6):
            dve("tensor_scalar", out=dexp3[:, :, hh], in0=dec[R12, :], scalar1=msel[R12, hh:hh + 1], scalar2=None, op0=ALU.mult,
                reads=["dec", "c_msel"], writes=["Dexp"])
        pe("matmul", pp[6][:, 0:96], lhsT=ones_f[R12, :], rhs=Dexp[R12, :], start=True, stop=True, reads=["Dexp", "c_ones"], writes=[PK[6]])
        dve("tensor_copy", decay_bc.rearrange("p b r -> p (b r)"), pp[6][:, 0:96], reads=[PK[6]], writes=["decay_bc"])
        defer["list"] = None
        if stop == "gates":
            flush_deferred(gate_ops, len(gate_ops))
            dump("negcT", negcT.rearrange("p b r -> p (b r)"), ["negcT"])
            for qi in range(4):
                dump("stat%d" % qi, stat[qi].rearrange("p b r -> p (b r)"), [("stat", qi)])
            dump("decay", decay_bc.rearrange("p b r -> p (b r)"), ["decay_bc"])
            break

        m_m = AR.mark()
        qTs, kTs = [AR.bf16(T) for _ in range(3)], [AR.bf16(T) for _ in range(3)]
        m_c = AR.mark()
        Ub2 = [AR.f32(T + 8) for _ in range(2)]
        Dg2 = [[AR.f32(128) for _ in range(4)] for _ in range(2)]
        for i_ in range(2):
            dve("memset", Ub2[i_][:, 0:4], 0.0, writes=[("Ub", i_)])
        MQ, MV, MO = 1926, 2694, 3078
        nhp_, nch_ = (int(stop.split(":")[1]), int(stop.split(":")[2])) if (stop or "").startswith("ml:") else (3, NTB)
        ml_w = {}
        cts = [(hp, which) for hp in range(nhp_) for which in ("q", "k")]

        def conv_proj(ci):
            hp, which = cts[ci]
            if which == "q":
                ml_w[hp] = wload([w_in_d[l][:, MQ + hp * 128:MQ + (hp + 1) * 128], w_in_d[l][:, MQ + 384 + hp * 128:MQ + 384 + (hp + 1) * 128],
                                  w_in_d[l][:, MV + hp * 128:MV + (hp + 1) * 128], w_in_d[l][:, MO + hp * 128:MO + (hp + 1) * 128]], KC)
            wmv, wmk = ml_w[hp]
            c0, chunk = (0, hp) if which == "q" else (128, 3 + hp)
            Ub, ubk, Dg = Ub2[ci % 2], ("Ub", ci % 2), Dg2[ci % 2]
            for tap in range(4):
                col = (l * 6 + chunk) * 4 + tap
                dve("tensor_scalar", out=Dg[tap], in0=ident_f, scalar1=convw[:, col:col + 1], scalar2=None, op0=ALU.mult,
                    reads=["c_identf", "c_convw"], writes=[("Dg", ci % 2, tap)])
            for tt in range(NTT):
                ps, pk = pp[tt % 2], PK[tt % 2]
                proj_fm(wmv, wmk, c0, 128, xnT, "xnT", KC, tt, ps, pk)
                if tt % 2 == 0:
                    dve("tensor_copy", Ub[:, 4 + tt * 512:4 + (tt + 1) * 512], ps[:, :], reads=[pk], writes=[ubk])
                else:
                    act("copy", out=Ub[:, 4 + tt * 512:4 + (tt + 1) * 512], in_=ps[:, :], reads=[pk], writes=[ubk])
                flush_deferred(gate_ops, 3)

        def conv_conv(ci):
            hp, which = cts[ci]
            Ub, ubk, Dg = Ub2[ci % 2], ("Ub", ci % 2), Dg2[ci % 2]
            dstT, dk = (qTs[hp], ("qT", hp)) if which == "q" else (kTs[hp], ("kT", hp))
            for tt in range(NTT):
                pc, pck = pp[2 + tt % 2], PK[2 + tt % 2]
                for tap in range(4):
                    pe("matmul", pc[:, :], lhsT=Dg[tap], rhs=Ub[:, tt * 512 + tap + 1:tt * 512 + tap + 513], start=(tap == 0), stop=(tap == 3),
                       reads=[ubk, ("Dg", ci % 2, tap)], writes=[pck])
                act("activation", out=dstT[:, tt * 512:(tt + 1) * 512], in_=pc[:, :], func=AF.Silu, reads=[pck], writes=[dk])
                flush_deferred(gate_ops, 3)

        conv_proj(0)
        for ci in range(len(cts)):
            if ci + 1 < len(cts):
                conv_proj(ci + 1)
            conv_conv(ci)
        flush_deferred(gate_ops, len(gate_ops))
        P.barrier()
        AR.release(m_c)
        Cn = AR.f32(65)
        Cnb3 = [AR.bf16(66) for _ in range(3)]
        two = lambda f: [f() for _ in range(2)]
        four = lambda f: [f() for _ in range(4)]
        three = lambda f: [f() for _ in range(3)]
        eo, sig = two(lambda: AR.f32(128)), [AR.f32(128) for _ in range(6)]
        Vw = three(lambda: AR.bf16(130).rearrange("p (e d) -> p e d", d=65))
        Ktok = two(lambda: AR.bf16(128))
        scm = three(lambda: AR.bf16(256).rearrange("p (e t) -> p e t", t=128))
        t1 = four(lambda: AR.f32(130).rearrange("p (e d) -> p e d", d=65))
        t2 = four(lambda: AR.f32(130).rearrange("p (e d) -> p e d", d=65))
        den, rcp, ss, rstd = four(lambda: AR.f32(2)), four(lambda: AR.f32(2)), four(lambda: AR.f32(2)), four(lambda: AR.f32(2))
        hh = four(lambda: AR.f32(128))
        hsq = four(lambda: AR.f32(128))
        ymlS = four(lambda: AR.bf16(128))
        for hp in range(nhp_):
            wmv, wmk = ml_w[hp]
            qT, kT = qTs[hp], kTs[hp]
            QK, KK = ("qT", hp), ("kT", hp)
            bgc = None
            dve("memset", Cn, 0.0, writes=["Cn"])
            dve("memset", Cnb3[2][:, 0:66], 0.0, writes=[("Cnb", 2)])
            r0 = 6 + 2 * hp
            bc = lambda a_, n_: a_.unsqueeze(2).broadcast_to([128, 2, n_])

            pO1 = pp[6][:, 0:130].rearrange("p (e d) -> p e d", d=65)
            pO2 = [pp[7][:, 0:65], pp[6][:, 256:321]]
            pO2k = [PK[7], PK[6]]
            p7t = pp[7][:, :].bitcast(BF16)[:, 256:384]
            ok = lambda c: 0 <= c < nch_

            def CBs(c):
                return slice(c * 128, (c + 1) * 128)

            def a_abs(c):
                q_ = c % 4
                act("activation", out=den[q_], in_=t2[q_][:, :, 64], func=AF.Abs, reads=[("t2", q_)], writes=[("den", q_)])

            def d_reduce(c):
                q_ = c % 4
                dve("tensor_reduce", out=ss[q_], in_=hsq[q_].rearrange("p (e d) -> p e d", d=64), axis=AX.X, op=ALU.add, reads=[("hsq", q_)], writes=[("ss", q_)])

            def d_norm(c):
                q_ = c % 4
                dve("tensor_tensor", out=den[q_], in0=den[q_], in1=stat[3][:, c, r0:r0 + 2], op=ALU.max, reads=[("den", q_), ("stat", 3)], writes=[("den", q_)])
                dve("reciprocal", rcp[q_], den[q_], reads=[("den", q_)], writes=[("rcp", q_)])
                hh3 = hh[q_].rearrange("p (e d) -> p e d", d=64)
                pool("tensor_tensor", out=hh3, in0=t2[q_][:, :, 0:64], in1=bc(rcp[q_], 64), op=ALU.mult, reads=[("t2", q_), ("rcp", q_)], writes=[("hh", q_)])

            def p_U(c):
                s_, s3 = c % 2, c % 3
                pU = pp[2 + s_][:, 256:386].rearrange("p (e d) -> p e d", d=65)
                for e in range(2):
                    pe("matmul", pU[:, e, :], lhsT=Ktok[s_], rhs=Vw[s3][:, e, :], start=True, stop=True, reads=[("Ktok", s_), ("Vw", s3)], writes=[PK[2 + s_]])

            def p_out(c):
                s3 = c % 3
                CB = CBs(c)
                cprev = (c - 1) % 3
                for e in range(2):
                    pe("matmul", pO1[:, e, :], lhsT=scm[s3][:, e, :], rhs=Vw[s3][:, e, :], start=True, stop=True,
                       reads=[("scm", s3), ("Vw", s3)], writes=[PK[6]])
                for e in range(2):
                    ER = slice(e * 64, (e + 1) * 64)
                    pe("matmul", pO2[e], lhsT=qT[ER, CB], rhs=Cnb3[cprev][ER, 0:65], start=True, stop=True, reads=[QK, ("Cnb", cprev)], writes=[pO2k[e]])

            def a_ycopy(c):
                act("copy", out=yT[:, 5 + hp, c * 128:(c + 1) * 128], in_=p7t, reads=[PK[7]], writes=[("yT", 5 + hp, c // 4)])

            def a_rstd(c):
                q_ = c % 4
                act("activation", out=rstd[q_], in_=ss[q_], func=AF.Ln, bias=epsc, scale=1.0 / 64, reads=[("ss", q_), "c_eps"], writes=[("rstd", q_)])
                act("activation", out=rstd[q_], in_=rstd[q_], func=AF.Exp, scale=-0.5, reads=[("rstd", q_)], writes=[("rstd", q_)])

            def d_y(c):
                q_ = c % 4
                for e in range(2):
                    dve("scalar_tensor_tensor", out=ymlS[q_][:, e * 64:(e + 1) * 64], in0=hh[q_][:, e * 64:(e + 1) * 64],
                        scalar=rstd[q_][:, e:e + 1], in1=sig[c % 6][:, e * 64:(e + 1) * 64], op0=ALU.mult, op1=ALU.mult,
                        reads=[("hh", q_), ("rstd", q_), ("sig", c % 6)], writes=[("ymlS", q_)])

            def p_front(c):
                s_ = c % 2
                CB = CBs(c)
                pvo, vok = pp[s_], PK[s_]
                for kc in range(KC):
                    pe("matmul", pvo[:, 0:256], lhsT=xnT[:, kc, CB], rhs=wmv[:, kc, 256:512], start=(kc == 0), stop=(kc == KC - 1),
                       reads=[wmk, ("xnT", kc, c // 4)], writes=[vok])
                pkt = pp[2 + s_][:, :].bitcast(BF16)
                pe("transpose", pkt[:, 0:128], kT[:, CB], ident_b, reads=[KK, "c_identb"], writes=[PK[2 + s_]])
                for e in range(2):
                    ER = slice(e * 64, (e + 1) * 64)
                    pe("matmul", pp[4 + e][:, 0:128], lhsT=kT[ER, CB], rhs=qT[ER, CB], start=True, stop=True,
                       reads=[KK, QK], writes=[PK[4 + e]])
                for e in range(2):
                    pe("matmul", pp[4 + e][:, 128:512], lhsT=ones_b, rhs=xnT[:, e, 0:384], start=True, stop=True,
                       reads=["c_onesb"], writes=[PK[4 + e]])

            def a_t1(c):
                q_ = c % 4
                for e in range(2):
                    act("activation", out=t1[q_][:, e, :], in_=pO1[:, e, :], func=AF.Copy, scale=stat[1][:, c, r0 + e:r0 + e + 1],
                        reads=[PK[6], ("stat", 1)], writes=[("t1", q_)])

            def a_front(c):
                s_, q_ = c % 2, c % 6
                pvo, vok = pp[s_], PK[s_]
                act("activation", out=eo[s_], in_=pvo[:, 128:256], func=AF.Exp, scale=-1.0, reads=[vok], writes=[("eo", s_)])
                act("activation", out=eo[s_], in_=eo[s_], func=AF.Ln, bias=onec, scale=1.0, reads=[("eo", s_), "c_one"], writes=[("eo", s_)])
                act("activation", out=eo[s_], in_=eo[s_], func=AF.Exp, scale=-1.0, reads=[("eo", s_)], writes=[("eo", s_)])
                pool("tensor_tensor", out=sig[q_], in0=eo[s_], in1=mlnw[:, l * 384 + hp * 128:l * 384 + (hp + 1) * 128], op=ALU.mult,
                     reads=[("eo", s_), "c_mlnw"], writes=[("sig", q_)])

            def d_front(c):
                s_, s3 = c % 2, c % 3
                pvo, vok = pp[s_], PK[s_]
                dve("tensor_tensor", out=Vw[s3][:, :, 0:64], in0=pvo[:, 0:128].rearrange("p (e d) -> p e d", d=64),
                    in1=bc(stat[0][:, c, r0:r0 + 2], 64), op=ALU.mult, reads=[vok, ("stat", 0)], writes=[("Vw", s3)])
                act("copy", out=Vw[s3][:, :, 64], in_=stat[0][:, c, r0:r0 + 2], reads=[("stat", 0)], writes=[("Vw", s3)])
                for e in range(2):
                    dve("tensor_tensor", out=scm[s3][:, e, :], in0=pp[4 + e][:, 0:128], in1=mask01T, op=ALU.mult,
                        reads=[PK[4 + e], "c_mask01T"], writes=[("scm", s3)])

            def a_ktok(c):
                s_ = c % 2
                pkt = pp[2 + s_][:, :].bitcast(BF16)
                act("copy", out=Ktok[s_], in_=pkt[:, 0:128], reads=[PK[2 + s_]], writes=[("Ktok", s_)])

            def d_cn(c):
                s_ = c % 2
                pU = pp[2 + s_][:, 256:386].rearrange("p (e d) -> p e d", d=65)
                for e in range(2):
                    ER = slice(e * 64, (e + 1) * 64)
                    dve("scalar_tensor_tensor", out=Cn[ER, :], in0=Cn[ER, :], scalar=decay_bc[ER, c, 2 * hp + e:2 * hp + e + 1], in1=pU[ER, e, :],
                        op0=ALU.mult, op1=ALU.add, reads=["Cn", "decay_bc", PK[2 + s_]], writes=["Cn"])

            def d_t2(c):
                q_ = c % 4
                for e in range(2):
                    dve("scalar_tensor_tensor", out=t2[q_][:, e, :], in0=pO2[e], scalar=stat[2][:, c, r0 + e:r0 + e + 1], in1=t1[q_][:, e, :],
                        op0=ALU.mult, op1=ALU.add, reads=[pO2k[e], ("stat", 2), ("t1", q_)], writes=[("t2", q_)])

            def a_cnb(c):
                act("copy", out=Cnb3[c % 3][:, 0:65], in_=Cn, reads=["Cn"], writes=[("Cnb", c % 3)])

            def a_square(c):
                q_ = c % 4
                act("activation", out=hsq[q_], in_=hh[q_], func=AF.Square, reads=[("hh", q_)], writes=[("hsq", q_)])

            def p_ytr(c):
                q_ = c % 4
                pe("transpose", p7t, ymlS[q_], ident_b, reads=[("ymlS", q_), "c_identb"], writes=[PK[7]])

            for st in range(nch_ + 7):
                if ok(st - 3): a_abs(st - 3)
                if ok(st - 4): d_reduce(st - 4)
                if ok(st - 1): p_U(st - 1)
                if ok(st - 2): p_out(st - 2)
                if ok(st - 6): a_ycopy(st - 6)
                if ok(st - 3): d_norm(st - 3)
                if ok(st - 4): a_rstd(st - 4)
                if ok(st - 5): d_y(st - 5)
                if ok(st): p_front(st)
                if ok(st - 1): d_cn(st - 1)
                if ok(st - 2): a_t1(st - 2)
                if ok(st): a_front(st)
                if ok(st): d_front(st)
                if ok(st): a_ktok(st)
                if ok(st - 2): d_t2(st - 2)
                if ok(st - 1): a_cnb(st - 1)
                if ok(st - 3): a_square(st - 3)
                if bgc is not None:
                    if next(bgc, "end") == "end":
                        bgc = None
                if ok(st - 5): p_ytr(st - 5)
            if bgc is not None:
                for _ in bgc:
                    pass
        if (stop or "").startswith("ml"):
            dump("qT", qT, [QK])
            dump("kT", kT, [KK])
            dump("yT", yT[:, 5:8, :], [("yT", 5 + k_, j_) for k_ in range(3) for j_ in range(NTT)])
            break
        P.barrier()
        AR.release(m_m)
        m_f = AR.mark()
        QE = [AR.bf16(T) for _ in range(2)]
        KE = [AR.bf16(T) for _ in range(2)]
        Vf = AR.bf16(NTB * 6 * 128).rearrange("p (b h d) -> p b h d", h=6, d=128)
        PT = [AR.bf16(512) for _ in range(4)]
        Rr = AR.f32(512)
        pool("memset", Vf[:, :, :, 64:128], 1.0, writes=["Vf_ones"])
        for i in range(2):
            pool("memset", KE[i][64:128, :], 0.0, writes=[("KEpad", i)])
            pool("memset", QE[i][64:128, :], 0.0, writes=[("QEpad", i)])
            pool("memset", KE[i][64:65, :], 1.0, writes=[("KEpad", i)])
        wvv, wvk = wload([w_in_d[l][:, 768:1152]], KC)
        for tb in range(NTB):
            ps, pk = pp[2 + tb % 2], PK[2 + tb % 2]
            for kc in range(KC):
                pe("matmul", ps[:, 0:384], lhsT=xnT[:, kc, tb * 128:(tb + 1) * 128], rhs=wvv[:, kc, :], start=(kc == 0), stop=(kc == KC - 1),
                   reads=[wvk, ("xnT", kc, tb // 4)], writes=[pk])
            o_, i_ = Vf[:, tb, :, 0:64], ps[:, 0:384].rearrange("p (h d) -> p h d", d=64)
            if tb % 2 == 0:
                dve("tensor_copy", o_, i_, reads=[pk], writes=["Vf"])
            else:
                act("copy", out=o_, in_=i_, reads=[pk], writes=["Vf"])
        wqv, wqk = wload([w_in_d[l][:, 0:384]], KC)
        wkv, wkk = wload([w_in_d[l][:, 384:768]], KC)

        def fox_proj(hd):
            b = hd % 2
            for tt in range(NTT):
                sl = slice(tt * 512, (tt + 1) * 512)
                ps, pk = pp[6], PK[6]
                proj_fm(wqv, wqk, hd * 64, 64, xnT, "xnT", KC, tt, ps, pk)
                dve("tensor_scalar", out=QE[b][0:64, sl], in0=ps[0:64, :], scalar1=0.125, scalar2=None, op0=ALU.mult,
                    reads=[pk], writes=[("QE", b)])
                yield
                ps, pk = pp[7], PK[7]
                proj_fm(wkv, wkk, hd * 64, 64, xnT, "xnT", KC, tt, ps, pk)
                dve("tensor_copy", KE[b][0:64, sl], ps[0:64, :], reads=[pk], writes=[("KE", b)])
                yield
            P.dma("sp", QE[b][64:65, :], Cb[hd:hd + 1, :], "qerow%d" % b, reads=["Cb"], writes=[("QE", b), ("QEpad", b)])

        ftiles = [(hd, j, kb) for hd in range(6) for j in range(NTT) for kb in range(4 * j + 4)]

        def f_geom(i):
            hd, j, kb = ftiles[i]
            ii = kb - 4 * j
            return hd, j, kb, ii, max(0, ii) * 128

        def f_S(i):
            hd, j, kb, ii, lo = f_geom(i)
            b = hd % 2
            pss, sk = pp[i % 4], PK[i % 4]
            pe("matmul", pss[:, lo:512], lhsT=KE[b][:, kb * 128:(kb + 1) * 128], rhs=QE[b][:, j * 512 + lo:(j + 1) * 512],
               start=True, stop=(ii < 0), reads=[("QE", b), ("KE", b), ("QEpad", b), ("KEpad", b)], writes=[sk])
            if ii >= 0:
                pe("matmul", pss[:, lo:lo + 128], lhsT=ident_b, rhs=maskneg, start=False, stop=True,
                   reads=["c_identb", "c_maskneg"], writes=[sk])

        def f_EXP(i):
            hd, j, kb, ii, lo = f_geom(i)
            pss, sk = pp[i % 4], PK[i % 4]
            act("activation", out=PT[i % 4][:, lo:512], in_=pss[:, lo:512], func=AF.Exp, bias=negcT[:, kb, hd:hd + 1], scale=1.0,
                reads=[sk, "negcT"], writes=[("PT", i % 4)])

        def f_AV(i):
            hd, j, kb, ii, lo = f_geom(i)
            grp = hd * NTT + j
            ybank = 4 + grp % 2
            psyT, yk = pp[ybank], PK[ybank]
            pe("matmul", psyT[:, lo:512], lhsT=Vf[:, kb, hd, :], rhs=PT[i % 4][:, lo:512],
               start=(kb == 0), stop=(kb == 4 * j + 3), reads=[("PT", i % 4), "Vf", "Vf_ones"], writes=[yk])
            if kb == 4 * j + 3:
                dve("reciprocal", Rr[64:128, :], psyT[64:128, :], reads=[yk], writes=["Rr"])
                dve("tensor_copy", Rr[0:64, :], Rr[64:128, :], reads=["Rr"], writes=["Rr"])
                dst = (hd % 2) * 64
                dve("tensor_tensor", out=yT[dst:dst + 64, hd // 2, j * 512:(j + 1) * 512], in0=psyT[0:64, :], in1=Rr[0:64, :], op=ALU.mult,
                    reads=[yk, "Rr"], writes=[("yT", hd // 2, j)])

        for _ in fox_proj(0):
            pass
        nft = len(ftiles)
        sb_w = {}
        LA = 3
        bg = None
        for i0 in range(min(LA, nft)):
            f_S(i0)
        for i in range(nft):
            hd, j, kb = ftiles[i]
            if j == 0 and kb == 0:
                bg = fox_proj(hd + 1) if hd + 1 < 6 else None
                if hd == 5:
                    sb_w["v"] = wload([w_in_d[l][:, 1670:1926]], KC)
                    sb_w["qk"] = wload([w_in_d[l][:, 1158:1670]], KC)
            if i + LA < nft:
                if ftiles[i + LA][0] != hd and bg is not None:
                    for _ in bg:
                        pass
                    bg = None
                f_S(i + LA)
            f_EXP(i)
            f_AV(i)
            if bg is not None and (i % 4 == 3):
                if next(bg, "end") == "end":
                    bg = None
        if stop == "fox":
            dump("yT", yT[:, 0:3, :], [("yT", k_, j_) for k_ in range(3) for j_ in range(NTT)])
            break
        P.barrier()
        AR.release(m_cb)
        m_s = AR.mark()
        qs = [AR.bf16(T) for _ in range(2)]
        ks = [AR.bf16(T) for _ in range(2)]
        Vs = AR.bf16(NTB * 256).rearrange("p (b n) -> p b n", n=256)
        Et = [AR.f32(512) for _ in range(4)]
        SPt = [AR.bf16(512) for _ in range(2)]
        Wt = [AR.f32(512) for _ in range(2)]
        At = [AR.bf16(512) for _ in range(2)]
        acc2 = [AR.bf16(512) for _ in range(2)]
        wsv, wsk = sb_w["v"]
        for tb in range(NTB):
            ps, pk = pp[2 + tb % 2], PK[2 + tb % 2]
            for kc in range(KC):
                pe("matmul", ps[:, 0:256], lhsT=xnT[:, kc, tb * 128:(tb + 1) * 128], rhs=wsv[:, kc, :], start=(kc == 0), stop=(kc == KC - 1),
                   reads=[wsk, ("xnT", kc, tb // 4)], writes=[pk])
            if tb % 2 == 0:
                dve("tensor_copy", Vs[:, tb, :], ps[:, 0:256], reads=[pk], writes=["Vs"])
            else:
                act("copy", out=Vs[:, tb, :], in_=ps[:, 0:256], reads=[pk], writes=["Vs"])
        wqkv, wqkk = sb_w["qk"]
        for hp in range(2):
            for tt in range(NTT):
                sl = slice(tt * 512, (tt + 1) * 512)
                ps, pk = pp[4 + tt % 2], PK[4 + tt % 2]
                proj_fm(wqkv, wqkk, hp * 128, 128, xnT, "xnT", KC, tt, ps, pk)
                dve("tensor_scalar", out=qs[hp][:, sl], in0=ps[:, :], scalar1=0.125, scalar2=None, op0=ALU.mult, reads=[pk], writes=[("qs", hp)])
                ps, pk = pp[6 + tt % 2], PK[6 + tt % 2]
                proj_fm(wqkv, wqkk, 256 + hp * 128, 128, xnT, "xnT", KC, tt, ps, pk)
                act("copy", out=ks[hp][:, sl], in_=ps[:, :], reads=[pk], writes=[("ks", hp)])
        wout_grp = []
        for g in range(2):
            wov, wok = wload([w_out_d[l][:, g * 512:(g + 1) * 512]], KC)
            wout_grp.append((wov, wok, [(ci * 128, g * 4 + ci) for ci in range(4)]))
        stiles = [(hd, j, kb) for hd in range(4) for j in range(NTT) for kb in range(4 * j + 3, -1, -1)]

        def s_geom(i):
            hd, j, kb = stiles[i]
            ii = kb - 4 * j
            lo = max(0, ii) * 128
            return hd, j, kb, ii, lo, slice(lo, 512), hd // 2, slice((hd % 2) * 64, (hd % 2) * 64 + 64)

        def s_Z(i):
            hd, j, kb, ii, lo, CS, hp, PR = s_geom(i)
            zb = (0, 1, 6)[i % 3]
            pe("matmul", pp[zb][:, CS], lhsT=ks[hp][PR, kb * 128:(kb + 1) * 128], rhs=qs[hp][PR, j * 512 + lo:(j + 1) * 512],
               start=True, stop=(ii < 0), reads=[("qs", hp), ("ks", hp)], writes=[PK[zb]])
            if ii >= 0:
                pe("matmul", pp[zb][:, lo:lo + 128], lhsT=ident_b, rhs=masknegS, start=False, stop=True,
                   reads=["c_identb", "c_masknegS"], writes=[PK[zb]])

        def s_E(i):
            hd, j, kb, ii, lo, CS, hp, PR = s_geom(i)
            zb = (0, 1, 6)[i % 3]
            act("activation", out=Et[i % 4][:, CS], in_=pp[zb][:, CS], func=AF.Exp, reads=[PK[zb]], writes=[("E", i % 4)])

        def s_SP(i):
            hd, j, kb, ii, lo, CS, hp, PR = s_geom(i)
            s_ = i % 2
            t_in_grp = (4 * j + 3) - kb
            acc_o, ako = acc2[t_in_grp % 2], ("acc", t_in_grp % 2)
            acc_n, akn = acc2[(t_in_grp + 1) % 2], ("acc", (t_in_grp + 1) % 2)
            if kb == 4 * j + 3:
                pool("memset", acc2[0], 0.0, writes=[("acc", 0)])
                pool("memset", acc2[1], 0.0, writes=[("acc", 1)])
            act("activation", out=SPt[s_][:, CS], in_=Et[i % 4][:, CS], func=AF.Ln, bias=onec, scale=1.0,
                reads=[("E", i % 4), "c_one"], writes=[("SP", s_)])
            pc, ck = pp[4 + s_], PK[4 + s_]
            lastblk = (kb == 4 * j + 3)
            pe("matmul", pc[:, CS], lhsT=tincl, rhs=SPt[s_][:, CS], start=True, stop=lastblk, reads=[("SP", s_), "c_tincl"], writes=[ck])
            if not lastblk:
                pe("matmul", pc[:, CS], lhsT=ones_b, rhs=acc_o[:, CS], start=False, stop=True, reads=[ako, "c_onesb"], writes=[ck])
            if kb > 0:
                pool("tensor_tensor", out=acc_n[:, CS], in0=acc_o[:, CS], in1=SPt[s_][:, CS], op=ALU.add, reads=[ako, ("SP", s_)], writes=[akn])
            for _w in range(3):
                pe("matmul", pp[7][:, :], lhsT=ones_b, rhs=xnT[:, _w, 0:512], start=True, stop=True, reads=["c_onesb"], writes=["ps7_dummy"])

        def s_W(i):
            hd, j, kb, ii, lo, CS, hp, PR = s_geom(i)
            s_ = i % 2
            grp = hd * NTT + j
            ybank = 2 + grp % 2
            psyT, yk = pp[ybank], PK[ybank]
            act("activation", out=Wt[s_][:, CS], in_=pp[4 + s_][:, CS], func=AF.Exp, scale=-1.0, reads=[PK[4 + s_]], writes=[("W", s_)])
            dve("tensor_tensor", out=At[s_][:, CS], in0=Et[i % 4][:, CS], in1=Wt[s_][:, CS], op=ALU.mult,
                reads=[("E", i % 4), ("W", s_)], writes=[("A", s_)])
            pe("matmul", psyT[0:64, CS], lhsT=Vs[:, kb, hd * 64:(hd + 1) * 64], rhs=At[s_][:, CS],
               start=(kb == 4 * j + 3), stop=(kb == 0), skip_group_check=True, reads=[("A", s_), "Vs"], writes=[yk])
            if kb == 0:
                dst = (hd % 2) * 64
                dve("tensor_copy", yT[dst:dst + 64, 3 + hd // 2, j * 512:(j + 1) * 512], psyT[0:64, :], reads=[yk], writes=[("yT", 3 + hd // 2, j)])

        nst = len(stiles)
        s_Z(0)
        if nst > 1:
            s_Z(1)
        s_E(0)
        for k in range(nst + 1):
            if k + 2 < nst:
                s_Z(k + 2)
            if k + 1 < nst:
                s_E(k + 1)
            if k < nst:
                s_SP(k)
            if k >= 1:
                s_W(k - 1)
        if stop == "sb":
            dump("yT", yT[:, 3:5, :], [("yT", 3 + k_, j_) for k_ in range(2) for j_ in range(NTT)])
            break
        P.barrier()
        AR.release(m_s)
        AR.release(m_layer)
        AR.bf16(KC * T)
        m_o = AR.mark()
        resid_add_ttmajor(wout_grp, yT, "yT", KC)
        if stop == "mixer":
            dump("yTall", yT[:, :, :], [("yT", k_, j_) for k_ in range(8) for j_ in range(NTT)])
            break
        P.barrier()
        AR.release(m_layer)

        m_x = AR.mark()
        memT = AR.f32(KC * MEM).rearrange("p (k m) -> p k m", m=MEM)
        mnT = AR.bf16(KC * MEM).rearrange("p (k m) -> p k m", m=MEM)
        KxT = AR.bf16(4 * MEM).rearrange("p (h m) -> p h m", m=MEM)
        Vx = AR.bf16(2 * 4 * 129).rearrange("p (b h d) -> p b h d", h=4, d=129)
        QxT = AR.bf16(4 * T).rearrange("p (h t) -> p h t", t=T)
        ox = AR.bf16(NTB * 512).rearrange("p (b n) -> p b n", n=512)
        oxT = AR.bf16(4 * T).rearrange("p (h t) -> p h t", t=T)
        PTx = [AR.bf16(512) for _ in range(3)]
        recx = AR.f32(4)
        rmsnorm(hT, "hT", T, 1 * DEPTH + l, xnT, "xnT")
        P.dma("sp", memT, memT_d.rearrange("(k p) m -> p k m", p=128), "memload", writes=[("memT", kc, 0) for kc in range(KC)])
        rmsnorm(memT, "memT", MEM, 2 * DEPTH + l, mnT, "mnT", tw=MEM)
        pool("memset", Vx[:, :, :, 128:129], 1.0, writes=["Vx"])
        wkxv, wkxk = wload([wx_kv_d[l][:, 0:512]], KC)
        for xh in range(4):
            ps, pk = pp[2 + xh % 2], PK[2 + xh % 2]
            for kc in range(KC):
                pe("matmul", ps[:, 0:MEM], lhsT=wkxv[:, kc, xh * 128:(xh + 1) * 128], rhs=mnT[:, kc, :], start=(kc == 0), stop=(kc == KC - 1),
                   reads=[wkxk, ("mnT", kc, 0)], writes=[pk])
            dve("tensor_copy", KxT[:, xh, :], ps[:, 0:MEM], reads=[pk], writes=["KxT"])
        wvxv, wvxk = wload([wx_kv_d[l][:, 512:1024]], KC)
        for mb in range(2):
            ps, pk = pp[4 + mb], PK[4 + mb]
            for kc in range(KC):
                pe("matmul", ps[:, :], lhsT=mnT[:, kc, mb * 128:(mb + 1) * 128], rhs=wvxv[:, kc, :], start=(kc == 0), stop=(kc == KC - 1),
                   reads=[wvxk, ("mnT", kc, 0)], writes=[pk])
            dve("tensor_copy", Vx[:, mb, :, 0:128], ps[:, :].rearrange("p (h d) -> p h d", d=128), reads=[pk], writes=["Vx"])
        wqxv, wqxk = wload([wx_q_d[l]], KC)
        for xh in range(4):
            for tt in range(NTT):
                ps, pk = pp[4 + tt % 2], PK[4 + tt % 2]
                proj_fm(wqxv, wqxk, xh * 128, 128, xnT, "xnT", KC, tt, ps, pk)
                if tt % 2 == 0:
                    dve("tensor_copy", QxT[:, xh, tt * 512:(tt + 1) * 512], ps[:, :], reads=[pk], writes=[("QxT", xh, tt)])
                else:
                    act("copy", out=QxT[:, xh, tt * 512:(tt + 1) * 512], in_=ps[:, :], reads=[pk], writes=[("QxT", xh, tt)])
        XS = float(128 ** -0.5)
        xtiles = [(xh, tt, mb) for xh in range(4) for tt in range(NTT) for mb in range(2)]
        XB = (0, 1, 6, 7)

        def x_S(i):
            xh, tt, mb = xtiles[i]
            bnk = XB[i % 4]
            pe("matmul", pp[bnk][:, :], lhsT=KxT[:, xh, mb * 128:(mb + 1) * 128], rhs=QxT[:, xh, tt * 512:(tt + 1) * 512], start=True, stop=True,
               reads=["KxT", ("QxT", xh, tt)], writes=[PK[bnk]])

        def x_rest(i):
            xh, tt, mb = xtiles[i]
            bnk = XB[i % 4]
            s_ = i % 3
            pOa = pp[2][:, 0:258].rearrange("p (q d) -> p q d", d=129)
            pOb = pp[3][:, 0:258].rearrange("p (q d) -> p q d", d=129)
            act("activation", out=PTx[s_], in_=pp[bnk][:, :], func=AF.Exp, scale=XS, reads=[PK[bnk]], writes=[("PTx", s_)])
            for qi in range(4):
                po, pok = (pOa, PK[2]) if qi < 2 else (pOb, PK[3])
                pe("matmul", po[:, qi % 2, :], lhsT=PTx[s_][:, qi * 128:(qi + 1) * 128], rhs=Vx[:, mb, xh, :],
                   start=(mb == 0 and qi % 2 == 0), stop=(mb == 1), skip_group_check=True, reads=[("PTx", s_), "Vx"], writes=[pok])
            if mb == 1:
                for half, (po, pok) in enumerate(((pOa, PK[2]), (pOb, PK[3]))):
                    dve("reciprocal", recx[:, 2 * half:2 * half + 2], po[:, :, 128], reads=[pok], writes=["recx"])
                    dve("tensor_tensor", out=ox[:, 4 * tt + 2 * half:4 * tt + 2 * half + 2, xh * 128:(xh + 1) * 128], in0=po[:, :, 0:128],
                        in1=recx[:, 2 * half:2 * half + 2].unsqueeze(2).broadcast_to([128, 2, 128]), op=ALU.mult,
                        reads=[pok, "recx"], writes=[("ox", tt)])

        nxt = len(xtiles)
        x_S(0)
        x_S(1)
        for i in range(nxt):
            if i + 2 < nxt:
                x_S(i + 2)
            x_rest(i)
        to_fm(ox, lambda tb: [("ox", tb // 4)], 4, oxT, "oxT")
        grp_ = []
        for g in range(2):
            wxov, wxok = wload([wx_o_d[l][:, g * 512:(g + 1) * 512]], 4)
            grp_.append((wxov, wxok, [(ci * 128, g * 4 + ci) for ci in range(4)]))
        resid_add_ttmajor(grp_, oxT, "oxT", 4)
        if stop == "cross":
            break
        P.barrier()
        AR.release(m_x)

        m_ff = AR.mark()
        NFH = 11
        aT = AR.bf16(NFH * T).rearrange("p (f t) -> p f t", t=T)
        sg = [AR.f32(512) for _ in range(2)]
        rmsnorm(hT, "hT", T, 3 * DEPTH + l, xnT, "xnT")
        cnt_s = 0
        for half in range(2):
            fbase = half * NFH * 128
            for g in range(6):
                nfc = 2 if g < 5 else 1
                c0 = fbase + g * 256
                wfv, wfk = wload([w_gate_d[l][:, c0:c0 + nfc * 128], w_up_d[l][:, c0:c0 + nfc * 128]], KC)
                for f_ in range(nfc):
                    fc = g * 2 + f_
                    for tt in range(NTT):
                        s_ = cnt_s % 2
                        cnt_s += 1
                        pg, pgk = pp[s_], PK[s_]
                        pu, puk = pp[2 + s_], PK[2 + s_]
                        proj_fm(wfv, wfk, f_ * 128, 128, xnT, "xnT", KC, tt, pg, pgk)
                        proj_fm(wfv, wfk, nfc * 128 + f_ * 128, 128, xnT, "xnT", KC, tt, pu, puk)
                        act("activation", out=sg[s_], in_=pg[:, :], func=AF.Silu, reads=[pgk], writes=[("sg", s_)])
                        dve("tensor_tensor", out=aT[:, fc, tt * 512:(tt + 1) * 512], in0=pu[:, :], in1=sg[s_], op=ALU.mult,
                            reads=[puk, ("sg", s_)], writes=[("aT", fc, tt)])
            for g in range(4):
                wdv, wdk = wload([w_down_d[l][fbase:fbase + NFH * 128, g * 256:(g + 1) * 256]], NFH)
                for ci in range(2):
                    resid_add(wdv, wdk, ci * 128, g * 2 + ci, aT, "aT", NFH)
        P.barrier()
        AR.release(m_ff)
        AR.release(m_layer)

    if stop is None:
        rmsnorm(hT, "hT", T, 8, hT, "hT")
    for (name, d, ap, keys) in dbg:
        P.dma("sp", d, ap, "dbg_" + name, reads=keys, is_out=True)
    P.barrier()
    for kc in range(KC):
        P.dma("sp", outT_d[kc * 128:(kc + 1) * 128, :], hT[:, kc, :], "ostore", reads=[("hT", kc, tt) for tt in range(NTT)], is_out=True)
    P.finish()
    P.emit(es)
    print('arena high-water', AR.hw, 'of', ARW)
    return nc, es


def prep_inputs(inp, b):
    f = lambda a: np.ascontiguousarray(np.asarray(a, dtype=np.float32))
    w_in = np.asarray(inp["w_in"], dtype=np.float32)
    FF, MI, MF = 1152, 3462, 3468
    w_gates = np.concatenate([w_in[:, :, FF:FF + 6], w_in[:, :, MF:MF + 6], w_in[:, :, FF:FF + 6], w_in[:, :, MI:MI + 6]], axis=2)
    nw = np.stack([inp["norm_mix_w"][0], inp["norm_mix_w"][1], inp["norm_x_w"][0], inp["norm_x_w"][1],
                   inp["mem_norm_w"][0], inp["mem_norm_w"][1], inp["norm_ffn_w"][0], inp["norm_ffn_w"][1],
                   inp["final_norm_w"]], axis=0)
    normw = np.asarray(nw, np.float32).reshape(9, KC, 128).transpose(2, 0, 1).reshape(128, 9 * KC)
    gb = np.zeros((12, DEPTH * 2), np.float32)
    for l in range(DEPTH):
        gb[0:6, 2 * l] = inp["fox_f_b"][l]
        gb[6:12, 2 * l] = inp["ml_f_b"][l]
        gb[0:6, 2 * l + 1] = inp["fox_f_b"][l]
        gb[6:12, 2 * l + 1] = inp["ml_i_b"][l]
    cw = np.asarray(inp["ml_conv_w"], np.float32)
    convw = cw.reshape(DEPTH, 4, 6, 128).transpose(3, 0, 2, 1).reshape(128, DEPTH * 24)
    mlnw = np.broadcast_to(np.asarray(inp["ml_norm_w"], np.float32).reshape(1, DEPTH * 384), (128, DEPTH * 384))
    return {
        "xT": f(np.asarray(inp["x"][b]).T), "memT": f(np.asarray(inp["mem"][b]).T),
        "normw": f(normw), "w_in": f(w_in), "w_gates": f(w_gates), "gate_b": f(gb), "convw": f(convw),
        "mlnw": f(mlnw), "w_out": f(inp["w_out"]), "wx_q": f(inp["wx_q"]), "wx_kv": f(inp["wx_kv"]),
        "wx_o": f(inp["wx_o"]), "w_gate": f(inp["w_gate"]), "w_up": f(inp["w_up"]), "w_down": f(inp["w_down"]),
    }


_CACHE = {}


def kernel(**inputs):
    if "nc" not in _CACHE:
        _CACHE["nc"] = build()
    nc, _es = _CACHE["nc"]
    shared = None
    in_maps = []
    for b in range(8):
        m = prep_inputs(inputs, b)
        if shared is None:
            shared = m
        else:
            for k in m:
                if k not in ("xT", "memT"):
                    m[k] = shared[k]
        in_maps.append(m)
    res = run_bass_kernel_spmd(nc, in_maps, core_ids=list(range(8)))
    out = np.stack([np.ascontiguousarray(r["outT"].T) for r in res.results], axis=0)
    return out.astype(np.float32)
```

```python
import numpy as np
from contextlib import ExitStack
import concourse.bass as bass
import concourse.mybir as mybir
from concourse.bass_utils import run_bass_kernel_spmd

F32 = mybir.dt.float32
BF16 = mybir.dt.bfloat16
AF = mybir.ActivationFunctionType
ALU = mybir.AluOpType
AX = mybir.AxisListType

ENGS = ("pe", "act", "dve", "pool", "sp")

D = 1024
T = 2048
KC = 8
NTB = 16
NTT = 4
DEPTH = 2
MEM = 256
DFF = 2816
NIN = 3474
EPS = 1e-6


class Prog:
    def __init__(self, nc):
        self.nc = nc
        self.ins = {e: [] for e in ENGS}
        self.known = {e: {} for e in ENGS}
        self.lastw = {}
        self.readers = {}
        self.dma_cnt = {}
        self.snap = {}
        self.out_tokens = []

    def _deps(self, eng, reads, writes):
        deps = {}

        def add(tok):
            if tok is None:
                return
            sk, v = tok
            if sk == "pe" and eng == "pe":
                return
            if deps.get(sk, -1) < v:
                deps[sk] = v
        for k in reads:
            add(self.lastw.get(k))
        for k in writes:
            add(self.lastw.get(k))
            for t in self.readers.get(k, ()):
                add(t)
        kn = self.known[eng]
        waits = []
        for sk, v in deps.items():
            if kn.get(sk, -1) >= v:
                continue
            waits.append((sk, v))
        for sk, v in waits:
            sn = self.snap.get((sk, v))
            if sn:
                for a, b in sn.items():
                    if kn.get(a, -1) < b:
                        kn[a] = b
            if kn.get(sk, -1) < v:
                kn[sk] = v
        return waits

    def _commit(self, tok, reads, writes):
        for k in reads:
            self.readers.setdefault(k, []).append(tok)
        for k in writes:
            self.lastw[k] = tok
            self.readers[k] = []

    def op(self, eng, meth, args, kw, reads=(), writes=()):
        waits = self._deps(eng, reads, writes)
        tok = (eng, len(self.ins[eng]))
        self.ins[eng].append(dict(fn=(meth, args, kw), waits=waits, tok=tok, dma=None))
        self.snap[tok] = dict(self.known[eng])
        self._commit(tok, reads, writes)
        return tok

    def dma(self, eng, out, in_, semname, reads=(), writes=(), is_out=False):
        fn = ("dma_start", (), dict(out=out, in_=in_))
        waits = self._deps(eng, reads, writes)
        sk = ("dma", semname)
        n = self.dma_cnt.get(sk, 0)
        self.dma_cnt[sk] = n + 1
        tok = (sk, n)
        self.ins[eng].append(dict(fn=fn, waits=waits, tok=tok, dma=sk))
        self.snap[tok] = dict(self.known[eng])
        self._commit(tok, reads, writes)
        if is_out:
            self.out_tokens.append(tok)
        return tok

    def barrier(self):
        last = []
        for e in ENGS:
            for rec in reversed(self.ins[e]):
                if rec["tok"] is not None and rec["dma"] is None:
                    last.append(rec["tok"])
                    break
        for sk, n in self.dma_cnt.items():
            last.append((sk, n - 1))
        for e in ENGS:
            kn = self.known[e]
            waits = []
            for sk, v in last:
                if sk == e:
                    continue
                if kn.get(sk, -1) >= v:
                    continue
                waits.append((sk, v))
                kn[sk] = v
            if waits:
                self.ins[e].append(dict(fn=None, waits=waits, tok=None, dma=None))
        self.lastw = {}
        self.readers = {}

    def finish(self, eng="sp"):
        best = {}
        for sk, v in self.out_tokens:
            if best.get(sk, -1) < v:
                best[sk] = v
        self.ins[eng].append(dict(fn=None, waits=list(best.items()), tok=None, dma=None))

    def emit(self, es):
        nc = self.nc
        needed = set()
        for e in ENGS:
            for rec in self.ins[e]:
                for w in rec["waits"]:
                    needed.add(w)
        rank = {}
        for e in ENGS:
            c = 0
            for rec in self.ins[e]:
                if rec["dma"] is None and rec["tok"] is not None and rec["tok"] in needed:
                    c += 1
                    rank[rec["tok"]] = c
        sems = {}
        for e in ENGS:
            sems[e] = es.enter_context(nc.semaphore("sem_" + e))
        for sk in self.dma_cnt:
            sems[sk] = es.enter_context(nc.semaphore("semd_" + str(sk[1])))

        def val(tok):
            sk, v = tok
            if isinstance(sk, tuple):
                return 16 * (v + 1)
            return rank[tok]

        def run(e, h):
            for rec in self.ins[e]:
                for w in rec["waits"]:
                    h.wait_ge(sems[w[0]], val(w))
                if rec["fn"] is None:
                    continue
                meth, a, kw = rec["fn"]
                ins = getattr(h, meth)(*a, **kw)
                if rec["dma"] is not None:
                    ins.then_inc(sems[rec["dma"]], 16)
                elif rec["tok"] in needed:
                    ins.then_inc(sems[e], 1)

        block = es.enter_context(nc.Block())

        @block.tensor
        def _(h):
            run("pe", h)

        @block.scalar
        def _(h):
            run("act", h)

        @block.vector
        def _(h):
            run("dve", h)

        @block.gpsimd
        def _(h):
            run("pool", h)

        @block.sync
        def _(h):
            run("sp", h)


class Arena:
    def __init__(self, ap, nwords):
        self.ap = ap
        self.n = nwords
        self.top = 0

    def f32(self, cols):
        a = self.ap[:, self.top:self.top + cols]
        self.top += cols
        self.hw = max(getattr(self, "hw", 0), self.top)
        assert self.top <= self.n, ("arena overflow", self.top, self.n)
        return a

    def bf16(self, cols):
        w = (cols + 1) // 2
        a = self.ap[:, self.top:self.top + w].bitcast(BF16)
        self.top += w
        self.hw = max(getattr(self, "hw", 0), self.top)
        assert self.top <= self.n, ("arena overflow", self.top, self.n)
        return a[:, 0:cols]

    def mark(self):
        return self.top

    def release(self, m):
        self.top = m


def build(nlayers=DEPTH, stop=None):
    nc = bass.Bass("TRN2", target_bir_lowering=False, dynamic_dma_scratch_size=4096)

    def din(name, shape):
        return nc.dram_tensor(name, list(shape), F32, kind="ExternalInput").ap()

    xT_d = din("xT", [D, T])
    memT_d = din("memT", [D, MEM])
    normw_d = din("normw", [128, 9 * KC])
    w_in_d = din("w_in", [DEPTH, D, NIN])
    wg_d = din("w_gates", [DEPTH, D, 24])
    gb_d = din("gate_b", [12, DEPTH * 2])
    convw_d = din("convw", [128, DEPTH * 6 * 4])
    mlnw_d = din("mlnw", [128, DEPTH * 384])
    w_out_d = din("w_out", [DEPTH, D, D])
    wx_q_d = din("wx_q", [DEPTH, D, 512])
    wx_kv_d = din("wx_kv", [DEPTH, D, 1024])
    wx_o_d = din("wx_o", [DEPTH, 512, D])
    w_gate_d = din("w_gate", [DEPTH, D, DFF])
    w_up_d = din("w_up", [DEPTH, D, DFF])
    w_down_d = din("w_down", [DEPTH, DFF, D])
    outT_d = nc.dram_tensor("outT", [D, T], F32, kind="ExternalOutput").ap()

    es = ExitStack()
    P = Prog(nc)
    dbg = []

    def dump(name, ap, keys):
        d = nc.dram_tensor("dbg_" + name, list(ap.shape), ap.dtype, kind="ExternalOutput").ap()
        dbg.append((name, d, ap, list(keys)))

    sbt = lambda name, shape, dt: es.enter_context(nc.sbuf_tensor(name, shape, dt))
    hT = sbt("hT", [128, KC, T], F32)
    xnT = sbt("xnT", [128, KC, T], BF16)
    NSLOT = 3
    WSLOT = 4096
    wring = sbt("wring", [128, NSLOT, WSLOT], BF16)
    consts = sbt("consts", [128, 1280], F32)
    mlnw = sbt("mlnw_sb", [128, DEPTH * 384], F32)
    ARW = 20200 + 3072
    arena_t = sbt("arena", [128, ARW], F32)
    AR = Arena(arena_t, ARW)
    pp = [es.enter_context(nc.psum_tensor("pp%d" % i, [128, 512], F32)) for i in range(8)]
    PK = ["ps%d" % i for i in range(8)]

    def mk(eng):
        def f(meth, *a, reads=(), writes=(), **kw):
            reads, writes = list(reads), list(writes)
            if eng != "pe":
                for k in reads:
                    if isinstance(k, str) and k.startswith("ps") and k not in writes:
                        writes.append(k)
            if defer["list"] is not None:
                defer["list"].append((eng, meth, a, kw, reads, writes))
                return None
            return P.op(eng, meth, a, kw, reads, writes)
        return f

    defer = {"list": None}

    def flush_deferred(lst, n):
        for _ in range(n):
            if not lst:
                return
            eng_, meth_, a_, kw_, r_, w_ = lst.pop(0)
            P.op(eng_, meth_, a_, kw_, r_, w_)
    pe, act, dve, pool = mk("pe"), mk("act"), mk("dve"), mk("pool")

    CA = Arena(consts, 1280)
    ident_f = CA.f32(128)
    ident_b = CA.bf16(128)
    ones_b = CA.bf16(128)
    maskneg = CA.bf16(128)
    strict01 = CA.f32(128)
    tincl = CA.bf16(128)
    mask01T = CA.f32(128)
    zeros_f = CA.f32(128)
    ones_f = CA.f32(128)
    normw = CA.f32(9 * KC)
    gate_b = CA.f32(DEPTH * 2)
    neg_gb = CA.f32(DEPTH * 2)
    convw = CA.f32(DEPTH * 24)
    epsc = CA.f32(1)
    onec = CA.f32(1)
    msel = CA.f32(6)
    ln8c = CA.f32(1)
    masknegS = CA.bf16(128)

    pool("memset", zeros_f, 0.0, writes=["c_zeros"])
    pool("memset", ones_f, 1.0, writes=["c_ones"])
    pool("memset", ones_b, 1.0, writes=["c_onesb"])
    pool("memset", epsc, EPS, writes=["c_eps"])
    pool("memset", onec, 1.0, writes=["c_one"])
    pool("memset", ln8c, float(np.log(8.0)), writes=["c_ln8"])

    def asel(out, in_, pattern, cm, cmp, fill, base=0, rk=(), wk=()):
        pool("affine_select", out=out, in_=in_, pattern=pattern, compare_op=cmp, fill=fill, base=base,
             channel_multiplier=cm, reads=rk, writes=wk)

    asel(ident_f, ones_f, [[-1, 128]], 1, ALU.is_equal, 0.0, rk=["c_ones"], wk=["c_identf"])
    asel(ident_b, ones_f, [[-1, 128]], 1, ALU.is_equal, 0.0, rk=["c_ones"], wk=["c_identb"])
    asel(maskneg, zeros_f, [[1, 128]], -1, ALU.is_ge, -30000.0, rk=["c_zeros"], wk=["c_maskneg"])
    asel(strict01, ones_f, [[1, 128]], -1, ALU.is_gt, 0.0, rk=["c_ones"], wk=["c_strict"])
    asel(masknegS, zeros_f, [[1, 128]], -1, ALU.is_gt, -30000.0, rk=["c_zeros"], wk=["c_masknegS"])
    asel(tincl, ones_f, [[-1, 128]], 1, ALU.is_ge, 0.0, rk=["c_ones"], wk=["c_tincl"])
    asel(mask01T, ones_f, [[1, 128]], -1, ALU.is_ge, 0.0, rk=["c_ones"], wk=["c_mask01T"])
    asel(msel[0:12, :], ones_f[0:12, 0:6], [[-1, 6]], 1, ALU.is_equal, 0.0, base=-6, rk=["c_ones"], wk=["c_msel"])

    P.dma("sp", normw, normw_d, "cst0", writes=["c_normw"])
    P.dma("sp", gate_b[0:12, :], gb_d, "cst1", writes=["c_gateb"])
    P.dma("sp", convw, convw_d, "cst2", writes=["c_convw"])
    P.dma("sp", mlnw[:], mlnw_d, "cst3", writes=["c_mlnw"])
    dve("tensor_scalar", out=neg_gb[0:12, :], in0=gate_b[0:12, :], scalar1=-1.0, scalar2=None, op0=ALU.mult,
        reads=["c_gateb"], writes=["c_neggb"])

    for kc in range(KC):
        P.dma("sp", hT[:, kc, :], xT_d[kc * 128:(kc + 1) * 128, :], "hload%d" % kc, writes=[("hT", kc, tt) for tt in range(NTT)])

    wstate = dict(n=0)

    def wload(segs, kcn):
        tot = sum(s.shape[1] for s in segs)
        assert kcn * tot <= WSLOT, (kcn, tot)
        s_ = wstate["n"] % NSLOT
        wstate["n"] += 1
        key = ("w", s_)
        view = wring[:, s_, 0:kcn * tot].rearrange("p (k n) -> p k n", n=tot)
        off = 0
        for sg in segs:
            n = sg.shape[1]
            P.dma("pool", view[:, :, off:off + n], sg.rearrange("(k p) n -> p k n", p=128), "w%d" % s_, writes=[key])
            off += n
        return view, key

    def rmsnorm(src, skey, ntok, wcol, dst, dkey, tw=512, scr=None):
        m = AR.mark()
        if scr is None:
            sq = [AR.bf16(tw) for _ in range(2)]
            lnv = AR.f32(tw)
            rstd = AR.f32(tw)
        else:
            sq = [scr[:, 0:tw // 2].bitcast(BF16), scr[:, tw // 2:tw].bitcast(BF16)]
            lnv = scr[:, tw:2 * tw]
            rstd = scr[:, 2 * tw:3 * tw]
        for tt in range(ntok // tw):
            sl = slice(tt * tw, (tt + 1) * tw)
            ps, pk = pp[tt % 2], PK[tt % 2]
            for kc in range(KC):
                s, sk = sq[kc % 2], ("nsq", kc % 2)
                act("activation", out=s, in_=src[:, kc, sl], func=AF.Square, reads=[(skey, kc, tt)], writes=[sk])
                pe("matmul", ps[:, 0:tw], lhsT=ones_b, rhs=s, start=(kc == 0), stop=(kc == KC - 1), reads=[sk, "c_onesb"], writes=[pk])
            act("activation", out=lnv, in_=ps[:, 0:tw], func=AF.Ln, bias=epsc, scale=1.0 / D, reads=[pk, "c_eps"], writes=["nln"])
            act("activation", out=rstd, in_=lnv, func=AF.Exp, scale=-0.5, reads=["nln"], writes=["nrstd"])
            for kc in range(KC):
                dve("scalar_tensor_tensor", out=dst[:, kc, sl], in0=src[:, kc, sl], scalar=normw[:, wcol * KC + kc:wcol * KC + kc + 1],
                    in1=rstd, op0=ALU.mult, op1=ALU.mult, reads=[(skey, kc, tt), "nrstd", "c_normw"], writes=[(dkey, kc, tt)])
        AR.release(m)

    def proj_fm(wv, wkey, c0, mcols, xT_, xkey, kcn, tt, ps, pk, tw=512):
        for kc in range(kcn):
            pe("matmul", ps[0:mcols, 0:tw], lhsT=wv[:, kc, c0:c0 + mcols], rhs=xT_[:, kc, tt * tw:(tt + 1) * tw],
               start=(kc == 0), stop=(kc == kcn - 1), reads=[wkey, (xkey, kc, tt)], writes=[pk])

    def resid_add(wv, wkey, ncol0, nch, xT_, xkey, kcn):
        for tt in range(NTT):
            b = (nch * NTT + tt) % 2
            proj_fm(wv, wkey, ncol0, 128, xT_, xkey, kcn, tt, pp[4 + b], PK[4 + b])
            sl = slice(tt * 512, (tt + 1) * 512)
            dve("tensor_tensor", out=hT[:, nch, sl], in0=pp[4 + b][:, :], in1=hT[:, nch, sl], op=ALU.add,
                reads=[PK[4 + b], ("hT", nch, tt)], writes=[("hT", nch, tt)])

    def resid_add_ttmajor(groups, xT_, xkey, kcn):
        cnt_ = 0
        for tt in range(NTT):
            sl = slice(tt * 512, (tt + 1) * 512)
            for (wv, wk, lst) in groups:
                for (c0, nch) in lst:
                    b = cnt_ % 2
                    cnt_ += 1
                    proj_fm(wv, wk, c0, 128, xT_, xkey, kcn, tt, pp[4 + b], PK[4 + b])
                    dve("tensor_tensor", out=hT[:, nch, sl], in0=pp[4 + b][:, :], in1=hT[:, nch, sl], op=ALU.add,
                        reads=[PK[4 + b], ("hT", nch, tt)], writes=[("hT", nch, tt)])

    def to_fm(src, skeyf, nfc, dst, dkey):
        for tb in range(NTB):
            b = tb % 2
            pt = pp[6 + b][:, :].bitcast(BF16)
            for fc in range(nfc):
                pe("transpose", pt[:, fc * 128:(fc + 1) * 128], src[:, tb, fc * 128:(fc + 1) * 128], ident_b,
                   reads=list(skeyf(tb)) + ["c_identb"], writes=[PK[6 + b]])
            o = dst[:, 0:nfc, tb * 128:(tb + 1) * 128]
            i = pt[:, 0:nfc * 128].rearrange("p (f t) -> p f t", t=128)
            if tb % 2 == 0:
                dve("tensor_copy", o, i, reads=[PK[6 + b]], writes=[(dkey, fc, tb // 4) for fc in range(nfc)])
            else:
                act("copy", out=o, in_=i, reads=[PK[6 + b]], writes=[(dkey, fc, tb // 4) for fc in range(nfc)])

    m_persist = AR.mark()
    R12 = slice(0, 12)
    done = False
    for l in range(nlayers):
        rmsnorm(hT, "hT", T, 0 * DEPTH + l, xnT, "xnT", scr=arena_t[:, AR.top + 6400:AR.top + 6400 + 1536])
        m_layer = AR.mark()
        y0 = AR.mark()
        yT = AR.bf16(KC * T).rearrange("p (k t) -> p k t", t=T)
        negcT = AR.f32(192).rearrange("p (b r) -> p b r", r=12)
        stat = [AR.f32(192).rearrange("p (b r) -> p b r", r=12) for _ in range(4)]
        decay_bc = AR.f32(96).rearrange("p (b r) -> p b r", r=6)
        m_cb = AR.mark()
        Cb = AR.bf16(T)

        wgv, wgk = wload([wg_d[l]], KC)
        X1, X2, X3 = (arena_t[:, y0 + i_ * T:y0 + (i_ + 1) * T] for i_ in range(3))
        Mp, Dexp, dec = arena_t[:, y0 + 3 * T:y0 + 3 * T + 17], arena_t[:, y0 + 3 * T + 32:y0 + 3 * T + 128], arena_t[:, y0 + 3 * T + 128:y0 + 3 * T + 144]
        for gi in range(2):
            for tt in range(NTT):
                ps, pk = pp[4 + tt % 2], PK[4 + tt % 2]
                sl = slice(tt * 512, (tt + 1) * 512)
                proj_fm(wgv, wgk, gi * 12, 12, xnT, "xnT", KC, tt, ps, pk)
                if gi == 0:
                    act("activation", out=X1[R12, sl], in_=ps[R12, :], func=AF.Exp, bias=neg_gb[R12, 2 * l:2 * l + 1], scale=-1.0,
                        reads=[pk, "c_neggb"], writes=["X1"])
                else:
                    act("activation", out=X2[R12, sl], in_=ps[R12, :], func=AF.Identity, bias=gate_b[R12, 2 * l + 1:2 * l + 2], scale=1.0,
                        reads=[pk, "c_gateb"], writes=["X2"])
        gate_ops = []
        defer["list"] = gate_ops
        act("activation", out=X1[R12, :], in_=X1[R12, :], func=AF.Ln, bias=onec[R12, :], scale=1.0, reads=["X1", "c_one"], writes=["X1"])
        dve("tensor_tensor_scan", out=X3[R12, :], data0=ones_f[R12, 0:1].broadcast_to([12, T]), data1=X1[R12, :], initial=0.0,
            op0=ALU.mult, op1=ALU.subtract, reads=["X1", "c_ones"], writes=["X3"])
        dve("tensor_copy", Cb[R12, :], X3[R12, :], reads=["X3"], writes=["Cb"])
        for blk in range(NTB):
            pe("transpose", pp[4][:, blk * 12:(blk + 1) * 12], X3[R12, blk * 128:(blk + 1) * 128], ident_f[R12, 0:12],
               reads=["X3", "c_identf"], writes=[PK[4]])
        dve("tensor_scalar", out=negcT.rearrange("p b r -> p (b r)"), in0=pp[4][:, 0:192], scalar1=-1.0, scalar2=None, op0=ALU.mult,
            reads=[PK[4]], writes=["negcT"])
        dve("tensor_tensor", out=X2[R12, :], in0=X2[R12, :], in1=X3[R12, :], op=ALU.subtract, reads=["X2", "X3"], writes=["X2"])
        dve("tensor_tensor_scan", out=X1[R12, :], data0=X2[R12, :], data1=X2[R12, :], initial=0.0, op0=ALU.max, op1=ALU.max,
            reads=["X2", "X1"], writes=["X1"])
        dve("memset", Mp[R12, 0:1], 0.0, writes=["Mp"])
        v3 = lambda a: a[R12, :].rearrange("p (c t) -> p c t", t=128)
        dve("tensor_copy", Mp[R12, 1:17], v3(X1)[:, :, 127], reads=["X1", "Mp"], writes=["Mp"])
        mend_bc = Mp[R12, 1:17].unsqueeze(2).broadcast_to([12, 16, 128])
        mprev_bc = Mp[R12, 0:16].unsqueeze(2).broadcast_to([12, 16, 128])
        dve("tensor_tensor", out=dec[R12, :], in0=Mp[R12, 0:16], in1=Mp[R12, 1:17], op=ALU.subtract, reads=["Mp"], writes=["dec"])
        act("activation", out=dec[R12, :], in_=dec[R12, :], func=AF.Exp, reads=["dec"], writes=["dec"])
        dve("tensor_tensor", out=v3(X2), in0=v3(X2), in1=mend_bc, op=ALU.subtract, reads=["Mp", "X2"], writes=["X2"])
        dve("scalar_tensor_tensor", out=X3[R12, :], in0=X3[R12, :], scalar=-1.0, in1=X1[R12, :], op0=ALU.mult, op1=ALU.subtract,
            reads=["X3", "X1", "Cb", PK[4]], writes=["X3"])

        def exp_tr(qi, X, xk):
            if qi == 3:
                act("activation", out=X[R12, :], in_=X[R12, :], func=AF.Exp, bias=ln8c[R12, :], scale=1.0, reads=[xk, "c_ln8"], writes=[xk])
            else:
                act("activation", out=X[R12, :], in_=X[R12, :], func=AF.Exp, reads=[xk], writes=[xk])
            bank = 4 + qi
            for blk in range(NTB):
                pe("transpose", pp[bank][:, blk * 12:(blk + 1) * 12], X[R12, blk * 128:(blk + 1) * 128], ident_f[R12, 0:12],
                   reads=[xk, "c_identf", "negcT"], writes=[PK[bank]])
            dve("tensor_copy", stat[qi].rearrange("p b r -> p (b r)"), pp[bank][:, 0:192], reads=[PK[bank]], writes=[("stat", qi)])

        exp_tr(0, X2, "X2")
        dve("tensor_tensor", out=v3(X2), in0=mend_bc, in1=v3(X1), op=ALU.subtract, reads=["Mp", "X1", "X2"], writes=["X2"])
        exp_tr(1, X2, "X2")
        dve("tensor_tensor", out=v3(X1), in0=mprev_bc, in1=v3(X1), op=ALU.subtract, reads=["Mp", "X1"], writes=["X1"])
        exp_tr(2, X1, "X1")
        exp_tr(3, X3, "X3")
        dexp3 = Dexp[R12, :].rearrange("p (c h) -> p c h", h=6)
        for hh in range(6):
            dve("tensor_scalar", out=dexp3[:, :, hh], in0=dec[R12, :], scalar1=msel[R12, hh:hh + 1], scalar2=None, op0=ALU.mult,
                reads=["dec", "c_msel"], writes=["Dexp"])
        pe("matmul", pp[6][:, 0:96], lhsT=ones_f[R12, :], rhs=Dexp[R12, :], start=True, stop=True, reads=["Dexp", "c_ones"], writes=[PK[6]])
        dve("tensor_copy", decay_bc.rearrange("p b r -> p (b r)"), pp[6][:, 0:96], reads=[PK[6]], writes=["decay_bc"])
        defer["list"] = None
        if stop == "gates":
            flush_deferred(gate_ops, len(gate_ops))
            dump("negcT", negcT.rearrange("p b r -> p (b r)"), ["negcT"])
            for qi in range(4):
                dump("stat%d" % qi, stat[qi].rearrange("p b r -> p (b r)"), [("stat", qi)])
            dump("decay", decay_bc.rearrange("p b r -> p (b r)"), ["decay_bc"])
            break

        m_m = AR.mark()
        qTs, kTs = [AR.bf16(T) for _ in range(3)], [AR.bf16(T) for _ in range(3)]
        m_c = AR.mark()
        Ub2 = [AR.f32(T + 8) for _ in range(2)]
        Dg2 = [[AR.f32(128) for _ in range(4)] for _ in range(2)]
        for i_ in range(2):
            dve("memset", Ub2[i_][:, 0:4], 0.0, writes=[("Ub", i_)])
        MQ, MV, MO = 1926, 2694, 3078
        nhp_, nch_ = (int(stop.split(":")[1]), int(stop.split(":")[2])) if (stop or "").startswith("ml:") else (3, NTB)
        ml_w = {}
        cts = [(hp, which) for hp in range(nhp_) for which in ("q", "k")]

        def conv_proj(ci):
            hp, which = cts[ci]
            if which == "q":
                ml_w[hp] = wload([w_in_d[l][:, MQ + hp * 128:MQ + (hp + 1) * 128], w_in_d[l][:, MQ + 384 + hp * 128:MQ + 384 + (hp + 1) * 128],
                                  w_in_d[l][:, MV + hp * 128:MV + (hp + 1) * 128], w_in_d[l][:, MO + hp * 128:MO + (hp + 1) * 128]], KC)
            wmv, wmk = ml_w[hp]
            c0, chunk = (0, hp) if which == "q" else (128, 3 + hp)
            Ub, ubk, Dg = Ub2[ci % 2], ("Ub", ci % 2), Dg2[ci % 2]
            for tap in range(4):
                col = (l * 6 + chunk) * 4 + tap
                dve("tensor_scalar", out=Dg[tap], in0=ident_f, scalar1=convw[:, col:col + 1], scalar2=None, op0=ALU.mult,
                    reads=["c_identf", "c_convw"], writes=[("Dg", ci % 2, tap)])
            for tt in range(NTT):
                ps, pk = pp[tt % 2], PK[tt % 2]
                proj_fm(wmv, wmk, c0, 128, xnT, "xnT", KC, tt, ps, pk)
                if tt % 2 == 0:
                    dve("tensor_copy", Ub[:, 4 + tt * 512:4 + (tt + 1) * 512], ps[:, :], reads=[pk], writes=[ubk])
                else:
                    act("copy", out=Ub[:, 4 + tt * 512:4 + (tt + 1) * 512], in_=ps[:, :], reads=[pk], writes=[ubk])
                flush_deferred(gate_ops, 3)

        def conv_conv(ci):
            hp, which = cts[ci]
            Ub, ubk, Dg = Ub2[ci % 2], ("Ub", ci % 2), Dg2[ci % 2]
            dstT, dk = (qTs[hp], ("qT", hp)) if which == "q" else (kTs[hp], ("kT", hp))
            for tt in range(NTT):
                pc, pck = pp[2 + tt % 2], PK[2 + tt % 2]
                for tap in range(4):
                    pe("matmul", pc[:, :], lhsT=Dg[tap], rhs=Ub[:, tt * 512 + tap + 1:tt * 512 + tap + 513], start=(tap == 0), stop=(tap == 3),
                       reads=[ubk, ("Dg", ci % 2, tap)], writes=[pck])
                act("activation", out=dstT[:, tt * 512:(tt + 1) * 512], in_=pc[:, :], func=AF.Silu, reads=[pck], writes=[dk])
                flush_deferred(gate_ops, 3)

        conv_proj(0)
        for ci in range(len(cts)):
            if ci + 1 < len(cts):
                conv_proj(ci + 1)
            conv_conv(ci)
        flush_deferred(gate_ops, len(gate_ops))
        P.barrier()
        AR.release(m_c)
        Cn = AR.f32(65)
        Cnb3 = [AR.bf16(66) for _ in range(3)]
        two = lambda f: [f() for _ in range(2)]
        four = lambda f: [f() for _ in range(4)]
        three = lambda f: [f() for _ in range(3)]
        eo, sig = two(lambda: AR.f32(128)), [AR.f32(128) for _ in range(6)]
        Vw = three(lambda: AR.bf16(130).rearrange("p (e d) -> p e d", d=65))
        Ktok = two(lambda: AR.bf16(128))
        scm = three(lambda: AR.bf16(256).rearrange("p (e t) -> p e t", t=128))
        t1 = four(lambda: AR.f32(130).rearrange("p (e d) -> p e d", d=65))
        t2 = four(lambda: AR.f32(130).rearrange("p (e d) -> p e d", d=65))
        den, rcp, ss, rstd = four(lambda: AR.f32(2)), four(lambda: AR.f32(2)), four(lambda: AR.f32(2)), four(lambda: AR.f32(2))
        hh = four(lambda: AR.f32(128))
        hsq = four(lambda: AR.f32(128))
        ymlS = four(lambda: AR.bf16(128))
        for hp in range(nhp_):
            wmv, wmk = ml_w[hp]
            qT, kT = qTs[hp], kTs[hp]
            QK, KK = ("qT", hp), ("kT", hp)
            bgc = None
            dve("memset", Cn, 0.0, writes=["Cn"])
            dve("memset", Cnb3[2][:, 0:66], 0.0, writes=[("Cnb", 2)])
            r0 = 6 + 2 * hp
            bc = lambda a_, n_: a_.unsqueeze(2).broadcast_to([128, 2, n_])

            pO1 = pp[6][:, 0:130].rearrange("p (e d) -> p e d", d=65)
            pO2 = [pp[7][:, 0:65], pp[6][:, 256:321]]
            pO2k = [PK[7], PK[6]]
            p7t = pp[7][:, :].bitcast(BF16)[:, 256:384]
            ok = lambda c: 0 <= c < nch_

            def CBs(c):
                return slice(c * 128, (c + 1) * 128)

            def a_abs(c):
                q_ = c % 4
                act("activation", out=den[q_], in_=t2[q_][:, :, 64], func=AF.Abs, reads=[("t2", q_)], writes=[("den", q_)])

            def d_reduce(c):
                q_ = c % 4
                dve("tensor_reduce", out=ss[q_], in_=hsq[q_].rearrange("p (e d) -> p e d", d=64), axis=AX.X, op=ALU.add, reads=[("hsq", q_)], writes=[("ss", q_)])

            def d_norm(c):
                q_ = c % 4
                dve("tensor_tensor", out=den[q_], in0=den[q_], in1=stat[3][:, c, r0:r0 + 2], op=ALU.max, reads=[("den", q_), ("stat", 3)], writes=[("den", q_)])
                dve("reciprocal", rcp[q_], den[q_], reads=[("den", q_)], writes=[("rcp", q_)])
                hh3 = hh[q_].rearrange("p (e d) -> p e d", d=64)
                pool("tensor_tensor", out=hh3, in0=t2[q_][:, :, 0:64], in1=bc(rcp[q_], 64), op=ALU.mult, reads=[("t2", q_), ("rcp", q_)], writes=[("hh", q_)])

            def p_U(c):
                s_, s3 = c % 2, c % 3
                pU = pp[2 + s_][:, 256:386].rearrange("p (e d) -> p e d", d=65)
                for e in range(2):
                    pe("matmul", pU[:, e, :], lhsT=Ktok[s_], rhs=Vw[s3][:, e, :], start=True, stop=True, reads=[("Ktok", s_), ("Vw", s3)], writes=[PK[2 + s_]])

            def p_out(c):
                s3 = c % 3
                CB = CBs(c)
                cprev = (c - 1) % 3
                for e in range(2):
                    pe("matmul", pO1[:, e, :], lhsT=scm[s3][:, e, :], rhs=Vw[s3][:, e, :], start=True, stop=True,
                       reads=[("scm", s3), ("Vw", s3)], writes=[PK[6]])
                for e in range(2):
                    ER = slice(e * 64, (e + 1) * 64)
                    pe("matmul", pO2[e], lhsT=qT[ER, CB], rhs=Cnb3[cprev][ER, 0:65], start=True, stop=True, reads=[QK, ("Cnb", cprev)], writes=[pO2k[e]])

            def a_ycopy(c):
                act("copy", out=yT[:, 5 + hp, c * 128:(c + 1) * 128], in_=p7t, reads=[PK[7]], writes=[("yT", 5 + hp, c // 4)])

            def a_rstd(c):
                q_ = c % 4
                act("activation", out=rstd[q_], in_=ss[q_], func=AF.Ln, bias=epsc, scale=1.0 / 64, reads=[("ss", q_), "c_eps"], writes=[("rstd", q_)])
                act("activation", out=rstd[q_], in_=rstd[q_], func=AF.Exp, scale=-0.5, reads=[("rstd", q_)], writes=[("rstd", q_)])

            def d_y(c):
                q_ = c % 4
                for e in range(2):
                    dve("scalar_tensor_tensor", out=ymlS[q_][:, e * 64:(e + 1) * 64], in0=hh[q_][:, e * 64:(e + 1) * 64],
                        scalar=rstd[q_][:, e:e + 1], in1=sig[c % 6][:, e * 64:(e + 1) * 64], op0=ALU.mult, op1=ALU.mult,
                        reads=[("hh", q_), ("rstd", q_), ("sig", c % 6)], writes=[("ymlS", q_)])

            def p_front(c):
                s_ = c % 2
                CB = CBs(c)
                pvo, vok = pp[s_], PK[s_]
                for kc in range(KC):
                    pe("matmul", pvo[:, 0:256], lhsT=xnT[:, kc, CB], rhs=wmv[:, kc, 256:512], start=(kc == 0), stop=(kc == KC - 1),
                       reads=[wmk, ("xnT", kc, c // 4)], writes=[vok])
                pkt = pp[2 + s_][:, :].bitcast(BF16)
                pe("transpose", pkt[:, 0:128], kT[:, CB], ident_b, reads=[KK, "c_identb"], writes=[PK[2 + s_]])
                for e in range(2):
                    ER = slice(e * 64, (e + 1) * 64)
                    pe("matmul", pp[4 + e][:, 0:128], lhsT=kT[ER, CB], rhs=qT[ER, CB], start=True, stop=True,
                       reads=[KK, QK], writes=[PK[4 + e]])
                for e in (0, 1, 0, 1):
                    pe("matmul", pp[4 + e][:, 128:512], lhsT=ones_b, rhs=xnT[:, e, 0:384], start=True, stop=True,
                       reads=["c_onesb"], writes=[PK[4 + e]])

            def a_t1(c):
                q_ = c % 4
                for e in range(2):
                    act("activation", out=t1[q_][:, e, :], in_=pO1[:, e, :], func=AF.Copy, scale=stat[1][:, c, r0 + e:r0 + e + 1],
                        reads=[PK[6], ("stat", 1)], writes=[("t1", q_)])

            def a_front(c):
                s_, q_ = c % 2, c % 6
                pvo, vok = pp[s_], PK[s_]
                act("activation", out=eo[s_], in_=pvo[:, 128:256], func=AF.Exp, scale=-1.0, reads=[vok], writes=[("eo", s_)])
                act("activation", out=eo[s_], in_=eo[s_], func=AF.Ln, bias=onec, scale=1.0, reads=[("eo", s_), "c_one"], writes=[("eo", s_)])
                act("activation", out=eo[s_], in_=eo[s_], func=AF.Exp, scale=-1.0, reads=[("eo", s_)], writes=[("eo", s_)])
                pool("tensor_tensor", out=sig[q_], in0=eo[s_], in1=mlnw[:, l * 384 + hp * 128:l * 384 + (hp + 1) * 128], op=ALU.mult,
                     reads=[("eo", s_), "c_mlnw"], writes=[("sig", q_)])

            def d_front(c):
                s_, s3 = c % 2, c % 3
                pvo, vok = pp[s_], PK[s_]
                dve("tensor_tensor", out=Vw[s3][:, :, 0:64], in0=pvo[:, 0:128].rearrange("p (e d) -> p e d", d=64),
                    in1=bc(stat[0][:, c, r0:r0 + 2], 64), op=ALU.mult, reads=[vok, ("stat", 0)], writes=[("Vw", s3)])
                act("copy", out=Vw[s3][:, :, 64], in_=stat[0][:, c, r0:r0 + 2], reads=[("stat", 0)], writes=[("Vw", s3)])
                for e in range(2):
                    dve("tensor_tensor", out=scm[s3][:, e, :], in0=pp[4 + e][:, 0:128], in1=mask01T, op=ALU.mult,
                        reads=[PK[4 + e], "c_mask01T"], writes=[("scm", s3)])

            def a_ktok(c):
                s_ = c % 2
                pkt = pp[2 + s_][:, :].bitcast(BF16)
                act("copy", out=Ktok[s_], in_=pkt[:, 0:128], reads=[PK[2 + s_]], writes=[("Ktok", s_)])

            def d_cn(c):
                s_ = c % 2
                pU = pp[2 + s_][:, 256:386].rearrange("p (e d) -> p e d", d=65)
                for e in range(2):
                    ER = slice(e * 64, (e + 1) * 64)
                    dve("scalar_tensor_tensor", out=Cn[ER, :], in0=Cn[ER, :], scalar=decay_bc[ER, c, 2 * hp + e:2 * hp + e + 1], in1=pU[ER, e, :],
                        op0=ALU.mult, op1=ALU.add, reads=["Cn", "decay_bc", PK[2 + s_]], writes=["Cn"])

            def d_t2(c):
                q_ = c % 4
                for e in range(2):
                    dve("scalar_tensor_tensor", out=t2[q_][:, e, :], in0=pO2[e], scalar=stat[2][:, c, r0 + e:r0 + e + 1], in1=t1[q_][:, e, :],
                        op0=ALU.mult, op1=ALU.add, reads=[pO2k[e], ("stat", 2), ("t1", q_)], writes=[("t2", q_)])

            def a_cnb(c):
                act("copy", out=Cnb3[c % 3][:, 0:65], in_=Cn, reads=["Cn"], writes=[("Cnb", c % 3)])

            def a_square(c):
                q_ = c % 4
                act("activation", out=hsq[q_], in_=hh[q_], func=AF.Square, reads=[("hh", q_)], writes=[("hsq", q_)])

            def p_ytr(c):
                q_ = c % 4
                pe("transpose", p7t, ymlS[q_], ident_b, reads=[("ymlS", q_), "c_identb"], writes=[PK[7]])

            for st in range(nch_ + 7):
                if ok(st - 3): a_abs(st - 3)
                if ok(st - 4): d_reduce(st - 4)
                if ok(st - 1): p_U(st - 1)
                if ok(st - 2): p_out(st - 2)
                if ok(st - 6): a_ycopy(st - 6)
                if ok(st - 3): d_norm(st - 3)
                if ok(st - 4): a_rstd(st - 4)
                if ok(st - 5): d_y(st - 5)
                if ok(st): p_front(st)
                if ok(st - 1): d_cn(st - 1)
                if ok(st - 2): a_t1(st - 2)
                if ok(st): a_front(st)
                if ok(st): d_front(st)
                if ok(st): a_ktok(st)
                if ok(st - 2): d_t2(st - 2)
                if ok(st - 1): a_cnb(st - 1)
                if ok(st - 3): a_square(st - 3)
                if bgc is not None:
                    if next(bgc, "end") == "end":
                        bgc = None
                if ok(st - 5): p_ytr(st - 5)
            if bgc is not None:
                for _ in bgc:
                    pass
        if (stop or "").startswith("ml"):
            dump("qT", qT, [QK])
            dump("kT", kT, [KK])
            dump("yT", yT[:, 5:8, :], [("yT", 5 + k_, j_) for k_ in range(3) for j_ in range(NTT)])
            break
        P.barrier()
        AR.release(m_m)
        m_f = AR.mark()
        QE = [AR.bf16(T) for _ in range(2)]
        KE = [AR.bf16(T) for _ in range(2)]
        Vf = AR.bf16(NTB * 6 * 128).rearrange("p (b h d) -> p b h d", h=6, d=128)
        PT = [AR.bf16(512) for _ in range(4)]
        Rr = AR.f32(512)
        pool("memset", Vf[:, :, :, 64:128], 1.0, writes=["Vf_ones"])
        for i in range(2):
            pool("memset", KE[i][64:128, :], 0.0, writes=[("KEpad", i)])
            pool("memset", QE[i][64:128, :], 0.0, writes=[("QEpad", i)])
            pool("memset", KE[i][64:65, :], 1.0, writes=[("KEpad", i)])
        wvv, wvk = wload([w_in_d[l][:, 768:1152]], KC)
        for tb in range(NTB):
            ps, pk = pp[2 + tb % 2], PK[2 + tb % 2]
            for kc in range(KC):
                pe("matmul", ps[:, 0:384], lhsT=xnT[:, kc, tb * 128:(tb + 1) * 128], rhs=wvv[:, kc, :], start=(kc == 0), stop=(kc == KC - 1),
                   reads=[wvk, ("xnT", kc, tb // 4)], writes=[pk])
            o_, i_ = Vf[:, tb, :, 0:64], ps[:, 0:384].rearrange("p (h d) -> p h d", d=64)
            if tb % 2 == 0:
                dve("tensor_copy", o_, i_, reads=[pk], writes=["Vf"])
            else:
                act("copy", out=o_, in_=i_, reads=[pk], writes=["Vf"])
        wqv, wqk = wload([w_in_d[l][:, 0:384]], KC)
        wkv, wkk = wload([w_in_d[l][:, 384:768]], KC)

        def fox_proj(hd):
            b = hd % 2
            for tt in range(NTT):
                sl = slice(tt * 512, (tt + 1) * 512)
                ps, pk = pp[6], PK[6]
                proj_fm(wqv, wqk, hd * 64, 64, xnT, "xnT", KC, tt, ps, pk)
                dve("tensor_scalar", out=QE[b][0:64, sl], in0=ps[0:64, :], scalar1=0.125, scalar2=None, op0=ALU.mult,
                    reads=[pk], writes=[("QE", b)])
                yield
                ps, pk = pp[7], PK[7]
                proj_fm(wkv, wkk, hd * 64, 64, xnT, "xnT", KC, tt, ps, pk)
                dve("tensor_copy", KE[b][0:64, sl], ps[0:64, :], reads=[pk], writes=[("KE", b)])
                yield
            P.dma("sp", QE[b][64:65, :], Cb[hd:hd + 1, :], "qerow%d" % b, reads=["Cb"], writes=[("QE", b), ("QEpad", b)])

        ftiles = [(hd, j, kb) for hd in range(6) for j in range(NTT) for kb in range(4 * j + 4)]

        def f_geom(i):
            hd, j, kb = ftiles[i]
            ii = kb - 4 * j
            return hd, j, kb, ii, max(0, ii) * 128

        def f_S(i):
            hd, j, kb, ii, lo = f_geom(i)
            b = hd % 2
            pss, sk = pp[i % 4], PK[i % 4]
            pe("matmul", pss[:, lo:512], lhsT=KE[b][:, kb * 128:(kb + 1) * 128], rhs=QE[b][:, j * 512 + lo:(j + 1) * 512],
               start=True, stop=(ii < 0), reads=[("QE", b), ("KE", b), ("QEpad", b), ("KEpad", b)], writes=[sk])
            if ii >= 0:
                pe("matmul", pss[:, lo:lo + 128], lhsT=ident_b, rhs=maskneg, start=False, stop=True,
                   reads=["c_identb", "c_maskneg"], writes=[sk])

        def f_EXP(i):
            hd, j, kb, ii, lo = f_geom(i)
            pss, sk = pp[i % 4], PK[i % 4]
            act("activation", out=PT[i % 4][:, lo:512], in_=pss[:, lo:512], func=AF.Exp, bias=negcT[:, kb, hd:hd + 1], scale=1.0,
                reads=[sk, "negcT"], writes=[("PT", i % 4)])

        def f_AV(i):
            hd, j, kb, ii, lo = f_geom(i)
            grp = hd * NTT + j
            ybank = 4 + grp % 2
            psyT, yk = pp[ybank], PK[ybank]
            pe("matmul", psyT[:, lo:512], lhsT=Vf[:, kb, hd, :], rhs=PT[i % 4][:, lo:512],
               start=(kb == 0), stop=(kb == 4 * j + 3), reads=[("PT", i % 4), "Vf", "Vf_ones"], writes=[yk])
            if kb == 4 * j + 3:
                dve("reciprocal", Rr[64:128, :], psyT[64:128, :], reads=[yk], writes=["Rr"])
                dve("tensor_copy", Rr[0:64, :], Rr[64:128, :], reads=["Rr"], writes=["Rr"])
                dst = (hd % 2) * 64
                dve("tensor_tensor", out=yT[dst:dst + 64, hd // 2, j * 512:(j + 1) * 512], in0=psyT[0:64, :], in1=Rr[0:64, :], op=ALU.mult,
                    reads=[yk, "Rr"], writes=[("yT", hd // 2, j)])

        for _ in fox_proj(0):
            pass
        nft = len(ftiles)
        sb_w = {}
        LA = 3
        bg = None
        for i0 in range(min(LA, nft)):
            f_S(i0)
        for i in range(nft):
            hd, j, kb = ftiles[i]
            if j == 0 and kb == 0:
                bg = fox_proj(hd + 1) if hd + 1 < 6 else None
                if hd == 5:
                    sb_w["v"] = wload([w_in_d[l][:, 1670:1926]], KC)
                    sb_w["qk"] = wload([w_in_d[l][:, 1158:1670]], KC)
            if i + LA < nft:
                if ftiles[i + LA][0] != hd and bg is not None:
                    for _ in bg:
                        pass
                    bg = None
                f_S(i + LA)
            f_EXP(i)
            f_AV(i)
            if bg is not None and (i % 4 == 3):
                if next(bg, "end") == "end":
                    bg = None
        if stop == "fox":
            dump("yT", yT[:, 0:3, :], [("yT", k_, j_) for k_ in range(3) for j_ in range(NTT)])
            break
        P.barrier()
        AR.release(m_cb)
        m_s = AR.mark()
        qs = [AR.bf16(T) for _ in range(2)]
        ks = [AR.bf16(T) for _ in range(2)]
        Vs = AR.bf16(NTB * 256).rearrange("p (b n) -> p b n", n=256)
        Et = [AR.f32(512) for _ in range(4)]
        SPt = [AR.bf16(512) for _ in range(2)]
        Wt = [AR.f32(512) for _ in range(2)]
        At = [AR.bf16(512) for _ in range(2)]
        acc2 = [AR.bf16(512) for _ in range(2)]
        wsv, wsk = sb_w["v"]
        for tb in range(NTB):
            ps, pk = pp[2 + tb % 2], PK[2 + tb % 2]
            for kc in range(KC):
                pe("matmul", ps[:, 0:256], lhsT=xnT[:, kc, tb * 128:(tb + 1) * 128], rhs=wsv[:, kc, :], start=(kc == 0), stop=(kc == KC - 1),
                   reads=[wsk, ("xnT", kc, tb // 4)], writes=[pk])
            if tb % 2 == 0:
                dve("tensor_copy", Vs[:, tb, :], ps[:, 0:256], reads=[pk], writes=["Vs"])
            else:
                act("copy", out=Vs[:, tb, :], in_=ps[:, 0:256], reads=[pk], writes=["Vs"])
        wqkv, wqkk = sb_w["qk"]
        for hp in range(2):
            for tt in range(NTT):
                sl = slice(tt * 512, (tt + 1) * 512)
                ps, pk = pp[4 + tt % 2], PK[4 + tt % 2]
                proj_fm(wqkv, wqkk, hp * 128, 128, xnT, "xnT", KC, tt, ps, pk)
                dve("tensor_scalar", out=qs[hp][:, sl], in0=ps[:, :], scalar1=0.125, scalar2=None, op0=ALU.mult, reads=[pk], writes=[("qs", hp)])
                ps, pk = pp[6 + tt % 2], PK[6 + tt % 2]
                proj_fm(wqkv, wqkk, 256 + hp * 128, 128, xnT, "xnT", KC, tt, ps, pk)
                act("copy", out=ks[hp][:, sl], in_=ps[:, :], reads=[pk], writes=[("ks", hp)])
        wout_grp = []
        for g in range(2):
            wov, wok = wload([w_out_d[l][:, g * 512:(g + 1) * 512]], KC)
            wout_grp.append((wov, wok, [(ci * 128, g * 4 + ci) for ci in range(4)]))
        stiles = [(hd, j, kb) for hd in range(4) for j in range(NTT) for kb in range(4 * j + 3, -1, -1)]

        def s_geom(i):
            hd, j, kb = stiles[i]
            ii = kb - 4 * j
            lo = max(0, ii) * 128
            return hd, j, kb, ii, lo, slice(lo, 512), hd // 2, slice((hd % 2) * 64, (hd % 2) * 64 + 64)

        def s_Z(i):
            hd, j, kb, ii, lo, CS, hp, PR = s_geom(i)
            zb = (0, 1, 6)[i % 3]
            pe("matmul", pp[zb][:, CS], lhsT=ks[hp][PR, kb * 128:(kb + 1) * 128], rhs=qs[hp][PR, j * 512 + lo:(j + 1) * 512],
               start=True, stop=(ii < 0), reads=[("qs", hp), ("ks", hp)], writes=[PK[zb]])
            if ii >= 0:
                pe("matmul", pp[zb][:, lo:lo + 128], lhsT=ident_b, rhs=masknegS, start=False, stop=True,
                   reads=["c_identb", "c_masknegS"], writes=[PK[zb]])

        def s_E(i):
            hd, j, kb, ii, lo, CS, hp, PR = s_geom(i)
            zb = (0, 1, 6)[i % 3]
            act("activation", out=Et[i % 4][:, CS], in_=pp[zb][:, CS], func=AF.Exp, reads=[PK[zb]], writes=[("E", i % 4)])

        def s_SP(i):
            hd, j, kb, ii, lo, CS, hp, PR = s_geom(i)
            s_ = i % 2
            t_in_grp = (4 * j + 3) - kb
            acc_o, ako = acc2[t_in_grp % 2], ("acc", t_in_grp % 2)
            acc_n, akn = acc2[(t_in_grp + 1) % 2], ("acc", (t_in_grp + 1) % 2)
            if kb == 4 * j + 3:
                pool("memset", acc2[0], 0.0, writes=[("acc", 0)])
                pool("memset", acc2[1], 0.0, writes=[("acc", 1)])
            act("activation", out=SPt[s_][:, CS], in_=Et[i % 4][:, CS], func=AF.Ln, bias=onec, scale=1.0,
                reads=[("E", i % 4), "c_one"], writes=[("SP", s_)])
            pc, ck = pp[4 + s_], PK[4 + s_]
            lastblk = (kb == 4 * j + 3)
            pe("matmul", pc[:, CS], lhsT=tincl, rhs=SPt[s_][:, CS], start=True, stop=lastblk, reads=[("SP", s_), "c_tincl"], writes=[ck])
            if not lastblk:
                pe("matmul", pc[:, CS], lhsT=ones_b, rhs=acc_o[:, CS], start=False, stop=True, reads=[ako, "c_onesb"], writes=[ck])
            if kb > 0:
                pool("tensor_tensor", out=acc_n[:, CS], in0=acc_o[:, CS], in1=SPt[s_][:, CS], op=ALU.add, reads=[ako, ("SP", s_)], writes=[akn])
            for _w in range(2):
                pe("matmul", pp[7][:, :], lhsT=ones_b, rhs=xnT[:, _w, 0:512], start=True, stop=True, reads=["c_onesb"], writes=["ps7_dummy"])

        def s_W(i):
            hd, j, kb, ii, lo, CS, hp, PR = s_geom(i)
            s_ = i % 2
            grp = hd * NTT + j
            ybank = 2 + grp % 2
            psyT, yk = pp[ybank], PK[ybank]
            act("activation", out=Wt[s_][:, CS], in_=pp[4 + s_][:, CS], func=AF.Exp, scale=-1.0, reads=[PK[4 + s_]], writes=[("W", s_)])
            dve("tensor_tensor", out=At[s_][:, CS], in0=Et[i % 4][:, CS], in1=Wt[s_][:, CS], op=ALU.mult,
                reads=[("E", i % 4), ("W", s_)], writes=[("A", s_)])
            pe("matmul", psyT[0:64, CS], lhsT=Vs[:, kb, hd * 64:(hd + 1) * 64], rhs=At[s_][:, CS],
               start=(kb == 4 * j + 3), stop=(kb == 0), skip_group_check=True, reads=[("A", s_), "Vs"], writes=[yk])
            if kb == 0:
                dst = (hd % 2) * 64
                dve("tensor_copy", yT[dst:dst + 64, 3 + hd // 2, j * 512:(j + 1) * 512], psyT[0:64, :], reads=[yk], writes=[("yT", 3 + hd // 2, j)])

        nst = len(stiles)
        s_Z(0)
        if nst > 1:
            s_Z(1)
        s_E(0)
        for k in range(nst + 1):
            if k + 2 < nst:
                s_Z(k + 2)
            if k + 1 < nst:
                s_E(k + 1)
            if k < nst:
                s_SP(k)
            if k >= 1:
                s_W(k - 1)
        if stop == "sb":
            dump("yT", yT[:, 3:5, :], [("yT", 3 + k_, j_) for k_ in range(2) for j_ in range(NTT)])
            break
        P.barrier()
        AR.release(m_s)
        AR.release(m_layer)
        AR.bf16(KC * T)
        m_o = AR.mark()
        resid_add_ttmajor(wout_grp, yT, "yT", KC)
        if stop == "mixer":
            dump("yTall", yT[:, :, :], [("yT", k_, j_) for k_ in range(8) for j_ in range(NTT)])
            break
        P.barrier()
        AR.release(m_layer)

        m_x = AR.mark()
        memT = AR.f32(KC * MEM).rearrange("p (k m) -> p k m", m=MEM)
        mnT = AR.bf16(KC * MEM).rearrange("p (k m) -> p k m", m=MEM)
        KxT = AR.bf16(4 * MEM).rearrange("p (h m) -> p h m", m=MEM)
        Vx = AR.bf16(2 * 4 * 129).rearrange("p (b h d) -> p b h d", h=4, d=129)
        QxT = AR.bf16(4 * T).rearrange("p (h t) -> p h t", t=T)
        ox = AR.bf16(NTB * 512).rearrange("p (b n) -> p b n", n=512)
        oxT = AR.bf16(4 * T).rearrange("p (h t) -> p h t", t=T)
        PTx = [AR.bf16(512) for _ in range(3)]
        recx = AR.f32(4)
        rmsnorm(hT, "hT", T, 1 * DEPTH + l, xnT, "xnT")
        P.dma("sp", memT, memT_d.rearrange("(k p) m -> p k m", p=128), "memload", writes=[("memT", kc, 0) for kc in range(KC)])
        rmsnorm(memT, "memT", MEM, 2 * DEPTH + l, mnT, "mnT", tw=MEM)
        pool("memset", Vx[:, :, :, 128:129], 1.0, writes=["Vx"])
        wkxv, wkxk = wload([wx_kv_d[l][:, 0:512]], KC)
        for xh in range(4):
            ps, pk = pp[2 + xh % 2], PK[2 + xh % 2]
            for kc in range(KC):
                pe("matmul", ps[:, 0:MEM], lhsT=wkxv[:, kc, xh * 128:(xh + 1) * 128], rhs=mnT[:, kc, :], start=(kc == 0), stop=(kc == KC - 1),
                   reads=[wkxk, ("mnT", kc, 0)], writes=[pk])
            dve("tensor_copy", KxT[:, xh, :], ps[:, 0:MEM], reads=[pk], writes=["KxT"])
        wvxv, wvxk = wload([wx_kv_d[l][:, 512:1024]], KC)
        for mb in range(2):
            ps, pk = pp[4 + mb], PK[4 + mb]
            for kc in range(KC):
                pe("matmul", ps[:, :], lhsT=mnT[:, kc, mb * 128:(mb + 1) * 128], rhs=wvxv[:, kc, :], start=(kc == 0), stop=(kc == KC - 1),
                   reads=[wvxk, ("mnT", kc, 0)], writes=[pk])
            dve("tensor_copy", Vx[:, mb, :, 0:128], ps[:, :].rearrange("p (h d) -> p h d", d=128), reads=[pk], writes=["Vx"])
        wqxv, wqxk = wload([wx_q_d[l]], KC)
        for xh in range(4):
            for tt in range(NTT):
                ps, pk = pp[4 + tt % 2], PK[4 + tt % 2]
                proj_fm(wqxv, wqxk, xh * 128, 128, xnT, "xnT", KC, tt, ps, pk)
                if tt % 2 == 0:
                    dve("tensor_copy", QxT[:, xh, tt * 512:(tt + 1) * 512], ps[:, :], reads=[pk], writes=[("QxT", xh, tt)])
                else:
                    act("copy", out=QxT[:, xh, tt * 512:(tt + 1) * 512], in_=ps[:, :], reads=[pk], writes=[("QxT", xh, tt)])
        XS = float(128 ** -0.5)
        xtiles = [(xh, tt, mb) for xh in range(4) for tt in range(NTT) for mb in range(2)]
        XB = (0, 1, 6, 7)

        def x_S(i):
            xh, tt, mb = xtiles[i]
            bnk = XB[i % 4]
            pe("matmul", pp[bnk][:, :], lhsT=KxT[:, xh, mb * 128:(mb + 1) * 128], rhs=QxT[:, xh, tt * 512:(tt + 1) * 512], start=True, stop=True,
               reads=["KxT", ("QxT", xh, tt)], writes=[PK[bnk]])

        def x_rest(i):
            xh, tt, mb = xtiles[i]
            bnk = XB[i % 4]
            s_ = i % 3
            pOa = pp[2][:, 0:258].rearrange("p (q d) -> p q d", d=129)
            pOb = pp[3][:, 0:258].rearrange("p (q d) -> p q d", d=129)
            act("activation", out=PTx[s_], in_=pp[bnk][:, :], func=AF.Exp, scale=XS, reads=[PK[bnk]], writes=[("PTx", s_)])
            for qi in range(4):
                po, pok = (pOa, PK[2]) if qi < 2 else (pOb, PK[3])
                pe("matmul", po[:, qi % 2, :], lhsT=PTx[s_][:, qi * 128:(qi + 1) * 128], rhs=Vx[:, mb, xh, :],
                   start=(mb == 0 and qi % 2 == 0), stop=(mb == 1), skip_group_check=True, reads=[("PTx", s_), "Vx"], writes=[pok])
            if mb == 1:
                for half, (po, pok) in enumerate(((pOa, PK[2]), (pOb, PK[3]))):
                    dve("reciprocal", recx[:, 2 * half:2 * half + 2], po[:, :, 128], reads=[pok], writes=["recx"])
                    dve("tensor_tensor", out=ox[:, 4 * tt + 2 * half:4 * tt + 2 * half + 2, xh * 128:(xh + 1) * 128], in0=po[:, :, 0:128],
                        in1=recx[:, 2 * half:2 * half + 2].unsqueeze(2).broadcast_to([128, 2, 128]), op=ALU.mult,
                        reads=[pok, "recx"], writes=[("ox", tt)])

        nxt = len(xtiles)
        x_S(0)
        x_S(1)
        for i in range(nxt):
            if i + 2 < nxt:
                x_S(i + 2)
            x_rest(i)
        to_fm(ox, lambda tb: [("ox", tb // 4)], 4, oxT, "oxT")
        grp_ = []
        for g in range(2):
            wxov, wxok = wload([wx_o_d[l][:, g * 512:(g + 1) * 512]], 4)
            grp_.append((wxov, wxok, [(ci * 128, g * 4 + ci) for ci in range(4)]))
        resid_add_ttmajor(grp_, oxT, "oxT", 4)
        if stop == "cross":
            break
        P.barrier()
        AR.release(m_x)

        m_ff = AR.mark()
        NFH = 11
        aT = AR.bf16(NFH * T).rearrange("p (f t) -> p f t", t=T)
        sg = [AR.f32(512) for _ in range(2)]
        rmsnorm(hT, "hT", T, 3 * DEPTH + l, xnT, "xnT")
        cnt_s = 0
        for half in range(2):
            fbase = half * NFH * 128
            for g in range(6):
                nfc = 2 if g < 5 else 1
                c0 = fbase + g * 256
                wfv, wfk = wload([w_gate_d[l][:, c0:c0 + nfc * 128], w_up_d[l][:, c0:c0 + nfc * 128]], KC)
                for f_ in range(nfc):
                    fc = g * 2 + f_
                    for tt in range(NTT):
                        s_ = cnt_s % 2
                        cnt_s += 1
                        pg, pgk = pp[s_], PK[s_]
                        pu, puk = pp[2 + s_], PK[2 + s_]
                        proj_fm(wfv, wfk, f_ * 128, 128, xnT, "xnT", KC, tt, pg, pgk)
                        proj_fm(wfv, wfk, nfc * 128 + f_ * 128, 128, xnT, "xnT", KC, tt, pu, puk)
                        act("activation", out=sg[s_], in_=pg[:, :], func=AF.Silu, reads=[pgk], writes=[("sg", s_)])
                        dve("tensor_tensor", out=aT[:, fc, tt * 512:(tt + 1) * 512], in0=pu[:, :], in1=sg[s_], op=ALU.mult,
                            reads=[puk, ("sg", s_)], writes=[("aT", fc, tt)])
            for g in range(4):
                wdv, wdk = wload([w_down_d[l][fbase:fbase + NFH * 128, g * 256:(g + 1) * 256]], NFH)
                for ci in range(2):
                    resid_add(wdv, wdk, ci * 128, g * 2 + ci, aT, "aT", NFH)
        P.barrier()
        AR.release(m_ff)
        AR.release(m_layer)

    if stop is None:
        rmsnorm(hT, "hT", T, 8, hT, "hT")
    for (name, d, ap, keys) in dbg:
        P.dma("sp", d, ap, "dbg_" + name, reads=keys, is_out=True)
    P.barrier()
    for kc in range(KC):
        P.dma("sp", outT_d[kc * 128:(kc + 1) * 128, :], hT[:, kc, :], "ostore", reads=[("hT", kc, tt) for tt in range(NTT)], is_out=True)
    P.finish()
    P.emit(es)
    print('arena high-water', AR.hw, 'of', ARW)
    return nc, es


def prep_inputs(inp, b):
    f = lambda a: np.ascontiguousarray(np.asarray(a, dtype=np.float32))
    w_in = np.asarray(inp["w_in"], dtype=np.float32)
    FF, MI, MF = 1152, 3462, 3468
    w_gates = np.concatenate([w_in[:, :, FF:FF + 6], w_in[:, :, MF:MF + 6], w_in[:, :, FF:FF + 6], w_in[:, :, MI:MI + 6]], axis=2)
    nw = np.stack([inp["norm_mix_w"][0], inp["norm_mix_w"][1], inp["norm_x_w"][0], inp["norm_x_w"][1],
                   inp["mem_norm_w"][0], inp["mem_norm_w"][1], inp["norm_ffn_w"][0], inp["norm_ffn_w"][1],
                   inp["final_norm_w"]], axis=0)
    normw = np.asarray(nw, np.float32).reshape(9, KC, 128).transpose(2, 0, 1).reshape(128, 9 * KC)
    gb = np.zeros((12, DEPTH * 2), np.float32)
    for l in range(DEPTH):
        gb[0:6, 2 * l] = inp["fox_f_b"][l]
        gb[6:12, 2 * l] = inp["ml_f_b"][l]
        gb[0:6, 2 * l + 1] = inp["fox_f_b"][l]
        gb[6:12, 2 * l + 1] = inp["ml_i_b"][l]
    cw = np.asarray(inp["ml_conv_w"], np.float32)
    convw = cw.reshape(DEPTH, 4, 6, 128).transpose(3, 0, 2, 1).reshape(128, DEPTH * 24)
    mlnw = np.broadcast_to(np.asarray(inp["ml_norm_w"], np.float32).reshape(1, DEPTH * 384), (128, DEPTH * 384))
    return {
        "xT": f(np.asarray(inp["x"][b]).T), "memT": f(np.asarray(inp["mem"][b]).T),
        "normw": f(normw), "w_in": f(w_in), "w_gates": f(w_gates), "gate_b": f(gb), "convw": f(convw),
        "mlnw": f(mlnw), "w_out": f(inp["w_out"]), "wx_q": f(inp["wx_q"]), "wx_kv": f(inp["wx_kv"]),
        "wx_o": f(inp["wx_o"]), "w_gate": f(inp["w_gate"]), "w_up": f(inp["w_up"]), "w_down": f(inp["w_down"]),
    }


_CACHE = {}


def kernel(**inputs):
    if "nc" not in _CACHE:
        _CACHE["nc"] = build()
    nc, _es = _CACHE["nc"]
    shared = None
    in_maps = []
    for b in range(8):
        m = prep_inputs(inputs, b)
        if shared is None:
            shared = m
        else:
            for k in m:
                if k not in ("xT", "memT"):
                    m[k] = shared[k]
        in_maps.append(m)
    res = run_bass_kernel_spmd(nc, in_maps, core_ids=list(range(8)))
    out = np.stack([np.ascontiguousarray(r["outT"].T) for r in res.results], axis=0)
    return out.astype(np.float32)
```
